# Optimizing a Trainium2 kernel written in Bass

```python
import jax, jax.numpy as jnp
from jax import lax
import numpy as np

D_MODEL = 2048
BATCH = 4
SEQ = 4096
DEPTH = 4

N_EVEN = (DEPTH + 1) // 2
N_ODD = DEPTH // 2
NORM_EPS = 1e-6

CONV_DIM = D_MODEL // 2
CONV_WIDTH = 3
RWKV_DIM = D_MODEL // 2
RWKV_HEAD = 64
RWKV_HEADS = RWKV_DIM // RWKV_HEAD
W_LORA = 64
A_LORA = 64
G_LORA = 160
GN_EPS = 64e-5
RWKV_IN = 3 * RWKV_DIM + W_LORA + A_LORA + G_LORA
EVEN_IN = 3 * CONV_DIM + RWKV_IN
EVEN_OUT = CONV_DIM + RWKV_DIM
POOL_WINDOWS = (2, 4, 8, 16)
POOL_GROUP = D_MODEL // 16
POOL_DIM = len(POOL_WINDOWS) * POOL_GROUP
MLA_HEADS = 12
Q_LORA = 512
KV_LORA = 512
QK_NOPE = 128
QK_ROPE = 64
V_HEAD = 128
ROPE_THETA = 10000.0
ATTN_BLOCK = 128
ODD_IN = POOL_DIM + Q_LORA + KV_LORA + QK_ROPE
ODD_OUT = POOL_DIM + MLA_HEADS * V_HEAD
D_FF = ((8 * D_MODEL + 3 * 256 - 1) // (3 * 256)) * 256

kernel_name = 'hybrid_conv_rwkv7_pool_mla_trunk'


def rms_norm(x, g):
    xf = x.astype(jnp.float32)
    y = xf * lax.rsqrt(jnp.mean(xf * xf, axis=-1, keepdims=True) + NORM_EPS)
    return (y * g.astype(jnp.float32)).astype(x.dtype)


def shift_seq(u, n):
    return jnp.pad(u, ((0, 0), (n, 0), (0, 0)))[:, :u.shape[1]]


def short_conv_mixer(p, conv_w):
    b_gate, c_gate, h = jnp.split(p, 3, axis=-1)
    u = c_gate * h
    y = conv_w[:, 2] * u + conv_w[:, 1] * shift_seq(u, 1) + conv_w[:, 0] * shift_seq(u, 2)
    return b_gate * y


def wkv7_scan(r, decay, k, v, kk, a):
    bsz, _, nh, n = r.shape

    def step(state, inp):
        r_t, w_t, k_t, v_t, kk_t, a_t = inp
        sa = jnp.einsum('bhvk,bhk->bhv', state, -kk_t)
        state = (state * w_t[:, :, None, :]
                 + sa[..., None] * (kk_t * a_t)[:, :, None, :]
                 + v_t[..., None] * k_t[:, :, None, :])
        y_t = jnp.einsum('bhvk,bhk->bhv', state, r_t)
        return state, y_t

    xs = tuple(jnp.moveaxis(t, 1, 0) for t in (r, decay, k, v, kk, a))
    init = jnp.zeros((bsz, nh, n, n), jnp.float32)
    _, ys = lax.scan(step, init, xs)
    return jnp.moveaxis(ys, 0, 1)


def rwkv7_mixer(p, mu, w0, w2, a0, a2, g2, k_k, k_a, r_k, ln_w, ln_b):
    bsz, s = p.shape[:2]
    p = p.astype(jnp.float32)
    p = p + (shift_seq(p, 1) - p) * mu
    r, k, v, xw, xa, xg = jnp.split(
        p, [RWKV_DIM, 2 * RWKV_DIM, 3 * RWKV_DIM, 3 * RWKV_DIM + W_LORA,
            3 * RWKV_DIM + W_LORA + A_LORA], axis=-1)
    w = -jax.nn.softplus(-(w0 + jnp.tanh(xw) @ w2)) - 0.5
    a = jax.nn.sigmoid(a0 + xa @ a2)
    g = jax.nn.sigmoid(xg) @ g2
    hs = lambda t: t.reshape(bsz, s, RWKV_HEADS, RWKV_HEAD)
    kk = hs(k * k_k)
    kk = kk / jnp.maximum(jnp.linalg.norm(kk, axis=-1, keepdims=True), 1e-12)
    k = k * (1.0 + (a - 1.0) * k_a)
    r, k, v, a, w = hs(r), hs(k), hs(v), hs(a), hs(w)
    decay = jnp.exp(-jnp.exp(w))
    y = wkv7_scan(r, decay, k, v, kk, a)
    mean = jnp.mean(y, axis=-1, keepdims=True)
    var = jnp.mean(jnp.square(y - mean), axis=-1, keepdims=True)
    y = ((y - mean) * lax.rsqrt(var + GN_EPS)).reshape(bsz, s, RWKV_DIM) * ln_w + ln_b
    bonus = jnp.sum(r * k * r_k, axis=-1, keepdims=True) * v
    y = y + bonus.reshape(bsz, s, RWKV_DIM)
    return y * g


def pool_mixer(u, pool_w, pool_scale):
    bsz, s = u.shape[:2]
    uf = u.astype(jnp.float32)
    cs = jnp.cumsum(uf, axis=1)
    pos = jnp.arange(s)
    diffs = []
    for gi, win in enumerate(POOL_WINDOWS):
        sl = slice(gi * POOL_GROUP, (gi + 1) * POOL_GROUP)
        c = cs[..., sl]
        count = jnp.minimum(pos + 1, win).astype(jnp.float32)[None, :, None]
        diffs.append((c - shift_seq(c, win)) / count - uf[..., sl])
    d = jnp.stack(diffs, axis=2)
    y = jnp.einsum('bsgi,gio->bsgo', d, pool_w).reshape(bsz, s, POOL_DIM)
    return y * pool_scale


def rope_tables(positions):
    inv = 1.0 / (ROPE_THETA ** (jnp.arange(0, QK_ROPE, 2, dtype=jnp.float32) / QK_ROPE))
    ang = positions.astype(jnp.float32)[..., None] * inv
    return jnp.cos(ang), jnp.sin(ang)


def apply_rope(t, cos, sin):
    t = t.astype(jnp.float32)
    t1, t2 = jnp.split(t, 2, axis=-1)
    c, s = cos[:, :, None, :], sin[:, :, None, :]
    return jnp.concatenate([t1 * c - t2 * s, t1 * s + t2 * c], axis=-1)


def mla_mixer(q_lat, kv_lat, k_pe, q_norm, w_uq, kv_norm, w_ukv, cos, sin):
    bsz, s = q_lat.shape[:2]
    q = (rms_norm(q_lat, q_norm) @ w_uq).reshape(bsz, s, MLA_HEADS, QK_NOPE + QK_ROPE)
    kv = (rms_norm(kv_lat, kv_norm) @ w_ukv).reshape(bsz, s, MLA_HEADS, QK_NOPE + V_HEAD)
    q_nope = q[..., :QK_NOPE].astype(jnp.float32)
    q_pe = apply_rope(q[..., QK_NOPE:], cos, sin)
    k_nope = kv[..., :QK_NOPE].astype(jnp.float32)
    v = kv[..., QK_NOPE:].astype(jnp.float32)
    k_pe = apply_rope(k_pe[:, :, None, :], cos, sin)[:, :, 0]
    scale = (QK_NOPE + QK_ROPE) ** -0.5
    outs = []
    for i in range(s // ATTN_BLOCK):
        s0, e = i * ATTN_BLOCK, (i + 1) * ATTN_BLOCK
        sc = (jnp.einsum('bqhd,bkhd->bhqk', q_nope[:, s0:e], k_nope[:, :e])
              + jnp.einsum('bqhr,bkr->bhqk', q_pe[:, s0:e], k_pe[:, :e])) * scale
        mask = (s0 + jnp.arange(ATTN_BLOCK))[:, None] >= jnp.arange(e)[None, :]
        pr = jax.nn.softmax(jnp.where(mask, sc, -1e30), axis=-1)
        outs.append(jnp.einsum('bhqk,bkhd->bqhd', pr, v[:, :e]))
    o = jnp.concatenate(outs, axis=1)
    return o.reshape(bsz, s, MLA_HEADS * V_HEAD)


def swiglu(x, wg, wu, wd):
    return (jax.nn.silu(x @ wg) * (x @ wu)) @ wd


def setup_inputs(seed: int = 0) -> dict:
    key = jax.random.key(seed)
    ks = iter(jax.random.split(key, 40))
    nrm = lambda shape, fan_in: jax.random.normal(next(ks), shape, jnp.float32) * (fan_in ** -0.5)
    gain = lambda shape: 1.0 + 0.02 * jax.random.normal(next(ks), shape, jnp.float32)
    x = jax.random.normal(next(ks), (BATCH, SEQ, D_MODEL), jnp.float32)
    offs = jax.random.randint(next(ks), (BATCH, 1), 0, 1024, dtype=jnp.int32)
    positions = (offs + jnp.arange(SEQ, dtype=jnp.int32)[None, :]).astype(jnp.int32)
    return {
        'x': x,
        'positions': positions,
        'ev_norm': gain((N_EVEN, D_MODEL)),
        'ev_w_in': nrm((N_EVEN, D_MODEL, EVEN_IN), D_MODEL),
        'ev_conv_w': nrm((N_EVEN, CONV_DIM, CONV_WIDTH), CONV_WIDTH),
        'ev_mu': jax.random.uniform(next(ks), (N_EVEN, RWKV_IN), jnp.float32),
        'ev_w0': jax.random.uniform(next(ks), (N_EVEN, RWKV_DIM), jnp.float32, -6.5, -1.5),
        'ev_w2': nrm((N_EVEN, W_LORA, RWKV_DIM), W_LORA),
        'ev_a0': 0.1 * jax.random.normal(next(ks), (N_EVEN, RWKV_DIM), jnp.float32),
        'ev_a2': nrm((N_EVEN, A_LORA, RWKV_DIM), A_LORA),
        'ev_g2': nrm((N_EVEN, G_LORA, RWKV_DIM), G_LORA),
        'ev_k_k': 0.85 + 0.05 * jax.random.normal(next(ks), (N_EVEN, RWKV_DIM), jnp.float32),
        'ev_k_a': 1.0 + 0.05 * jax.random.normal(next(ks), (N_EVEN, RWKV_DIM), jnp.float32),
        'ev_r_k': 0.1 * jax.random.normal(next(ks), (N_EVEN, RWKV_HEADS, RWKV_HEAD), jnp.float32),
        'ev_ln_w': gain((N_EVEN, RWKV_DIM)),
        'ev_ln_b': 0.02 * jax.random.normal(next(ks), (N_EVEN, RWKV_DIM), jnp.float32),
        'ev_w_out': nrm((N_EVEN, EVEN_OUT, D_MODEL), EVEN_OUT),
        'od_norm': gain((N_ODD, D_MODEL)),
        'od_w_in': nrm((N_ODD, D_MODEL, ODD_IN), D_MODEL),
        'od_pool_w': nrm((N_ODD, len(POOL_WINDOWS), POOL_GROUP, POOL_GROUP), POOL_GROUP),
        'od_pool_scale': 0.5 + 0.1 * jax.random.normal(next(ks), (N_ODD, POOL_DIM), jnp.float32),
        'od_q_norm': gain((N_ODD, Q_LORA)),
        'od_w_uq': nrm((N_ODD, Q_LORA, MLA_HEADS * (QK_NOPE + QK_ROPE)), Q_LORA),
        'od_kv_norm': gain((N_ODD, KV_LORA)),
        'od_w_ukv': nrm((N_ODD, KV_LORA, MLA_HEADS * (QK_NOPE + V_HEAD)), KV_LORA),
        'od_w_out': nrm((N_ODD, ODD_OUT, D_MODEL), ODD_OUT),
        'ffn_norm': gain((DEPTH, D_MODEL)),
        'ffn_w_gate': nrm((DEPTH, D_MODEL, D_FF), D_MODEL),
        'ffn_w_up': nrm((DEPTH, D_MODEL, D_FF), D_MODEL),
        'ffn_w_down': nrm((DEPTH, D_FF, D_MODEL), D_FF),
        'final_norm': gain((D_MODEL,)),
    }


def reference(x, positions, ev_norm, ev_w_in, ev_conv_w, ev_mu, ev_w0, ev_w2, ev_a0, ev_a2,
              ev_g2, ev_k_k, ev_k_a, ev_r_k, ev_ln_w, ev_ln_b, ev_w_out, od_norm, od_w_in,
              od_pool_w, od_pool_scale, od_q_norm, od_w_uq, od_kv_norm, od_w_ukv, od_w_out,
              ffn_norm, ffn_w_gate, ffn_w_up, ffn_w_down, final_norm):
    dt = x.dtype
    cos, sin = rope_tables(positions)
    h = x
    for layer in range(DEPTH):
        j = layer // 2
        if layer % 2 == 0:
            p = rms_norm(h, ev_norm[j]) @ ev_w_in[j]
            ya = short_conv_mixer(p[..., :3 * CONV_DIM], ev_conv_w[j])
            yb = rwkv7_mixer(p[..., 3 * CONV_DIM:], ev_mu[j], ev_w0[j], ev_w2[j], ev_a0[j],
                             ev_a2[j], ev_g2[j], ev_k_k[j], ev_k_a[j], ev_r_k[j],
                             ev_ln_w[j], ev_ln_b[j])
            mix = jnp.concatenate([ya.astype(dt), yb.astype(dt)], axis=-1)
            h = h + mix @ ev_w_out[j]
        else:
            p = rms_norm(h, od_norm[j]) @ od_w_in[j]
            o1 = POOL_DIM
            o2 = o1 + Q_LORA
            o3 = o2 + KV_LORA
            yc = pool_mixer(p[..., :o1], od_pool_w[j], od_pool_scale[j])
            yd = mla_mixer(p[..., o1:o2], p[..., o2:o3], p[..., o3:], od_q_norm[j],
                           od_w_uq[j], od_kv_norm[j], od_w_ukv[j], cos, sin)
            mix = jnp.concatenate([yc.astype(dt), yd.astype(dt)], axis=-1)
            h = h + mix @ od_w_out[j]
        h = h + swiglu(rms_norm(h, ffn_norm[layer]), ffn_w_gate[layer], ffn_w_up[layer],
                       ffn_w_down[layer])
    return rms_norm(h, final_norm)
```

```python
import math
import numpy as np
import ml_dtypes
from contextlib import ExitStack
import concourse.bass as bass
import concourse.mybir as mybir
from concourse.bass_utils import run_bass_kernel_spmd

F32 = mybir.dt.float32
BF16 = mybir.dt.bfloat16
I32 = mybir.dt.int32
ALU = mybir.AluOpType
AF = mybir.ActivationFunctionType

ENGS = ("pe", "act", "dve", "pool", "sp")
DMA_RING = 8

D = 2048
DFF = 5632
FFC = 44
NH_R = 16
NH_A = 12
EV_FC = 52
OD_FC = 14
SM_SCALE = 192.0 ** -0.5


class Res:
    __slots__ = ("name", "writers", "readers", "prev_readers")

    def __init__(self, name=""):
        self.name = name
        self.writers = []
        self.readers = []
        self.prev_readers = []


class Op:
    __slots__ = ("eng", "fn", "deps", "sig", "sig_no", "dma", "dma_idx", "epoch", "idx")


class Sched:
    def __init__(self, nc):
        self.nc = nc
        self.ops = {e: [] for e in ENGS}
        self.epoch = 0
        self.dma_count = {e: 0 for e in ENGS}
        self.pending = {e: None for e in ENGS}

    def new_epoch(self):
        self.epoch += 1

    def barrier(self):
        last = []
        for e in ENGS:
            ops = self.ops[e]
            if not ops:
                continue
            if e == "sp" or any(o.dma for o in ops[-DMA_RING * 4:]):
                dm = [o for o in ops if o.dma][-DMA_RING:]
                last.extend(dm)
            nd = [o for o in ops if not o.dma]
            if nd:
                last.append(nd[-1])
        for e in ENGS:
            self.pending[e] = list(last)

    def add(self, eng, fn, reads=(), writes=(), dma=False, pwrites=()):
        op = Op()
        op.eng = eng
        op.fn = fn
        op.dma = dma
        op.sig = False
        op.sig_no = None
        op.epoch = self.epoch
        op.idx = len(self.ops[eng])
        deps = {}

        def add_dep(p):
            if p.dma:
                deps[id(p)] = p
                return
            if p.eng == eng and not dma and eng == "pe":
                return
            key = ("c", p.eng)
            q = deps.get(key)
            if q is None or q.idx < p.idx:
                deps[key] = p

        if self.pending[eng] is not None:
            for p in self.pending[eng]:
                add_dep(p)
            self.pending[eng] = None
        for r in reads:
            for p in r.writers:
                add_dep(p)
        for w in writes:
            for p in w.writers:
                add_dep(p)
            for rd in w.readers:
                add_dep(rd)
        for w in pwrites:
            for rd in w.readers:
                add_dep(rd)
            for rd in w.prev_readers:
                add_dep(rd)
        for r in reads:
            if not dma:
                r.readers = [x for x in r.readers if x.dma or x.eng != eng]
            r.readers.append(op)
        for w in writes:
            w.prev_readers = w.readers
            w.writers = [op]
            w.readers = []
        for w in pwrites:
            if w.readers:
                w.prev_readers = w.readers
                w.writers = [op]
                w.readers = []
            else:
                if not dma:
                    w.writers = [x for x in w.writers if x.dma or x.eng != eng]
                w.writers.append(op)
        op.deps = list(deps.values())
        for p in op.deps:
            p.sig = True
        if dma:
            op.dma_idx = self.dma_count[eng]
            self.dma_count[eng] += 1
            op.sig = True
        self.ops[eng].append(op)
        return op

    def emit(self, stack, final_waits=()):
        nc = self.nc
        n_epochs = self.epoch + 1
        csem = {}
        for e in ("pe", "act", "dve", "pool"):
            for ep in range(n_epochs):
                if any(o.sig and not o.dma and o.epoch == ep for o in self.ops[e]):
                    csem[(e, ep)] = stack.enter_context(nc.semaphore(f"c_{e}_{ep}"))
        dsem = {}
        for e in ENGS:
            if self.dma_count[e]:
                dsem[e] = [stack.enter_context(nc.semaphore(f"d_{e}_{i}")) for i in range(DMA_RING)]
        for e in ENGS:
            cnt = {}
            for o in self.ops[e]:
                if o.dma:
                    continue
                if o.sig:
                    cnt[o.epoch] = cnt.get(o.epoch, 0) + 1
                    o.sig_no = cnt[o.epoch]

        def waits_for(o):
            ws = []
            for p in o.deps:
                if p.dma:
                    ws.append((dsem[p.eng][p.dma_idx % DMA_RING], 16 * (p.dma_idx // DMA_RING + 1)))
                else:
                    ws.append((csem[(p.eng, p.epoch)], p.sig_no))
            if o.dma and o.dma_idx >= DMA_RING:
                ws.append((dsem[o.eng][o.dma_idx % DMA_RING], 16 * (o.dma_idx // DMA_RING)))
            return ws

        block = stack.enter_context(nc.Block())
        engmap = {"pe": block.tensor, "act": block.scalar, "dve": block.vector,
                  "pool": block.gpsimd, "sp": block.sync}
        stats = {}
        for e in ENGS:
            ops = self.ops[e]
            if not ops:
                continue
            stats[e] = len(ops)

            def body(eng, ops=ops, e=e):
                have = {}
                for o in ops:
                    for (s, v) in waits_for(o):
                        k = id(s)
                        if have.get(k, 0) >= v:
                            continue
                        have[k] = v
                        eng.wait_ge(s, v)
                    ins = o.fn(eng)
                    if o.sig:
                        if o.dma:
                            ins.then_inc(dsem[o.eng][o.dma_idx % DMA_RING], 16)
                        else:
                            ins.then_inc(csem[(o.eng, o.epoch)], 1)
                if e == "sp":
                    for o in final_waits:
                        eng.wait_ge(dsem[o.eng][o.dma_idx % DMA_RING], 16 * (o.dma_idx // DMA_RING + 1))
            engmap[e](body)
        return stats


class T:
    def __init__(self, h, name=""):
        self.h = h
        self.r = Res(name)

    def __getitem__(self, k):
        return self.h[k]


DTSZ = {F32: 4, BF16: 2, I32: 4}


class KB:
    def __init__(self, nc, NT):
        self.nc = nc
        self.NT = NT
        self.NTB = NT // 512
        self.S = Sched(nc)
        self.cur = 16512
        self.end = 229376
        self.uid = 0
        self.bank_i = 0
        self.wi = 0
        self.ri = {}
        self.held = set()

    def alloc(self, shape, dt, name="t"):
        n = 1
        for s in shape[1:]:
            n *= s
        nb = (n * DTSZ[dt] + 63) // 64 * 64
        off = self.cur
        self.cur += nb
        assert self.cur <= self.end, f"SBUF overflow {self.cur} allocating {name}"
        self.uid += 1
        h = self.nc.alloc_sbuf_tensor_at(f"{name}{self.uid}", list(shape), dt, offset=off)
        return T(h, name)

    def mark(self):
        return self.cur

    def phase(self, mark):
        self.cur = mark
        self.S.barrier()

    def bank(self, avoid=(), hold=False):
        while True:
            b = self.bank_i % 8
            self.bank_i += 1
            if b not in avoid and b not in self.held:
                if hold:
                    self.held.add(b)
                return b

    def release(self, *bs):
        for b in bs:
            self.held.discard(b)

    def ring(self, key, tiles):
        i = self.ri.get(key, 0)
        self.ri[key] = i + 1
        return tiles[i % len(tiles)]

    def mm(self, out, lhsT, rhs, start=True, stop=True, R=(), W=(), PW=()):
        return self.S.add("pe", lambda e: e.matmul(out, lhsT=lhsT, rhs=rhs, start=start, stop=stop), R, W, pwrites=PW)

    def tr(self, out, in_, ident, R=(), W=(), PW=()):
        return self.S.add("pe", lambda e: e.transpose(out=out, in_=in_, identity=ident), R, W, pwrites=PW)

    def act(self, out, in_, func, R=(), W=(), PW=(), **kw):
        return self.S.add("act", lambda e: e.activation(out=out, in_=in_, func=func, **kw), R, W, pwrites=PW)

    def tt(self, eng, out, in0, in1, op, R=(), W=(), PW=()):
        return self.S.add(eng, lambda e: e.tensor_tensor(out=out, in0=in0, in1=in1, op=op), R, W, pwrites=PW)

    def ts(self, eng, out, in0, s1, s2, op0, op1=None, R=(), W=(), PW=()):
        if op1 is None:
            return self.S.add(eng, lambda e: e.tensor_scalar(out=out, in0=in0, scalar1=s1, scalar2=None, op0=op0), R, W, pwrites=PW)
        return self.S.add(eng, lambda e: e.tensor_scalar(out=out, in0=in0, scalar1=s1, scalar2=s2, op0=op0, op1=op1), R, W, pwrites=PW)

    def stt(self, out, in0, scalar, in1, op0, op1, R=(), W=(), PW=()):
        return self.S.add("dve", lambda e: e.scalar_tensor_tensor(out=out, in0=in0, scalar=scalar, in1=in1, op0=op0, op1=op1), R, W, pwrites=PW)

    def copy(self, eng, out, in_, R=(), W=(), PW=()):
        if eng == "act":
            return self.S.add("act", lambda e: e.copy(out=out, in_=in_), R, W, pwrites=PW)
        return self.S.add(eng, lambda e: e.tensor_copy(out=out, in_=in_), R, W, pwrites=PW)

    def memset(self, eng, ap, val, W=(), PW=()):
        return self.S.add(eng, lambda e: e.memset(ap, val), (), W, pwrites=PW)

    def dma(self, q, out, in_, R=(), W=(), PW=()):
        return self.S.add(q, lambda e: e.dma_start(out=out, in_=in_), R, W, dma=True, pwrites=PW)


def build_program(NT, colmap, NCOL, stop_after=None, depth=4):
    nc = bass.Bass("TRN2", target_bir_lowering=False)
    kb = KB(nc, NT)
    S = kb.S
    NTB = NT // 512
    dram_in = lambda n, shp, dt: nc.dram_tensor(n, list(shp), dt, kind="ExternalInput").ap()
    dram_tmp = lambda n, shp, dt: nc.dram_tensor(n, list(shp), dt, kind="Internal").ap()

    x_d = dram_in("x", [NT, D], F32)
    pos_d = dram_in("pos", [1, NT], I32)
    cols_d = dram_in("cols", [128, NCOL], F32)
    ident_d = dram_in("ident", [128, 128], F32)
    masks_d = dram_in("masks", [64, 5, 512], F32)
    tri_d = dram_in("tri", [128, 128], BF16)
    poolfix_d = dram_in("poolfix", [128, 4, 16], F32)
    win_e = [dram_in(f"ev_w_in{j}", [D, EV_FC * 128], F32) for j in range(2)]
    wout_e = [dram_in(f"ev_w_out{j}", [D, D], F32) for j in range(2)]
    w2_d = [dram_in(f"ev_w2{j}", [64, 1024], F32) for j in range(2)]
    a2_d = [dram_in(f"ev_a2{j}", [64, 1024], F32) for j in range(2)]
    g2_d = [dram_in(f"ev_g2{j}", [160, 1024], F32) for j in range(2)]
    win_o = [dram_in(f"od_w_in{j}", [D, OD_FC * 128], F32) for j in range(2)]
    wout_o = [dram_in(f"od_w_out{j}", [D, D], F32) for j in range(2)]
    poolw_d = [dram_in(f"od_pool_w{j}", [4, 128, 128], F32) for j in range(2)]
    wuq_d = [dram_in(f"od_w_uq{j}", [512, 24 * 128], F32) for j in range(2)]
    wuk_d = [dram_in(f"od_w_uk{j}", [512, 12 * 128], F32) for j in range(2)]
    wuv_d = [dram_in(f"od_w_uv{j}", [512, 12 * 128], F32) for j in range(2)]
    wg_d = [dram_in(f"ffn_wg{l}", [D, DFF], F32) for l in range(4)]
    wu_d = [dram_in(f"ffn_wu{l}", [D, DFF], F32) for l in range(4)]
    wd_d = [dram_in(f"ffn_wd{l}", [DFF, D], F32) for l in range(4)]
    out_d = nc.dram_tensor("out", [NT, D], F32, kind="ExternalOutput").ap()
    dbg_d = nc.dram_tensor("dbg", [D, NT], F32, kind="ExternalOutput").ap() if stop_after is not None else None
    dbgp_d = nc.dram_tensor("dbgp", [EV_FC * 128, NT], F32, kind="ExternalOutput").ap() if stop_after is not None else None
    dbgm_d = nc.dram_tensor("dbgm", [D, NT], BF16, kind="ExternalOutput").ap() if stop_after is not None else None
    dbgr_d = nc.dram_tensor("dbgr", [26, 64, 512], F32, kind="ExternalOutput").ap() if stop_after is not None else None
    dbg_outs = []

    hT_d = dram_tmp("hT", [D, NT], F32)
    pT_d = dram_tmp("pT", [EV_FC * 128, NT], F32)
    mixT_d = dram_tmp("mixT", [D, NT], BF16)
    cs_d = dram_tmp("cs", [2, 64, NT], F32)
    wb_in = dram_tmp("wb_in", [EV_FC, 128, 16 * 128], BF16)
    wb_out = dram_tmp("wb_out", [16, 128, 16 * 128], BF16)
    wb_g = dram_tmp("wb_g", [FFC, 128, 16 * 128], BF16)
    wb_u = dram_tmp("wb_u", [FFC, 128, 16 * 128], BF16)
    wb_d = dram_tmp("wb_d", [16, 128, FFC * 128], BF16)
    wb_uq = dram_tmp("wb_uq", [24, 128, 4 * 128], BF16)
    wb_uk = dram_tmp("wb_uk", [12, 128, 4 * 128], BF16)
    hT_r = [[Res(f"hT{i}_{c}") for c in range(16)] for i in range(NTB)]
    pT_r = [Res(f"pT{i}") for i in range(EV_FC)]
    mix_r = [Res(f"mix{i}") for i in range(16)]
    cs_r = Res("cs")
    wr = {k: Res(k) for k in ("in", "out", "g", "u", "d", "uq", "uk")}
    hT_v = hT_d.rearrange("(c p) t -> p c t", p=128)
    mixT_v = mixT_d.rearrange("(c p) t -> p c t", p=128)

    ps = [nc.alloc_psum_tensor(f"ps{i}", [128, 512], F32) for i in range(8)]
    psr = [Res(f"ps{i}") for i in range(8)]
    cols = kb.alloc([128, NCOL], F32, "cols")
    ident = kb.alloc([128, 128], F32, "ident")
    ones_bf = kb.alloc([128, 128], BF16, "ones")
    ones64 = kb.alloc([64, 64], F32, "ones64")
    mean64 = kb.alloc([64, 64], F32, "mean64")
    masks = kb.alloc([64, 5, 512], F32, "masks")
    tri = kb.alloc([128, 128], BF16, "tri")
    wring = [kb.alloc([128, FFC * 128], BF16, "wring") for _ in range(3)]
    kb.dma("sp", cols[:], cols_d, W=[cols.r])
    kb.dma("sp", ident[:], ident_d, W=[ident.r])
    kb.dma("sp", masks[:], masks_d, W=[masks.r])
    kb.dma("sp", tri[:], tri_d, W=[tri.r])
    kb.memset("pool", ones_bf[:], 1.0, W=[ones_bf.r])
    kb.memset("pool", ones64[:], 1.0, W=[ones64.r])
    kb.memset("pool", mean64[:], 1.0 / 64.0, W=[mean64.r])
    id64b_t = kb.alloc([64, 64], BF16, "id64b")
    kb.copy("act", id64b_t[:], ident[0:64, 0:64], R=[ident.r], W=[id64b_t.r])
    base_mark = kb.mark()

    def col(name, j=0, parts=128):
        i = colmap[name] + j
        return cols[0:parts, i:i + 1]

    def cast_weight(src, K, F, dst, dres, gain=None):
        S.barrier()
        m = kb.mark()
        wf = [kb.alloc([128, 2048], F32, "wf") for _ in range(3)]
        wbt = [kb.alloc([128, 2048], BF16, "wbt") for _ in range(3)]
        KCn = K // 128
        i = 0
        for c in range(KCn):
            for f0 in range(0, F, 2048):
                fw = min(2048, F - f0)
                a = wf[i % 3]
                b = wbt[i % 3]
                kb.dma("sp", a[:, 0:fw], src[c * 128:(c + 1) * 128, f0:f0 + fw], W=[a.r])
                eng = ("act", "pool", "dve")[i % 3]
                if gain is None:
                    kb.copy(eng, b[:, 0:fw], a[:, 0:fw], R=[a.r], W=[b.r])
                elif eng == "act":
                    kb.act(b[:, 0:fw], a[:, 0:fw], AF.Copy, R=[a.r, cols.r], W=[b.r], scale=col(gain, c))
                else:
                    kb.ts(eng, b[:, 0:fw], a[:, 0:fw], col(gain, c), None, ALU.mult, R=[a.r, cols.r], W=[b.r])
                dv = dst[f0 // 128:(f0 + fw) // 128, :, c * 128:(c + 1) * 128].rearrange("fc p f -> p fc f")
                kb.dma("pool", dv, b[:, 0:fw].rearrange("p (fc f) -> p fc f", f=128), R=[b.r], PW=[dres])
                i += 1
        kb.cur = m
        S.barrier()

    class Slot:
        pass

    def wring_slots():
        out = []
        for wt in wring:
            sl = Slot()
            sl.f32 = wt[:, 512:2560].bitcast(F32)
            sl.b16 = wt[:, 2560:3584]
            sl.rf = Res("slf")
            sl.rb = Res("slb")
            out.append(sl)
        return out

    def alloc_slots(n=3):
        out = []
        for _ in range(n):
            a = kb.alloc([128, 1024], F32, "slf")
            b = kb.alloc([128, 1024], BF16, "slb")
            sl = Slot()
            sl.f32, sl.b16, sl.rf, sl.rb = a[:], b[:], a.r, b.r
            out.append(sl)
        return out

    def cast_steps(jobs, slots):
        tiles = []
        for (src, K, F, dst, dres, gain) in jobs:
            for c in range(K // 128):
                for f0 in range(0, F, 1024):
                    tiles.append((src, dst, dres, gain, c, f0, min(1024, F - f0)))
        n = len(tiles)
        NS = len(slots)
        for k in range(n + 2):
            if k < n:
                (src, dst, dres, gain, c, f0, fw) = tiles[k]
                sl = slots[k % NS]
                kb.dma("sp", sl.f32[:, 0:fw], src[c * 128:(c + 1) * 128, f0:f0 + fw], W=[sl.rf])
            if 0 <= k - 1 < n:
                (src, dst, dres, gain, c, f0, fw) = tiles[k - 1]
                sl = slots[(k - 1) % NS]
                if gain is None:
                    kb.copy("act", sl.b16[:, 0:fw], sl.f32[:, 0:fw], R=[sl.rf], W=[sl.rb])
                else:
                    kb.ts("pool", sl.b16[:, 0:fw], sl.f32[:, 0:fw], col(gain, c), None, ALU.mult, R=[sl.rf, cols.r], W=[sl.rb])
            if 0 <= k - 2 < n:
                (src, dst, dres, gain, c, f0, fw) = tiles[k - 2]
                sl = slots[(k - 2) % NS]
                dv = dst[f0 // 128:(f0 + fw) // 128, :, c * 128:(c + 1) * 128].rearrange("fc p f -> p fc f")
                kb.dma("sp", dv, sl.b16[:, 0:fw].rearrange("p (fc f) -> p fc f", f=128), R=[sl.rb], PW=[dres])
            yield

    def bg_step(it):
        if it[0] is not None:
            try:
                next(it[0])
            except StopIteration:
                it[0] = None

    def bg_drain(it):
        while it[0] is not None:
            bg_step(it)

    def jobs_out_ffn(L):
        wo = wout_e[L // 2] if L % 2 == 0 else wout_o[L // 2]
        return [(wo, D, D, wb_out, wr["out"], None),
                (wg_d[L], D, DFF, wb_g, wr["g"], f"ffn_norm{L}"),
                (wu_d[L], D, DFF, wb_u, wr["u"], f"ffn_norm{L}"),
                (wd_d[L], DFF, D, wb_d, wr["d"], None)]

    def jobs_in(L):
        j_ = L // 2
        if L % 2 == 0:
            return [(win_e[j_], D, EV_FC * 128, wb_in, wr["in"], f"ev_norm{j_}")]
        return [(win_o[j_], D, OD_FC * 128, wb_in, wr["in"], f"od_norm{j_}"),
                (wuq_d[j_], 512, 24 * 128, wb_uq, wr["uq"], f"q_norm{j_}"),
                (wuk_d[j_], 512, 12 * 128, wb_uk, wr["uk"], f"kv_norm{j_}")]

    def linear(xT, KCn, wd, wres, flist, evac):
        for f in flist:
            wt = kb.ring("w", wring)
            kb.dma("sp", wt[:, 0:KCn * 128], wd[f], R=[wres], W=[wt.r])
            b = kb.bank()
            for c in range(KCn):
                kb.mm(ps[b][:, :], wt[:, c * 128:(c + 1) * 128], xT[:, c, :], start=(c == 0), stop=(c == KCn - 1),
                      R=[wt.r, xT.r], W=[psr[b]] if c == 0 else (), PW=() if c == 0 else [psr[b]])
            evac(f, b)

    def stats_rows(src_tiles, nchunks_total, rstd, inv_n, eps, sqring):
        b = kb.bank()
        k = 0
        for (tile_, n) in src_tiles:
            sq = kb.ring("sq", sqring)
            kb.tt("pool", sq[:, 0:n, :], tile_[:, 0:n, :], tile_[:, 0:n, :], ALU.mult, R=[tile_.r], W=[sq.r])
            for j in range(n):
                kb.mm(ps[b][:, :], ones_bf[:], sq[:, j, :], start=(k == 0), stop=(k == nchunks_total - 1),
                      R=[ones_bf.r, sq.r], W=[psr[b]] if k == 0 else (), PW=() if k == 0 else [psr[b]])
                k += 1
        kb.act(rstd[:], ps[b][:], AF.Sqrt, R=[psr[b], cols.r], W=[rstd.r], scale=inv_n, bias=col("eps6"))
        S.add("dve", lambda e: e.reciprocal(out=rstd[:], in_=rstd[:]), [rstd.r], [rstd.r])

    def norm_block(tb, xn, rstd, hring, sqring):
        tiles = []
        for q in range(4):
            ht = kb.ring("h", hring)
            kb.dma("sp", ht[:], hT_v[:, q * 4:(q + 1) * 4, tb * 512:(tb + 1) * 512], R=hT_r[tb][q * 4:(q + 1) * 4], W=[ht.r])
            kb.copy("act", xn[:, q * 4:(q + 1) * 4, :], ht[:], R=[ht.r], PW=[xn.r])
            tiles.append((ht, 4))
        stats_rows(tiles, 16, rstd, 1.0 / D, 1e-6, sqring)

    def resid_evac(tb, hcr, hnr):
        def ev(f, b):
            hc = kb.ring("hc", hcr)
            hn = kb.ring("hn", hnr)
            kb.dma("sp", hc[:], hT_d[f * 128:(f + 1) * 128, tb * 512:(tb + 1) * 512], R=[hT_r[tb][f]], W=[hc.r])
            kb.tt("dve", hn[:], ps[b][:], hc[:], ALU.add, R=[psr[b], hc.r], W=[hn.r])
            kb.dma("pool", hT_d[f * 128:(f + 1) * 128, tb * 512:(tb + 1) * 512], hn[:], R=[hn.r], W=[hT_r[tb][f]])
        return ev

    def outproj_phase(wsrc):
        kb.phase(base_mark)
        xm = [kb.alloc([128, 16, 512], BF16, "xm") for _ in range(2)]
        hcr = [kb.alloc([128, 512], F32, "hc") for _ in range(3)]
        hnr = [kb.alloc([128, 512], F32, "hn") for _ in range(3)]
        for tb in range(NTB):
            x_ = xm[tb % 2]
            kb.dma("sp", x_[:], mixT_v[:, :, tb * 512:(tb + 1) * 512], R=mix_r, W=[x_.r])
            linear(x_, 16, wb_out, wr["out"], range(16), resid_evac(tb, hcr, hnr))

    def ffn_phase(l, bg_jobs):
        kb.phase(base_mark)
        bgit = [cast_steps(bg_jobs, alloc_slots()) if bg_jobs else None]
        xn = kb.alloc([128, 16, 512], BF16, "xn")
        actT = kb.alloc([128, FFC, 512], BF16, "actT")
        rstd = kb.alloc([128, 512], F32, "rstd")
        hring = [kb.alloc([128, 4, 512], F32, "hr") for _ in range(4)]
        sqring = [kb.alloc([128, 4, 512], BF16, "sq") for _ in range(2)]
        t1r = [kb.alloc([128, 512], F32, "t1") for _ in range(2)]
        t2r = [kb.alloc([128, 512], F32, "t2") for _ in range(2)]
        t3r = [kb.alloc([128, 512], F32, "t3") for _ in range(2)]
        hcr = [kb.alloc([128, 512], F32, "hc") for _ in range(3)]
        hnr = [kb.alloc([128, 512], F32, "hn") for _ in range(3)]
        for tb in range(NTB):
            norm_block(tb, xn, rstd, hring, sqring)
            for f in range(FFC):
                bg = [None]
                linear(xn, 16, wb_g, wr["g"], [f], lambda f_, b_: bg.__setitem__(0, b_))
                bu = [None]
                linear(xn, 16, wb_u, wr["u"], [f], lambda f_, b_: bu.__setitem__(0, b_))
                t1 = kb.ring("t1", t1r)
                t2 = kb.ring("t2", t2r)
                t3 = kb.ring("t3", t3r)
                kb.tt("dve", t1[:], ps[bg[0]][:], rstd[:], ALU.mult, R=[psr[bg[0]], rstd.r], W=[t1.r])
                kb.act(t2[:], t1[:], AF.Silu, R=[t1.r], W=[t2.r])
                kb.tt("dve", t3[:], ps[bu[0]][:], rstd[:], ALU.mult, R=[psr[bu[0]], rstd.r], W=[t3.r])
                kb.tt("pool", actT[:, f, :], t2[:], t3[:], ALU.mult, R=[t2.r, t3.r], PW=[actT.r])
                bg_step(bgit)
            linear(actT, FFC, wb_d, wr["d"], range(16), resid_evac(tb, hcr, hnr))
        bg_drain(bgit)

    def inproj_phase(wsrc, FC, gain, precast):
        kb.phase(base_mark)
        if not precast:
            cast_weight(wsrc, D, FC * 128, wb_in, wr["in"], gain=gain)
            S.barrier()
        xn = kb.alloc([128, 16, 512], BF16, "xn")
        rstd = kb.alloc([128, 512], F32, "rstd")
        hring = [kb.alloc([128, 4, 512], F32, "hr") for _ in range(4)]
        sqring = [kb.alloc([128, 4, 512], BF16, "sq") for _ in range(2)]
        evr = [kb.alloc([128, 512], F32, "ev") for _ in range(3)]
        for tb in range(NTB):
            norm_block(tb, xn, rstd, hring, sqring)

            def ev(f, b, tb=tb):
                e_ = kb.ring("ev", evr)
                kb.tt("dve", e_[:], ps[b][:], rstd[:], ALU.mult, R=[psr[b], rstd.r], W=[e_.r])
                kb.dma("pool", pT_d[f * 128:(f + 1) * 128, tb * 512:(tb + 1) * 512], e_[:], R=[e_.r], PW=[pT_r[f]])
            linear(xn, 16, wb_in, wr["in"], range(FC), ev)

    def load_phase():
        xt = [kb.alloc([128, D], F32, "xt") for _ in range(2)]
        hs = [kb.alloc([128, 16, 128], F32, "hs") for _ in range(2)]
        for t in range(NT // 128):
            a = xt[t % 2]
            h_ = hs[t % 2]
            kb.dma("sp", a[:], x_d[t * 128:(t + 1) * 128, :], W=[a.r])
            for g in range(4):
                b = kb.bank()
                for j in range(4):
                    c = g * 4 + j
                    kb.tr(ps[b][:, j * 128:(j + 1) * 128], a[:, c * 128:(c + 1) * 128], ident[:],
                          R=[a.r, ident.r], W=[psr[b]] if j == 0 else (), PW=() if j == 0 else [psr[b]])
                kb.copy("dve" if g % 2 == 0 else "act", h_[:, g * 4:(g + 1) * 4, :],
                        ps[b][:].rearrange("p (a b) -> p a b", a=4), R=[psr[b]], PW=[h_.r])
            kb.dma("pool", hT_v[:, :, t * 128:(t + 1) * 128], h_[:], R=[h_.r], PW=hT_r[t // 4])

    def rope_phase():
        kb.phase(base_mark)
        pi_ = kb.alloc([64, NT], I32, "posi")
        pf = kb.alloc([64, NT], F32, "posf")
        t1 = kb.alloc([64, NT], F32, "rt1")
        t2 = kb.alloc([64, NT], F32, "rt2")
        kb.dma("sp", pi_[:], pos_d.partition_broadcast(64), W=[pi_.r])
        kb.copy("dve", pf[:], pi_[:], R=[pi_.r], W=[pf.r])
        kb.ts("dve", pf[:], pf[:], col("invf", 0, 64), None, ALU.mult, R=[pf.r, cols.r], W=[pf.r])
        C1 = 6.28125
        C2 = 2.0 * math.pi - 6.28125
        ki = kb.alloc([64, NT], I32, "ki")
        for which, shift in ((0, math.pi / 2), (1, 0.0)):
            kb.ts("dve", t1[:], pf[:], shift, None, ALU.add, R=[pf.r], W=[t1.r])
            kb.ts("dve", t2[:], t1[:], 1.0 / (2.0 * math.pi), None, ALU.mult, R=[t1.r], W=[t2.r])
            kb.copy("dve", ki[:], t2[:], R=[t2.r], W=[ki.r])
            kb.copy("dve", t2[:], ki[:], R=[ki.r], W=[t2.r])
            kb.stt(t1[:], t2[:], -C1, t1[:], ALU.mult, ALU.add, R=[t1.r, t2.r], W=[t1.r])
            kb.stt(t1[:], t2[:], -C2, t1[:], ALU.mult, ALU.add, R=[t1.r, t2.r], W=[t1.r])
            kb.ts("dve", t1[:], t1[:], -3.141592, 3.141592, ALU.max, ALU.min, R=[t1.r], W=[t1.r])
            kb.act(t2[:], t1[:], AF.Sin, R=[t1.r], W=[t2.r])
            if which == 1:
                kb.ts("dve", t2[:], t2[:], col("sinsign", 0, 64), None, ALU.mult, R=[t2.r, cols.r], W=[t2.r])
            kb.dma("pool", cs_d[which], t2[:], R=[t2.r], PW=[cs_r])

    def conv_phase(j):
        kb.phase(base_mark)
        NR = 2
        bt = [kb.alloc([128, 512], F32, "cb") for _ in range(NR)]
        ct = [kb.alloc([128, 514], F32, "cc") for _ in range(NR)]
        htl = [kb.alloc([128, 514], F32, "ch") for _ in range(NR)]
        ut = [kb.alloc([128, 514], F32, "cu") for _ in range(NR)]
        y1 = [kb.alloc([128, 512], F32, "cy") for _ in range(NR)]
        yo = [kb.alloc([128, 512], BF16, "co") for _ in range(NR)]
        i = 0
        for c in range(8):
            for tb in range(NTB):
                b_, c_, h_, u_, y_, o_ = bt[i % NR], ct[i % NR], htl[i % NR], ut[i % NR], y1[i % NR], yo[i % NR]
                i += 1
                t0 = tb * 512
                kb.dma("sp", b_[:], pT_d[c * 128:(c + 1) * 128, t0:t0 + 512], R=[pT_r[c]], W=[b_.r])
                if tb == 0:
                    kb.memset("pool", c_[:, 0:2], 0.0, W=[c_.r])
                    kb.memset("pool", h_[:, 0:2], 0.0, W=[h_.r])
                    kb.dma("sp", c_[:, 2:514], pT_d[(8 + c) * 128:(9 + c) * 128, 0:512], R=[pT_r[8 + c]], PW=[c_.r])
                    kb.dma("sp", h_[:, 2:514], pT_d[(16 + c) * 128:(17 + c) * 128, 0:512], R=[pT_r[16 + c]], PW=[h_.r])
                else:
                    kb.dma("sp", c_[:], pT_d[(8 + c) * 128:(9 + c) * 128, t0 - 2:t0 + 512], R=[pT_r[8 + c]], W=[c_.r])
                    kb.dma("sp", h_[:], pT_d[(16 + c) * 128:(17 + c) * 128, t0 - 2:t0 + 512], R=[pT_r[16 + c]], W=[h_.r])
                kb.tt("pool", u_[:], c_[:], h_[:], ALU.mult, R=[c_.r, h_.r], W=[u_.r])
                kb.ts("dve", y_[:], u_[:, 2:514], col(f"conv{j}", c * 3 + 2), None, ALU.mult, R=[u_.r, cols.r], W=[y_.r])
                kb.stt(y_[:], u_[:, 1:513], col(f"conv{j}", c * 3 + 1), y_[:], ALU.mult, ALU.add, R=[u_.r, y_.r, cols.r], W=[y_.r])
                kb.stt(y_[:], u_[:, 0:512], col(f"conv{j}", c * 3 + 0), y_[:], ALU.mult, ALU.add, R=[u_.r, y_.r, cols.r], W=[y_.r])
                kb.tt("pool", o_[:], b_[:], y_[:], ALU.mult, R=[b_.r, y_.r], W=[o_.r])
                kb.dma("pool", mixT_d[c * 128:(c + 1) * 128, t0:t0 + 512], o_[:], R=[o_.r], PW=[mix_r[c]])

    def rwkv_phase(j, bg_jobs):
        kb.phase(base_mark)
        bg = [cast_steps(bg_jobs, wring_slots())]
        A = lambda shape, dt, n: kb.alloc(shape, dt, n)
        wtmp = A([128, 1024], F32, "wtmp")
        w2b = A([64, 1024], BF16, "w2b")
        a2b = A([64, 1024], BF16, "a2b")
        g2b0 = A([128, 1024], BF16, "g2b0")
        g2b1 = A([32, 1024], BF16, "g2b1")
        for (src, r0, n, dst) in ((w2_d[j], 0, 64, w2b), (a2_d[j], 0, 64, a2b), (g2_d[j], 0, 128, g2b0), (g2_d[j], 128, 32, g2b1)):
            kb.dma("sp", wtmp[0:n, :], src[r0:r0 + n, :], W=[wtmp.r])
            kb.copy("act", dst[0:n, :], wtmp[0:n, :], R=[wtmp.r], W=[dst.r])
        ST = [A([64, 64], F32, "ST") for _ in range(NH_R)]
        STb = [A([64, 64], BF16, "STb") for _ in range(NH_R)]
        for s_ in ST + STb:
            kb.memset("pool", s_[:], 0.0, W=[s_.r])
        id64b = id64b_t[:]
        xwr = A([64, 513], F32, "xwr"); xar = A([64, 513], F32, "xar")
        xg0r = A([128, 513], F32, "xg0r"); xg1r = A([32, 513], F32, "xg1r")
        sh_t = A([128, 512], F32, "sht")
        txw = A([64, 512], BF16, "txw"); xab = A([64, 512], BF16, "xab")
        sxg0 = A([128, 512], BF16, "sxg0"); sxg1 = A([32, 512], BF16, "sxg1")
        def make_set():
            raw_ = {n: A([64, 513], F32, n) for n in ["rr", "kr", "vr"]}
            f_ = {}
            for n in ["rm", "km", "vm", "d", "lw", "a", "g", "kk", "sq", "kappa", "kp", "b", "cum", "Ep", "Em", "Epv", "t"]:
                f_[n] = A([64, 512], F32, n)
            for n in ["rt", "at", "kh", "bh", "vmb", "Atok", "Vtok", "Khtok", "Bhtok", "P0", "PT0", "P1_", "PT1", "XT0", "XT1",
                      "AKT", "RKT", "RBT", "Pone", "U0", "ApT", "U"]:
                f_[n] = A([64, 512], BF16, n)
            for new, old in (("C0", "kk"), ("YT", "cum"), ("yc", "Em"), ("rk", "Epv"), ("rs", "d")):
                f_[new] = f_[old]
            ob_ = A([64, 512], BF16, "ob")
            return raw_, f_, ob_
        sets = [make_set(), make_set()]
        M_UP, M_LO, M_UPI, M_I8, M_RST = (masks[:, i, :] for i in range(5))
        id64 = ident[0:64, 0:64]
        NEG_E = -math.exp(-0.5)

        def shift_mix(dst, rawt, mucol, parts, d):
            kb.tt("pool", d[0:parts, :], rawt[0:parts, 0:512], rawt[0:parts, 1:513], ALU.subtract, R=[rawt.r], W=[d.r])
            kb.stt(dst[0:parts, :], d[0:parts, :], mucol, rawt[0:parts, 1:513], ALU.mult, ALU.add, R=[d.r, rawt.r, cols.r], W=[dst.r])

        def load_halo(t_, row0, parts, tg, res):
            t0 = tg * 512
            if tg == 0:
                kb.memset("pool", t_[0:parts, 0:1], 0.0, W=[t_.r])
                kb.dma("sp", t_[0:parts, 1:513], pT_d[row0:row0 + parts, 0:512], R=[res], PW=[t_.r])
            else:
                kb.dma("sp", t_[0:parts, :], pT_d[row0:row0 + parts, t0 - 1:t0 + 512], R=[res], W=[t_.r])

        def per_chunk_mm(outb, lhs, rhs, Rl):
            for c in range(8):
                cs = slice(c * 64, (c + 1) * 64)
                kb.mm(ps[outb][0:64, cs], lhs[:, cs], rhs[:, cs], R=Rl,
                      W=[psr[outb]] if c == 0 else (), PW=() if c == 0 else [psr[outb]])

        for tg in range(NTB):
            load_halo(xwr, 48 * 128, 64, tg, pT_r[48])
            load_halo(xar, 49 * 128, 64, tg, pT_r[49])
            load_halo(xg0r, 50 * 128, 128, tg, pT_r[50])
            load_halo(xg1r, 51 * 128, 32, tg, pT_r[51])
            f = sets[0][1]
            shift_mix(f["t"], xwr, col(f"mu_xw{j}", 0, 64), 64, f["d"])
            kb.act(txw[:], f["t"][:], AF.Tanh, R=[f["t"].r], W=[txw.r])
            shift_mix(f["t"], xar, col(f"mu_xa{j}", 0, 64), 64, f["d"])
            kb.copy("act", xab[:], f["t"][:], R=[f["t"].r], W=[xab.r])
            tmp128 = sh_t
            kb.tt("pool", wtmp[:, 0:512], xg0r[:, 0:512], xg0r[:, 1:513], ALU.subtract, R=[xg0r.r], W=[wtmp.r])
            kb.stt(tmp128[:, :], wtmp[:, 0:512], col(f"mu_xg0{j}"), xg0r[:, 1:513], ALU.mult, ALU.add, R=[wtmp.r, xg0r.r, cols.r], W=[tmp128.r])
            kb.act(sxg0[:], tmp128[:], AF.Sigmoid, R=[tmp128.r], W=[sxg0.r])
            kb.tt("pool", wtmp[0:32, 0:512], xg1r[0:32, 0:512], xg1r[0:32, 1:513], ALU.subtract, R=[xg1r.r], W=[wtmp.r])
            kb.stt(tmp128[0:32, :], wtmp[0:32, 0:512], col(f"mu_xg1{j}", 0, 32), xg1r[0:32, 1:513], ALU.mult, ALU.add, R=[wtmp.r, xg1r.r, cols.r], W=[tmp128.r])
            kb.act(sxg1[:], tmp128[0:32, :], AF.Sigmoid, R=[tmp128.r], W=[sxg1.r])

            def head_gen(hd, raw, f, ob):
                c0 = hd * 64
                hc = lambda nm: col(f"{nm}{j}", hd, 64)
                for (nm, base) in (("rr", 24 * 128), ("kr", 32 * 128), ("vr", 40 * 128)):
                    load_halo(raw[nm], base + c0, 64, tg, pT_r[(base + c0) // 128])
                shift_mix(f["rm"], raw["rr"], hc("mu_r"), 64, f["d"])
                shift_mix(f["km"], raw["kr"], hc("mu_k"), 64, f["d"])
                shift_mix(f["vm"], raw["vr"], hc("mu_v"), 64, f["d"])
                kb.copy("act", f["vmb"][:], f["vm"][:], R=[f["vm"].r], W=[f["vmb"].r])
                yield
                b = kb.bank()
                kb.mm(ps[b][0:64, :], w2b[:, c0:c0 + 64], txw[:], R=[w2b.r, txw.r], W=[psr[b]])
                kb.act(f["lw"][:], ps[b][0:64, :], AF.Sigmoid, R=[psr[b], cols.r], W=[f["lw"].r], bias=hc("w0"))
                kb.ts("pool", f["lw"][:], f["lw"][:], NEG_E, None, ALU.mult, R=[f["lw"].r], W=[f["lw"].r])
                b = kb.bank()
                kb.mm(ps[b][0:64, :], a2b[:, c0:c0 + 64], xab[:], R=[a2b.r, xab.r], W=[psr[b]])
                kb.act(f["a"][:], ps[b][0:64, :], AF.Sigmoid, R=[psr[b], cols.r], W=[f["a"].r], bias=hc("a0"))
                b = kb.bank()
                kb.mm(ps[b][0:64, :], g2b0[:, c0:c0 + 64], sxg0[:], start=True, stop=False, R=[g2b0.r, sxg0.r], W=[psr[b]])
                kb.mm(ps[b][0:64, :], g2b1[:, c0:c0 + 64], sxg1[:], start=False, stop=True, R=[g2b1.r, sxg1.r], PW=[psr[b]])
                kb.copy("act", f["g"][:], ps[b][0:64, :], R=[psr[b]], W=[f["g"].r])
                yield
                kb.ts("pool", f["kk"][:], f["km"][:], hc("k_k"), None, ALU.mult, R=[f["km"].r, cols.r], W=[f["kk"].r])
                kb.tt("pool", f["sq"][:], f["kk"][:], f["kk"][:], ALU.mult, R=[f["kk"].r], W=[f["sq"].r])
                b = kb.bank()
                kb.mm(ps[b][0:64, :], ones64[:], f["sq"][:], R=[ones64.r, f["sq"].r], W=[psr[b]])
                kb.act(f["rs"][:], ps[b][0:64, :], AF.Sqrt, R=[psr[b], cols.r], W=[f["rs"].r], bias=col("tiny", 0, 64))
                S.add("dve", lambda e: e.reciprocal(out=f["rs"][:], in_=f["rs"][:]), [f["rs"].r], [f["rs"].r])
                kb.tt("pool", f["kappa"][:], f["kk"][:], f["rs"][:], ALU.mult, R=[f["kk"].r, f["rs"].r], W=[f["kappa"].r])
                kb.ts("dve", f["t"][:], f["a"][:], -1.0, hc("k_a"), ALU.add, ALU.mult, R=[f["a"].r, cols.r], W=[f["t"].r])
                kb.stt(f["kp"][:], f["t"][:], 1.0, f["km"][:], ALU.add, ALU.mult, R=[f["t"].r, f["km"].r], W=[f["kp"].r])
                kb.tt("pool", f["b"][:], f["kappa"][:], f["a"][:], ALU.mult, R=[f["kappa"].r, f["a"].r], W=[f["b"].r])
                yield
                S.add("dve", lambda e: e.tensor_tensor_scan(out=f["cum"][:], data0=M_RST, data1=f["lw"][:], initial=0.0,
                                                            op0=ALU.mult, op1=ALU.add), [masks.r, f["lw"].r], [f["cum"].r])
                kb.act(f["Ep"][:], f["cum"][:], AF.Exp, R=[f["cum"].r], W=[f["Ep"].r])
                kb.act(f["Em"][:], f["cum"][:], AF.Exp, R=[f["cum"].r], W=[f["Em"].r], scale=-1.0)
                kb.tt("pool", f["t"][:], f["cum"][:], f["lw"][:], ALU.subtract, R=[f["cum"].r, f["lw"].r], W=[f["t"].r])
                kb.act(f["Epv"][:], f["t"][:], AF.Exp, R=[f["t"].r], W=[f["Epv"].r])
                kb.tt("pool", f["rt"][:], f["rm"][:], f["Ep"][:], ALU.mult, R=[f["rm"].r, f["Ep"].r], W=[f["rt"].r])
                kb.stt(f["at"][:], f["kappa"][:], -1.0, f["Epv"][:], ALU.mult, ALU.mult, R=[f["kappa"].r, f["Epv"].r], W=[f["at"].r])
                kb.tt("pool", f["kh"][:], f["kp"][:], f["Em"][:], ALU.mult, R=[f["kp"].r, f["Em"].r], W=[f["kh"].r])
                kb.tt("dve", f["bh"][:], f["b"][:], f["Em"][:], ALU.mult, R=[f["b"].r, f["Em"].r], W=[f["bh"].r])
                yield
                for (src, dst, eng) in (("at", "Atok", "dve"), ("vmb", "Vtok", "act"), ("kh", "Khtok", "dve"), ("bh", "Bhtok", "act")):
                    b = kb.bank()
                    psb = ps[b][0:64, :].bitcast(BF16)
                    for c in range(8):
                        cs = slice(c * 64, (c + 1) * 64)
                        kb.tr(psb[:, cs], f[src][:, cs], id64b, R=[f[src].r, id64b_t.r],
                              W=[psr[b]] if c == 0 else (), PW=() if c == 0 else [psr[b]])
                    kb.copy(eng, f[dst][:], psb[:, 0:512], R=[psr[b]], W=[f[dst].r])
                    yield
                for (lhs, rhs, dst, mk, eng) in (("bh", "at", "PT0", M_UP, "dve"), ("at", "bh", "P0", M_LO, "dve"),
                                                 ("kh", "at", "AKT", M_UP, "dve"), ("kh", "rt", "RKT", M_UPI, "dve"),
                                                 ("bh", "rt", "RBT", M_UPI, "dve")):
                    b = kb.bank()
                    per_chunk_mm(b, f[lhs], f[rhs], [f[lhs].r, f[rhs].r])
                    kb.tt(eng, f[dst][:], ps[b][0:64, :], mk, ALU.mult, R=[psr[b], masks.r], W=[f[dst].r])
                    yield
                b = kb.bank()
                per_chunk_mm(b, f["AKT"], f["Vtok"], [f["AKT"].r, f["Vtok"].r])
                kb.copy("act", f["Pone"][:], ps[b][0:64, :], R=[psr[b]], W=[f["Pone"].r])
                yield
                kb.tt("pool", f["XT0"][:], f["PT0"][:], M_I8, ALU.add, R=[f["PT0"].r, masks.r], W=[f["XT0"].r])
                Pc, PTc, Xc = "P0", "PT0", "XT0"
                for lvl in range(1, 6):
                    Pn = "P1_" if Pc == "P0" else "P0"
                    PTn = "PT1" if PTc == "PT0" else "PT0"
                    Xn = "XT1" if Xc == "XT0" else "XT0"
                    b = kb.bank()
                    per_chunk_mm(b, f[PTc], f[Pc], [f[PTc].r, f[Pc].r])
                    b2 = None
                    if lvl < 5:
                        b2 = kb.bank()
                        per_chunk_mm(b2, f[Pc], f[PTc], [f[PTc].r, f[Pc].r])
                    kb.copy("act", f[Pn][:], ps[b][0:64, :], R=[psr[b]], W=[f[Pn].r])
                    if b2 is not None:
                        kb.copy("dve", f[PTn][:], ps[b2][0:64, :], R=[psr[b2]], W=[f[PTn].r])
                    yield
                    b3 = kb.bank()
                    per_chunk_mm(b3, f[Pn], f[Xc], [f[Pn].r, f[Xc].r])
                    kb.tt("dve", f[Xn][:], ps[b3][0:64, :], f[Xc][:], ALU.add, R=[psr[b3], f[Xc].r], W=[f[Xn].r])
                    Pc, PTc, Xc = Pn, PTn, Xn
                    yield
                XT = f[Xc]
                b = kb.bank()
                per_chunk_mm(b, XT, f["Pone"], [XT.r, f["Pone"].r])
                kb.copy("act", f["U0"][:], ps[b][0:64, :], R=[psr[b]], W=[f["U0"].r])
                yield
                b = kb.bank()
                per_chunk_mm(b, f["Atok"], XT, [XT.r, f["Atok"].r])
                kb.copy("dve", f["ApT"][:], ps[b][0:64, :], R=[psr[b]], W=[f["ApT"].r])
                yield
                b = kb.bank()
                per_chunk_mm(b, f["Khtok"], f["Vtok"], [f["Khtok"].r, f["Vtok"].r])
                kb.copy("act", f["C0"][:], ps[b][0:64, :], R=[psr[b]], W=[f["C0"].r])
                yield
                bU = kb.bank(hold=True); bY = kb.bank(hold=True); bS = kb.bank(hold=True)
                st = ST[hd]
                stb = STb[hd]
                for c in range(8):
                    cs = slice(c * 64, (c + 1) * 64)
                    kb.mm(ps[bU][0:64, cs], f["ApT"][:, cs], stb[:], start=True, stop=False, R=[f["ApT"].r, stb.r],
                          W=[psr[bU]] if c == 0 else (), PW=() if c == 0 else [psr[bU]])
                    kb.mm(ps[bU][0:64, cs], id64b, f["U0"][:, cs], start=False, stop=True, R=[id64b_t.r, f["U0"].r], PW=[psr[bU]])
                    kb.copy("dve", f["U"][:, cs], ps[bU][0:64, cs], R=[psr[bU]], PW=[f["U"].r])
                    yield
                    kb.mm(ps[bY][0:64, cs], f["Vtok"][:, cs], f["RKT"][:, cs], start=True, stop=False, R=[f["Vtok"].r, f["RKT"].r],
                          W=[psr[bY]] if c == 0 else (), PW=() if c == 0 else [psr[bY]])
                    kb.mm(ps[bY][0:64, cs], f["U"][:, cs], f["RBT"][:, cs], start=False, stop=False, R=[f["U"].r, f["RBT"].r], PW=[psr[bY]])
                    kb.mm(ps[bY][0:64, cs], stb[:], f["rt"][:, cs], start=False, stop=True, R=[stb.r, f["rt"].r], PW=[psr[bY]])
                    kb.mm(ps[bS][0:64, 0:64] if False else ps[bS][0:64, cs], f["Bhtok"][:, cs], f["U"][:, cs], start=True, stop=False,
                          R=[f["Bhtok"].r, f["U"].r], W=[psr[bS]] if c == 0 else (), PW=() if c == 0 else [psr[bS]])
                    kb.mm(ps[bS][0:64, cs], id64, st[:], start=False, stop=False, R=[ident.r, st.r], PW=[psr[bS]])
                    kb.mm(ps[bS][0:64, cs], id64, f["C0"][:, cs], start=False, stop=True, R=[ident.r, f["C0"].r], PW=[psr[bS]])
                    kb.ts("dve", stb[:], ps[bS][0:64, cs], f["Ep"][:, c * 64 + 63:c * 64 + 64], None, ALU.mult,
                          R=[psr[bS], f["Ep"].r], W=[stb.r])
                    kb.ts("dve", st[:], ps[bS][0:64, cs], f["Ep"][:, c * 64 + 63:c * 64 + 64], None, ALU.mult,
                          R=[psr[bS], f["Ep"].r], W=[st.r])
                    yield
                kb.copy("act", f["YT"][:], ps[bY][0:64, :], R=[psr[bY]], W=[f["YT"].r])
                kb.release(bU, bY, bS)
                yield
                b = kb.bank()
                kb.mm(ps[b][0:64, :], mean64[:], f["YT"][:], R=[mean64.r, f["YT"].r], W=[psr[b]])
                kb.stt(f["yc"][:], ps[b][0:64, :], -1.0, f["YT"][:], ALU.mult, ALU.add, R=[psr[b], f["YT"].r], W=[f["yc"].r])
                yield
                kb.tt("pool", f["sq"][:], f["yc"][:], f["yc"][:], ALU.mult, R=[f["yc"].r], W=[f["sq"].r])
                b = kb.bank()
                kb.mm(ps[b][0:64, :], mean64[:], f["sq"][:], R=[mean64.r, f["sq"].r], W=[psr[b]])
                kb.act(f["rs"][:], ps[b][0:64, :], AF.Sqrt, R=[psr[b], cols.r], W=[f["rs"].r], bias=col("gneps", 0, 64))
                S.add("dve", lambda e: e.reciprocal(out=f["rs"][:], in_=f["rs"][:]), [f["rs"].r], [f["rs"].r])
                yield
                kb.tt("pool", f["yc"][:], f["yc"][:], f["rs"][:], ALU.mult, R=[f["yc"].r, f["rs"].r], W=[f["yc"].r])
                kb.act(f["t"][:], f["yc"][:], AF.Identity, R=[f["yc"].r, cols.r], W=[f["t"].r], scale=hc("ln_w"), bias=hc("ln_b"))
                kb.stt(f["rk"][:], f["rm"][:], hc("r_k"), f["kp"][:], ALU.mult, ALU.mult, R=[f["rm"].r, f["kp"].r, cols.r], W=[f["rk"].r])
                b = kb.bank()
                kb.mm(ps[b][0:64, :], ones64[:], f["rk"][:], R=[ones64.r, f["rk"].r], W=[psr[b]])
                kb.tt("dve", f["rk"][:], ps[b][0:64, :], f["vm"][:], ALU.mult, R=[psr[b], f["vm"].r], W=[f["rk"].r])
                kb.tt("pool", f["t"][:], f["t"][:], f["rk"][:], ALU.add, R=[f["t"].r, f["rk"].r], W=[f["t"].r])
                kb.tt("pool", ob[:], f["t"][:], f["g"][:], ALU.mult, R=[f["t"].r, f["g"].r], W=[ob.r])
                kb.dma("pool", mixT_d[1024 + c0:1024 + c0 + 64, tg * 512:(tg + 1) * 512], ob[:], R=[ob.r], PW=[mix_r[8 + hd // 2]])
                if False:
                    for i_, nm_ in enumerate(["lw", "a", "g", "kappa", "kp", "cum", "rt", "at", "kh", "bh", "Atok", "Vtok", "AKT", "RKT", "RBT",
                                              "Pone", Xc, "U0", "ApT", "C0", "U", "YT", "yc", "t", "rm", "vm"]):
                        dbg_outs.append(kb.dma("pool", dbgr_d[i_], f[nm_][:], R=[f[nm_].r]))

            for hp in range(0, NH_R, 2):
                gens = [head_gen(hp + k_, *sets[k_]) for k_ in range(2)]
                while gens:
                    for g_ in list(gens):
                        try:
                            next(g_)
                        except StopIteration:
                            gens.remove(g_)
                    bg_step(bg)
        bg_drain(bg)

    def pool_phase(j):
        kb.phase(base_mark)
        pw = kb.alloc([128, 4, 128], F32, "pw")
        pfix = kb.alloc([128, 4, 16], F32, "pfix")
        kb.dma("sp", pw[:], poolw_d[j].rearrange("g i o -> i g o"), W=[pw.r])
        kb.dma("sp", pfix[:], poolfix_d, W=[pfix.r])
        NR = 2
        ur = [kb.alloc([128, 527], F32, "pu") for _ in range(NR)]
        sa = [kb.alloc([128, 527], F32, "psa") for _ in range(NR)]
        sb_ = [kb.alloc([128, 527], F32, "psb") for _ in range(NR)]
        dr = [kb.alloc([128, 512], F32, "pd") for _ in range(NR)]
        orr = [kb.alloc([128, 512], BF16, "po") for _ in range(NR)]
        i = 0
        for g in range(4):
            win = 2 << g
            for tb in range(NTB):
                u_, a_, b_, d_, o_ = ur[i % NR], sa[i % NR], sb_[i % NR], dr[i % NR], orr[i % NR]
                i += 1
                t0 = tb * 512
                if tb == 0:
                    kb.memset("pool", u_[:, 0:15], 0.0, W=[u_.r])
                    kb.dma("sp", u_[:, 15:527], pT_d[g * 128:(g + 1) * 128, 0:512], R=[pT_r[g]], PW=[u_.r])
                else:
                    kb.dma("sp", u_[:], pT_d[g * 128:(g + 1) * 128, t0 - 15:t0 + 512], R=[pT_r[g]], W=[u_.r])
                src = u_
                w = 1
                dsts = [a_, b_]
                k = 0
                while w < win:
                    dst = dsts[k % 2]
                    k += 1
                    kb.tt("pool" if k % 2 else "dve", dst[:, 2 * w - 1:527], src[:, 2 * w - 1:527], src[:, w - 1:527 - w], ALU.add,
                          R=[src.r], W=[dst.r])
                    src = dst
                    w *= 2
                kb.stt(d_[:], src[:, 15:527], 1.0 / win, u_[:, 15:527], ALU.mult, ALU.subtract, R=[src.r, u_.r], W=[d_.r])
                if tb == 0:
                    kb.tt("dve", d_[:, 0:16], src[:, 15:31], pfix[:, g, :], ALU.mult, R=[src.r, pfix.r, d_.r], W=[d_.r])
                    kb.tt("dve", d_[:, 0:16], d_[:, 0:16], u_[:, 15:31], ALU.subtract, R=[d_.r, u_.r], W=[d_.r])
                b = kb.bank()
                kb.mm(ps[b][:, :], pw[:, g, :], d_[:], R=[pw.r, d_.r], W=[psr[b]])
                kb.act(o_[:], ps[b][:], AF.Copy, R=[psr[b], cols.r], W=[o_.r], scale=col(f"pool_scale{j}", g))
                kb.dma("pool", mixT_d[g * 128:(g + 1) * 128, t0:t0 + 512], o_[:], R=[o_.r], PW=[mix_r[g]])

    def mla_phase(j, bg_jobs):
        kb.phase(base_mark)
        bg = [cast_steps(bg_jobs, wring_slots())]
        A = kb.alloc
        wv = A([128, 4, 1536], BF16, "wv")
        wtmp = A([128, 512], F32, "wvt")
        for c in range(4):
            for q3 in range(3):
                kb.dma("sp", wtmp[:], wuv_d[j][c * 128:(c + 1) * 128, q3 * 512:(q3 + 1) * 512], W=[wtmp.r])
                kb.act(wv[:, c, q3 * 512:(q3 + 1) * 512], wtmp[:], AF.Copy, R=[wtmp.r, cols.r], PW=[wv.r], scale=col(f"kv_norm{j}", c))
        cosr = [A([64, 512], F32, "cos2") for _ in range(2)]
        sinr = [A([64, 512], F32, "sin2") for _ in range(2)]

        def load_cs(tb):
            c_ = kb.ring("cos", cosr)
            s_ = kb.ring("sin", sinr)
            kb.dma("sp", c_[:], cs_d[0][:, tb * 512:(tb + 1) * 512], R=[cs_r], W=[c_.r])
            kb.dma("sp", s_[:], cs_d[1][:, tb * 512:(tb + 1) * 512], R=[cs_r], W=[s_.r])
            return c_, s_
        qn = A([128, 4, NT], BF16, "qn")
        kvn = A([128, 4, NT], BF16, "kvn")
        kpeT = A([64, NT], BF16, "kpeT")
        lat = [A([128, 4, 512], F32, "lat") for _ in range(1)]
        sqring = [A([128, 4, 512], BF16, "sq") for _ in range(2)]
        rstd = A([128, 512], F32, "rstd")
        t64a = A([64, 512], F32, "t64a")
        t64b = A([64, 512], F32, "t64b")
        for (c0, dst) in ((4, qn), (8, kvn)):
            for tb in range(NTB):
                lt = kb.ring("lat", lat)
                kb.dma("sp", lt[:], pT_d[c0 * 128:(c0 + 4) * 128, tb * 512:(tb + 1) * 512].rearrange("(c p) t -> p c t", p=128),
                       R=pT_r[c0:c0 + 4], W=[lt.r])
                stats_rows([(lt, 4)], 4, rstd, 1.0 / 512, 1e-6, sqring)
                for c in range(4):
                    kb.tt("dve", dst[:, c, tb * 512:(tb + 1) * 512], lt[:, c, :], rstd[:], ALU.mult, R=[lt.r, rstd.r], PW=[dst.r])
        kp1 = A([64, 512], F32, "kp1")
        kp2 = A([64, 512], F32, "kp2")
        for tb in range(NTB):
            ts_ = slice(tb * 512, (tb + 1) * 512)
            kb.dma("sp", kp1[:], pT_d[12 * 128:12 * 128 + 64, ts_], R=[pT_r[12]], W=[kp1.r])
            kb.dma("sp", kp2[:], pT_d[13 * 128:13 * 128 + 64, ts_], R=[pT_r[13]], W=[kp2.r])
            c_, s_ = load_cs(tb)
            kb.tt("pool", t64a[:], kp1[:], c_[:], ALU.mult, R=[kp1.r, c_.r], W=[t64a.r])
            kb.tt("pool", t64b[:], kp2[:], s_[:], ALU.mult, R=[kp2.r, s_.r], W=[t64b.r])
            kb.tt("pool", kpeT[:, ts_], t64a[:], t64b[:], ALU.add, R=[t64a.r, t64b.r], PW=[kpeT.r])
        Qn = A([128, NT], BF16, "Qn")
        Qp = A([64, NT], BF16, "Qp")
        Kn = A([128, NT], BF16, "Kn")
        V = A([128, NT // 128, 128], BF16, "V")
        Pt = [A([128, 512], BF16, "Pt") for _ in range(3)]
        rl = A([128, 512], F32, "rl")
        ob = [A([128, 512], BF16, "aob") for _ in range(2)]
        for h in range(NH_A):
            wq1 = kb.ring("w", wring)
            kb.dma("sp", wq1[:, 0:512], wb_uq[2 * h], R=[wr["uq"]], W=[wq1.r])
            wq2 = kb.ring("w", wring)
            kb.dma("sp", wq2[:, 0:512], wb_uq[2 * h + 1], R=[wr["uq"]], W=[wq2.r])
            wk = kb.ring("w", wring)
            kb.dma("sp", wk[:, 0:512], wb_uk[h], R=[wr["uk"]], W=[wk.r])
            for tb in range(NTB):
                ts_ = slice(tb * 512, (tb + 1) * 512)
                b = kb.bank()
                for c in range(4):
                    kb.mm(ps[b][:, :], wq1[:, c * 128:(c + 1) * 128], qn[:, c, ts_], start=(c == 0), stop=(c == 3),
                          R=[wq1.r, qn.r], W=[psr[b]] if c == 0 else (), PW=() if c == 0 else [psr[b]])
                kb.act(Qn[:, ts_], ps[b][:], AF.Copy, R=[psr[b]], PW=[Qn.r], scale=SM_SCALE)
                b1 = kb.bank()
                b2 = kb.bank()
                for c in range(4):
                    kb.mm(ps[b1][0:64, :], wq2[:, c * 128:c * 128 + 64], qn[:, c, ts_], start=(c == 0), stop=(c == 3),
                          R=[wq2.r, qn.r], W=[psr[b1]] if c == 0 else (), PW=() if c == 0 else [psr[b1]])
                for c in range(4):
                    kb.mm(ps[b2][0:64, :], wq2[:, c * 128 + 64:c * 128 + 128], qn[:, c, ts_], start=(c == 0), stop=(c == 3),
                          R=[wq2.r, qn.r], W=[psr[b2]] if c == 0 else (), PW=() if c == 0 else [psr[b2]])
                c_, s_ = load_cs(tb)
                kb.tt("dve", t64a[:], ps[b1][0:64, :], c_[:], ALU.mult, R=[psr[b1], c_.r], W=[t64a.r])
                kb.tt("dve", t64b[:], ps[b2][0:64, :], s_[:], ALU.mult, R=[psr[b2], s_.r], W=[t64b.r])
                kb.tt("dve", t64a[:], t64a[:], t64b[:], ALU.add, R=[t64a.r, t64b.r], W=[t64a.r])
                kb.ts("pool", Qp[:, ts_], t64a[:], SM_SCALE, None, ALU.mult, R=[t64a.r], PW=[Qp.r])
                b = kb.bank()
                for c in range(4):
                    kb.mm(ps[b][:, :], wk[:, c * 128:(c + 1) * 128], kvn[:, c, ts_], start=(c == 0), stop=(c == 3),
                          R=[wk.r, kvn.r], W=[psr[b]] if c == 0 else (), PW=() if c == 0 else [psr[b]])
                kb.copy("act", Kn[:, ts_], ps[b][:], R=[psr[b]], PW=[Kn.r])
                b = kb.bank()
                for i4 in range(4):
                    it = tb * 4 + i4
                    for c in range(4):
                        kb.mm(ps[b][:, i4 * 128:(i4 + 1) * 128], kvn[:, c, it * 128:(it + 1) * 128], wv[:, c, h * 128:(h + 1) * 128],
                              start=(c == 0), stop=(c == 3), R=[kvn.r, wv.r],
                              W=[psr[b]] if (c == 0 and i4 == 0) else (), PW=() if (c == 0 and i4 == 0) else [psr[b]])
                kb.copy("dve", V[:, tb * 4:(tb + 1) * 4, :], ps[b][:].rearrange("p (a b) -> p a b", a=4), R=[psr[b]], PW=[V.r])
            for g in range(NTB):
                bO = kb.bank()
                bL = kb.bank()
                nj = 4 * g + 4
                for jt in range(nj):
                    q0 = max(0, jt - 4 * g) * 128
                    bS = kb.bank(avoid=(bO, bL))
                    qs = slice(g * 512 + q0, (g + 1) * 512)
                    kb.mm(ps[bS][:, q0:512], Kn[:, jt * 128:(jt + 1) * 128], Qn[:, qs], start=True, stop=False,
                          R=[Kn.r, Qn.r], W=[psr[bS]])
                    kb.mm(ps[bS][:, q0:512], kpeT[:, jt * 128:(jt + 1) * 128], Qp[:, qs], start=False, stop=True,
                          R=[kpeT.r, Qp.r], PW=[psr[bS]])
                    p_ = kb.ring("Pt", Pt)
                    kb.act(p_[:, q0:512], ps[bS][:, q0:512], AF.Exp, R=[psr[bS]], W=[p_.r])
                    if jt >= 4 * g:
                        kb.tt("pool", p_[:, q0:q0 + 128], p_[:, q0:q0 + 128], tri[:], ALU.mult, R=[p_.r, tri.r], W=[p_.r])
                    kb.mm(ps[bO][:, q0:512], V[:, jt, :], p_[:, q0:512], start=(jt == 0), stop=(jt == nj - 1),
                          R=[V.r, p_.r], W=[psr[bO]] if jt == 0 else (), PW=() if jt == 0 else [psr[bO]])
                    kb.mm(ps[bL][:, q0:512], ones_bf[:], p_[:, q0:512], start=(jt == 0), stop=(jt == nj - 1),
                          R=[ones_bf.r, p_.r], W=[psr[bL]] if jt == 0 else (), PW=() if jt == 0 else [psr[bL]])
                    bg_step(bg)
                S.add("dve", lambda e, bL=bL: e.reciprocal(out=rl[:], in_=ps[bL][:]), [psr[bL]], [rl.r])
                o_ = kb.ring("aob", ob)
                kb.tt("dve", o_[:], ps[bO][:], rl[:], ALU.mult, R=[psr[bO], rl.r], W=[o_.r])
                kb.dma("pool", mixT_d[512 + h * 128:512 + (h + 1) * 128, g * 512:(g + 1) * 512], o_[:], R=[o_.r], PW=[mix_r[4 + h]])
        bg_drain(bg)

    def final_phase():
        kb.phase(base_mark)
        hring = [kb.alloc([128, 4, 512], F32, "hr") for _ in range(4)]
        sqring = [kb.alloc([128, 4, 512], BF16, "sq") for _ in range(2)]
        rstd = kb.alloc([128, 512], F32, "rstd")
        yn = [kb.alloc([128, 4, 512], F32, "yn") for _ in range(2)]
        ot = [kb.alloc([128, D], F32, "ot") for _ in range(4)]
        outs = []
        for tb in range(NTB):
            tiles = []
            for q in range(4):
                ht = kb.ring("h", hring)
                kb.dma("sp", ht[:], hT_v[:, q * 4:(q + 1) * 4, tb * 512:(tb + 1) * 512], R=hT_r[tb][q * 4:(q + 1) * 4], W=[ht.r])
                tiles.append((ht, 4))
            stats_rows(tiles, 16, rstd, 1.0 / D, 1e-6, sqring)
            for q in range(4):
                ht = tiles[q][0]
                y_ = kb.ring("yn", yn)
                for c4 in range(4):
                    kb.stt(y_[:, c4, :], ht[:, c4, :], col("final_norm", q * 4 + c4), rstd[:], ALU.mult, ALU.mult,
                           R=[ht.r, rstd.r, cols.r], PW=[y_.r])
                for i4 in range(4):
                    o_ = ot[i4]
                    b = kb.bank()
                    for c4 in range(4):
                        kb.tr(ps[b][:, c4 * 128:(c4 + 1) * 128], y_[:, c4, i4 * 128:(i4 + 1) * 128], ident[:], R=[y_.r, ident.r],
                              W=[psr[b]] if c4 == 0 else (), PW=() if c4 == 0 else [psr[b]])
                    kb.copy("act" if i4 % 2 else "dve", o_[:, q * 512:(q + 1) * 512], ps[b][:], R=[psr[b]],
                            W=[o_.r] if q == 0 else (), PW=() if q == 0 else [o_.r])
            for i4 in range(4):
                t0 = tb * 512 + i4 * 128
                outs.append(kb.dma("pool", out_d[t0:t0 + 128, :], ot[i4][:], R=[ot[i4].r]))
        return outs

    def debug_dump():
        kb.phase(base_mark)
        t_ = [kb.alloc([128, 512], F32, "dbg") for _ in range(2)]
        outs = []
        for tb in range(NTB):
            for c in range(16):
                a = kb.ring("dbg", t_)
                kb.dma("sp", a[:], hT_d[c * 128:(c + 1) * 128, tb * 512:(tb + 1) * 512], R=[hT_r[tb][c]], W=[a.r])
                outs.append(kb.dma("pool", dbg_d[c * 128:(c + 1) * 128, tb * 512:(tb + 1) * 512], a[:], R=[a.r]))
        for tb in range(NTB):
            for c in range(EV_FC):
                a = kb.ring("dbg", t_)
                kb.dma("sp", a[:], pT_d[c * 128:(c + 1) * 128, tb * 512:(tb + 1) * 512], R=[pT_r[c]], W=[a.r])
                outs.append(kb.dma("pool", dbgp_d[c * 128:(c + 1) * 128, tb * 512:(tb + 1) * 512], a[:], R=[a.r]))
        tb_ = [kb.alloc([128, 512], BF16, "dbgb") for _ in range(2)]
        for tb in range(NTB):
            for c in range(16):
                a = kb.ring("dbgb", tb_)
                kb.dma("sp", a[:], mixT_d[c * 128:(c + 1) * 128, tb * 512:(tb + 1) * 512], R=[mix_r[c]], W=[a.r])
                outs.append(kb.dma("pool", dbgm_d[c * 128:(c + 1) * 128, tb * 512:(tb + 1) * 512], a[:], R=[a.r]))
        return outs

    load_phase()
    rope_phase()
    done = False
    for L in range(depth):
        S.new_epoch()
        j = L // 2
        if L % 2 == 0:
            inproj_phase(win_e[j], EV_FC, f"ev_norm{j}", precast=(L > 0))
            conv_phase(j)
            rwkv_phase(j, jobs_out_ffn(L))
            outproj_phase(wout_e[j])
        else:
            inproj_phase(win_o[j], OD_FC, f"od_norm{j}", precast=True)
            pool_phase(j)
            mla_phase(j, jobs_out_ffn(L))
            outproj_phase(wout_o[j])
        if stop_after == (L, "mix"):
            done = True
            break
        S.new_epoch()
        ffn_phase(L, jobs_in(L + 1) if L + 1 < depth else None)
        if stop_after == (L, "ffn"):
            done = True
            break
    S.new_epoch()
    outs = final_phase()
    if stop_after is not None:
        outs = outs + debug_dump() + dbg_outs
    with ExitStack() as st:
        stats = S.emit(st, final_waits=outs)
    return nc, stats


def _colpack(vec, parts=128):
    v = np.asarray(vec, np.float32).reshape(-1)
    n = (len(v) + parts - 1) // parts
    out = np.zeros((128, n), np.float32)
    vv = np.zeros(n * parts, np.float32)
    vv[:len(v)] = v
    out[:parts, :] = vv.reshape(n, parts).T
    return out


def prepare(inputs, NT):
    f32 = lambda a: np.ascontiguousarray(np.asarray(a, np.float32))
    colmap = {}
    colsl = []
    ncol = [0]

    def addc(name, arr):
        colmap[name] = ncol[0]
        colsl.append(arr)
        ncol[0] += arr.shape[1]

    shared = {}
    for j in range(2):
        addc(f"ev_norm{j}", _colpack(inputs["ev_norm"][j]))
        addc(f"od_norm{j}", _colpack(inputs["od_norm"][j]))
        cw = f32(inputs["ev_conv_w"][j])
        addc(f"conv{j}", cw.reshape(8, 128, 3).transpose(1, 0, 2).reshape(128, 24))
        mu = f32(inputs["ev_mu"][j])
        addc(f"mu_r{j}", _colpack(mu[0:1024], 64))
        addc(f"mu_k{j}", _colpack(mu[1024:2048], 64))
        addc(f"mu_v{j}", _colpack(mu[2048:3072], 64))
        addc(f"mu_xw{j}", _colpack(mu[3072:3136], 64))
        addc(f"mu_xa{j}", _colpack(mu[3136:3200], 64))
        addc(f"mu_xg0{j}", _colpack(mu[3200:3328], 128))
        addc(f"mu_xg1{j}", _colpack(mu[3328:3360], 32))
        for nm in ("w0", "a0", "k_k", "k_a", "ln_w", "ln_b"):
            addc(f"{nm}{j}", _colpack(inputs["ev_" + nm][j], 64))
        addc(f"r_k{j}", _colpack(f32(inputs["ev_r_k"][j]).reshape(-1), 64))
        addc(f"pool_scale{j}", _colpack(inputs["od_pool_scale"][j]))
        addc(f"q_norm{j}", _colpack(inputs["od_q_norm"][j]))
        addc(f"kv_norm{j}", _colpack(inputs["od_kv_norm"][j]))
    for l in range(4):
        addc(f"ffn_norm{l}", _colpack(inputs["ffn_norm"][l]))
    addc("final_norm", _colpack(inputs["final_norm"]))
    inv = (1.0 / (10000.0 ** (np.arange(0, 64, 2, dtype=np.float32) / np.float32(64)))).astype(np.float32)
    addc("invf", _colpack(np.concatenate([inv, inv]), 64))
    addc("negpi", _colpack(np.full(64, -math.pi, np.float32), 64))
    addc("negone", _colpack(np.full(64, -1.0, np.float32), 64))
    addc("sinsign", _colpack(np.concatenate([np.full(32, -1.0), np.full(32, 1.0)]).astype(np.float32), 64))
    addc("eps6", _colpack(np.full(128, 1e-6, np.float32)))
    addc("gneps", _colpack(np.full(128, 64e-5, np.float32)))
    addc("tiny", _colpack(np.full(128, 1e-24, np.float32)))
    cols = np.ascontiguousarray(np.concatenate(colsl, axis=1))
    shared["cols"] = cols
    shared["ident"] = np.eye(128, dtype=np.float32)
    s_ = np.arange(64)[:, None]
    t_ = np.arange(64)[None, :]
    m = np.zeros((64, 5, 8, 64), np.float32)
    m[:, 0] = (s_ < t_).astype(np.float32)[:, None, :]
    m[:, 1] = (s_ > t_).astype(np.float32)[:, None, :]
    m[:, 2] = (s_ <= t_).astype(np.float32)[:, None, :]
    m[:, 3] = np.eye(64, dtype=np.float32)[:, None, :]
    rst = np.ones((8, 64), np.float32)
    rst[:, 0] = 0.0
    m[:, 4] = rst[None]
    shared["masks"] = np.ascontiguousarray(m.reshape(64, 5, 512))
    shared["tri"] = (np.arange(128)[None, :] >= np.arange(128)[:, None]).astype(ml_dtypes.bfloat16)
    pf = np.zeros((128, 4, 16), np.float32)
    for g in range(4):
        win = 2 << g
        pf[:, g, :] = 1.0 / np.minimum(np.arange(16) + 1, win)
    shared["poolfix"] = pf
    for j in range(2):
        w = f32(inputs["ev_w_in"][j])
        wp = np.zeros((D, EV_FC * 128), np.float32)
        wp[:, 0:6144] = w[:, 0:6144]
        wp[:, 6144:6144 + 64] = w[:, 6144:6208]
        wp[:, 6272:6272 + 64] = w[:, 6208:6272]
        wp[:, 6400:6400 + 128] = w[:, 6272:6400]
        wp[:, 6528:6528 + 32] = w[:, 6400:6432]
        shared[f"ev_w_in{j}"] = wp
        shared[f"ev_w_out{j}"] = f32(inputs["ev_w_out"][j])
        shared[f"ev_w2{j}"] = f32(inputs["ev_w2"][j])
        shared[f"ev_a2{j}"] = f32(inputs["ev_a2"][j])
        shared[f"ev_g2{j}"] = f32(inputs["ev_g2"][j])
        w = f32(inputs["od_w_in"][j])
        wp = np.zeros((D, OD_FC * 128), np.float32)
        wp[:, 0:1536] = w[:, 0:1536]
        kpe = w[:, 1536:1600]
        wp[:, 1536:1600] = kpe
        wp[:, 1664:1696] = kpe[:, 32:64]
        wp[:, 1696:1728] = kpe[:, 0:32]
        shared[f"od_w_in{j}"] = wp
        shared[f"od_w_out{j}"] = f32(inputs["od_w_out"][j])
        shared[f"od_pool_w{j}"] = f32(inputs["od_pool_w"][j])
        wq = f32(inputs["od_w_uq"][j]).reshape(512, 12, 192)
        wqp = np.zeros((512, 12, 256), np.float32)
        wqp[:, :, 0:128] = wq[:, :, 0:128]
        wqp[:, :, 128:192] = wq[:, :, 128:192]
        wqp[:, :, 192:224] = wq[:, :, 160:192]
        wqp[:, :, 224:256] = wq[:, :, 128:160]
        shared[f"od_w_uq{j}"] = np.ascontiguousarray(wqp.reshape(512, 3072))
        wkv = f32(inputs["od_w_ukv"][j]).reshape(512, 12, 256)
        shared[f"od_w_uk{j}"] = np.ascontiguousarray(wkv[:, :, 0:128].reshape(512, 1536))
        shared[f"od_w_uv{j}"] = np.ascontiguousarray(wkv[:, :, 128:256].reshape(512, 1536))
    for l in range(4):
        shared[f"ffn_wg{l}"] = f32(inputs["ffn_w_gate"][l])
        shared[f"ffn_wu{l}"] = f32(inputs["ffn_w_up"][l])
        shared[f"ffn_wd{l}"] = f32(inputs["ffn_w_down"][l])
    return shared, colmap, cols.shape[1]


_CACHE = {}


def kernel(**inputs):
    x = np.asarray(inputs["x"], np.float32)
    pos = np.asarray(inputs["positions"], np.int32)
    B, NT, _ = x.shape
    shared, colmap, ncol = prepare(inputs, NT)
    key = (NT, ncol)
    if key not in _CACHE:
        _CACHE[key] = build_program(NT, colmap, ncol)
    nc, stats = _CACHE[key]
    n_cores = 8
    in_maps = []
    for c in range(n_cores):
        b = c % B
        m = dict(shared)
        m["x"] = np.ascontiguousarray(x[b])
        m["pos"] = np.ascontiguousarray(pos[b].reshape(1, NT))
        in_maps.append(m)
    res = run_bass_kernel_spmd(nc, in_maps, core_ids=list(range(n_cores)))
    out = np.stack([np.asarray(res.results[b]["out"], np.float32) for b in range(B)], axis=0)
    return out
```

```python
import math
import numpy as np
import ml_dtypes
from contextlib import ExitStack
import concourse.bass as bass
import concourse.mybir as mybir
from concourse.bass_utils import run_bass_kernel_spmd

F32 = mybir.dt.float32
BF16 = mybir.dt.bfloat16
I32 = mybir.dt.int32
ALU = mybir.AluOpType
AF = mybir.ActivationFunctionType

ENGS = ("pe", "act", "dve", "pool", "sp")
DMA_RING = 8

D = 2048
DFF = 5632
FFC = 44
NH_R = 16
NH_A = 12
EV_FC = 52
OD_FC = 14
SM_SCALE = 192.0 ** -0.5


class Res:
    __slots__ = ("name", "writers", "readers", "prev_readers")

    def __init__(self, name=""):
        self.name = name
        self.writers = []
        self.readers = []
        self.prev_readers = []


class Op:
    __slots__ = ("eng", "fn", "deps", "sig", "sig_no", "dma", "dma_idx", "epoch", "idx")


class Sched:
    def __init__(self, nc):
        self.nc = nc
        self.ops = {e: [] for e in ENGS}
        self.epoch = 0
        self.dma_count = {e: 0 for e in ENGS}
        self.pending = {e: None for e in ENGS}

    def new_epoch(self):
        self.epoch += 1

    def barrier(self):
        last = []
        for e in ENGS:
            ops = self.ops[e]
            if not ops:
                continue
            if e == "sp" or any(o.dma for o in ops[-DMA_RING * 4:]):
                dm = [o for o in ops if o.dma][-DMA_RING:]
                last.extend(dm)
            nd = [o for o in ops if not o.dma]
            if nd:
                last.append(nd[-1])
        for e in ENGS:
            self.pending[e] = list(last)

    def add(self, eng, fn, reads=(), writes=(), dma=False, pwrites=()):
        op = Op()
        op.eng = eng
        op.fn = fn
        op.dma = dma
        op.sig = False
        op.sig_no = None
        op.epoch = self.epoch
        op.idx = len(self.ops[eng])
        deps = {}

        def add_dep(p):
            if p.dma:
                deps[id(p)] = p
                return
            if p.eng == eng and not dma and eng == "pe":
                return
            key = ("c", p.eng)
            q = deps.get(key)
            if q is None or q.idx < p.idx:
                deps[key] = p

        if self.pending[eng] is not None:
            for p in self.pending[eng]:
                add_dep(p)
            self.pending[eng] = None
        for r in reads:
            for p in r.writers:
                add_dep(p)
        for w in writes:
            for p in w.writers:
                add_dep(p)
            for rd in w.readers:
                add_dep(rd)
        for w in pwrites:
            for rd in w.readers:
                add_dep(rd)
            for rd in w.prev_readers:
                add_dep(rd)
        for r in reads:
            if not dma:
                r.readers = [x for x in r.readers if x.dma or x.eng != eng]
            r.readers.append(op)
        for w in writes:
            w.prev_readers = w.readers
            w.writers = [op]
            w.readers = []
        for w in pwrites:
            if w.readers:
                w.prev_readers = w.readers
                w.writers = [op]
                w.readers = []
            else:
                if not dma:
                    w.writers = [x for x in w.writers if x.dma or x.eng != eng]
                w.writers.append(op)
        op.deps = list(deps.values())
        for p in op.deps:
            p.sig = True
        if dma:
            op.dma_idx = self.dma_count[eng]
            self.dma_count[eng] += 1
            op.sig = True
        self.ops[eng].append(op)
        return op

    def emit(self, stack, final_waits=()):
        nc = self.nc
        n_epochs = self.epoch + 1
        csem = {}
        for e in ("pe", "act", "dve", "pool"):
            for ep in range(n_epochs):
                if any(o.sig and not o.dma and o.epoch == ep for o in self.ops[e]):
                    csem[(e, ep)] = stack.enter_context(nc.semaphore(f"c_{e}_{ep}"))
        dsem = {}
        for e in ENGS:
            if self.dma_count[e]:
                dsem[e] = [stack.enter_context(nc.semaphore(f"d_{e}_{i}")) for i in range(DMA_RING)]
        for e in ENGS:
            cnt = {}
            for o in self.ops[e]:
                if o.dma:
                    continue
                if o.sig:
                    cnt[o.epoch] = cnt.get(o.epoch, 0) + 1
                    o.sig_no = cnt[o.epoch]

        def waits_for(o):
            ws = []
            for p in o.deps:
                if p.dma:
                    ws.append((dsem[p.eng][p.dma_idx % DMA_RING], 16 * (p.dma_idx // DMA_RING + 1)))
                else:
                    ws.append((csem[(p.eng, p.epoch)], p.sig_no))
            if o.dma and o.dma_idx >= DMA_RING:
                ws.append((dsem[o.eng][o.dma_idx % DMA_RING], 16 * (o.dma_idx // DMA_RING)))
            return ws

        block = stack.enter_context(nc.Block())
        engmap = {"pe": block.tensor, "act": block.scalar, "dve": block.vector,
                  "pool": block.gpsimd, "sp": block.sync}
        stats = {}
        for e in ENGS:
            ops = self.ops[e]
            if not ops:
                continue
            stats[e] = len(ops)

            def body(eng, ops=ops, e=e):
                have = {}
                for o in ops:
                    for (s, v) in waits_for(o):
                        k = id(s)
                        if have.get(k, 0) >= v:
                            continue
                        have[k] = v
                        eng.wait_ge(s, v)
                    ins = o.fn(eng)
                    if o.sig:
                        if o.dma:
                            ins.then_inc(dsem[o.eng][o.dma_idx % DMA_RING], 16)
                        else:
                            ins.then_inc(csem[(o.eng, o.epoch)], 1)
                if e == "sp":
                    for o in final_waits:
                        eng.wait_ge(dsem[o.eng][o.dma_idx % DMA_RING], 16 * (o.dma_idx // DMA_RING + 1))
            engmap[e](body)
        return stats


class T:
    def __init__(self, h, name=""):
        self.h = h
        self.r = Res(name)

    def __getitem__(self, k):
        return self.h[k]


DTSZ = {F32: 4, BF16: 2, I32: 4}


class KB:
    def __init__(self, nc, NT):
        self.nc = nc
        self.NT = NT
        self.NTB = NT // 512
        self.S = Sched(nc)
        self.cur = 16512
        self.end = 229376
        self.uid = 0
        self.bank_i = 0
        self.wi = 0
        self.ri = {}
        self.held = set()

    def alloc(self, shape, dt, name="t"):
        n = 1
        for s in shape[1:]:
            n *= s
        nb = (n * DTSZ[dt] + 63) // 64 * 64
        off = self.cur
        self.cur += nb
        assert self.cur <= self.end, f"SBUF overflow {self.cur} allocating {name}"
        self.uid += 1
        h = self.nc.alloc_sbuf_tensor_at(f"{name}{self.uid}", list(shape), dt, offset=off)
        return T(h, name)

    def mark(self):
        return self.cur

    def phase(self, mark):
        self.cur = mark
        self.S.barrier()

    def bank(self, avoid=(), hold=False):
        while True:
            b = self.bank_i % 8
            self.bank_i += 1
            if b not in avoid and b not in self.held:
                if hold:
                    self.held.add(b)
                return b

    def release(self, *bs):
        for b in bs:
            self.held.discard(b)

    def ring(self, key, tiles):
        i = self.ri.get(key, 0)
        self.ri[key] = i + 1
        return tiles[i % len(tiles)]

    def mm(self, out, lhsT, rhs, start=True, stop=True, R=(), W=(), PW=()):
        return self.S.add("pe", lambda e: e.matmul(out, lhsT=lhsT, rhs=rhs, start=start, stop=stop), R, W, pwrites=PW)

    def tr(self, out, in_, ident, R=(), W=(), PW=()):
        return self.S.add("pe", lambda e: e.transpose(out=out, in_=in_, identity=ident), R, W, pwrites=PW)

    def act(self, out, in_, func, R=(), W=(), PW=(), **kw):
        return self.S.add("act", lambda e: e.activation(out=out, in_=in_, func=func, **kw), R, W, pwrites=PW)

    def tt(self, eng, out, in0, in1, op, R=(), W=(), PW=()):
        return self.S.add(eng, lambda e: e.tensor_tensor(out=out, in0=in0, in1=in1, op=op), R, W, pwrites=PW)

    def ts(self, eng, out, in0, s1, s2, op0, op1=None, R=(), W=(), PW=()):
        if op1 is None:
            return self.S.add(eng, lambda e: e.tensor_scalar(out=out, in0=in0, scalar1=s1, scalar2=None, op0=op0), R, W, pwrites=PW)
        return self.S.add(eng, lambda e: e.tensor_scalar(out=out, in0=in0, scalar1=s1, scalar2=s2, op0=op0, op1=op1), R, W, pwrites=PW)

    def stt(self, out, in0, scalar, in1, op0, op1, R=(), W=(), PW=()):
        return self.S.add("dve", lambda e: e.scalar_tensor_tensor(out=out, in0=in0, scalar=scalar, in1=in1, op0=op0, op1=op1), R, W, pwrites=PW)

    def copy(self, eng, out, in_, R=(), W=(), PW=()):
        if eng == "act":
            return self.S.add("act", lambda e: e.copy(out=out, in_=in_), R, W, pwrites=PW)
        return self.S.add(eng, lambda e: e.tensor_copy(out=out, in_=in_), R, W, pwrites=PW)

    def memset(self, eng, ap, val, W=(), PW=()):
        return self.S.add(eng, lambda e: e.memset(ap, val), (), W, pwrites=PW)

    def dma(self, q, out, in_, R=(), W=(), PW=()):
        return self.S.add(q, lambda e: e.dma_start(out=out, in_=in_), R, W, dma=True, pwrites=PW)


def build_program(NT, colmap, NCOL, stop_after=None, depth=4):
    nc = bass.Bass("TRN2", target_bir_lowering=False)
    kb = KB(nc, NT)
    S = kb.S
    NTB = NT // 512
    dram_in = lambda n, shp, dt: nc.dram_tensor(n, list(shp), dt, kind="ExternalInput").ap()
    dram_tmp = lambda n, shp, dt: nc.dram_tensor(n, list(shp), dt, kind="Internal").ap()

    x_d = dram_in("x", [NT, D], F32)
    pos_d = dram_in("pos", [1, NT], I32)
    cols_d = dram_in("cols", [128, NCOL], F32)
    ident_d = dram_in("ident", [128, 128], F32)
    masks_d = dram_in("masks", [64, 5, 512], F32)
    tri_d = dram_in("tri", [128, 128], BF16)
    poolfix_d = dram_in("poolfix", [128, 4, 16], F32)
    win_e = [dram_in(f"ev_w_in{j}", [D, EV_FC * 128], F32) for j in range(2)]
    wout_e = [dram_in(f"ev_w_out{j}", [D, D], F32) for j in range(2)]
    w2_d = [dram_in(f"ev_w2{j}", [64, 1024], F32) for j in range(2)]
    a2_d = [dram_in(f"ev_a2{j}", [64, 1024], F32) for j in range(2)]
    g2_d = [dram_in(f"ev_g2{j}", [160, 1024], F32) for j in range(2)]
    win_o = [dram_in(f"od_w_in{j}", [D, OD_FC * 128], F32) for j in range(2)]
    wout_o = [dram_in(f"od_w_out{j}", [D, D], F32) for j in range(2)]
    poolw_d = [dram_in(f"od_pool_w{j}", [4, 128, 128], F32) for j in range(2)]
    wuq_d = [dram_in(f"od_w_uq{j}", [512, 24 * 128], F32) for j in range(2)]
    wuk_d = [dram_in(f"od_w_uk{j}", [512, 12 * 128], F32) for j in range(2)]
    wuv_d = [dram_in(f"od_w_uv{j}", [512, 12 * 128], F32) for j in range(2)]
    wg_d = [dram_in(f"ffn_wg{l}", [D, DFF], F32) for l in range(4)]
    wu_d = [dram_in(f"ffn_wu{l}", [D, DFF], F32) for l in range(4)]
    wd_d = [dram_in(f"ffn_wd{l}", [DFF, D], F32) for l in range(4)]
    out_d = nc.dram_tensor("out", [NT, D], F32, kind="ExternalOutput").ap()
    dbg_d = nc.dram_tensor("dbg", [D, NT], F32, kind="ExternalOutput").ap() if stop_after is not None else None
    dbgp_d = nc.dram_tensor("dbgp", [EV_FC * 128, NT], F32, kind="ExternalOutput").ap() if stop_after is not None else None
    dbgm_d = nc.dram_tensor("dbgm", [D, NT], BF16, kind="ExternalOutput").ap() if stop_after is not None else None
    dbgr_d = nc.dram_tensor("dbgr", [26, 64, 512], F32, kind="ExternalOutput").ap() if stop_after is not None else None
    dbg_outs = []

    hT_d = dram_tmp("hT", [D, NT], F32)
    pT_d = dram_tmp("pT", [EV_FC * 128, NT], F32)
    mixT_d = dram_tmp("mixT", [D, NT], BF16)
    cs_d = dram_tmp("cs", [2, 64, NT], F32)
    wb_in = dram_tmp("wb_in", [EV_FC, 128, 16 * 128], BF16)
    wb_out = dram_tmp("wb_out", [16, 128, 16 * 128], BF16)
    wb_g = dram_tmp("wb_g", [FFC, 128, 16 * 128], BF16)
    wb_u = dram_tmp("wb_u", [FFC, 128, 16 * 128], BF16)
    wb_d = dram_tmp("wb_d", [16, 128, FFC * 128], BF16)
    wb_uq = dram_tmp("wb_uq", [24, 128, 4 * 128], BF16)
    wb_uk = dram_tmp("wb_uk", [12, 128, 4 * 128], BF16)
    hT_r = [[Res(f"hT{i}_{c}") for c in range(16)] for i in range(NTB)]
    pT_r = [Res(f"pT{i}") for i in range(EV_FC)]
    mix_r = [Res(f"mix{i}") for i in range(16)]
    cs_r = Res("cs")
    wr = {k: Res(k) for k in ("in", "out", "g", "u", "d", "uq", "uk")}
    hT_v = hT_d.rearrange("(c p) t -> p c t", p=128)
    mixT_v = mixT_d.rearrange("(c p) t -> p c t", p=128)

    ps = [nc.alloc_psum_tensor(f"ps{i}", [128, 512], F32) for i in range(8)]
    psr = [Res(f"ps{i}") for i in range(8)]
    cols = kb.alloc([128, NCOL], F32, "cols")
    ident = kb.alloc([128, 128], F32, "ident")
    ones_bf = kb.alloc([128, 128], BF16, "ones")
    ones64 = kb.alloc([64, 64], F32, "ones64")
    mean64 = kb.alloc([64, 64], F32, "mean64")
    masks = kb.alloc([64, 5, 512], F32, "masks")
    tri = kb.alloc([128, 128], BF16, "tri")
    wring = [kb.alloc([128, FFC * 128], BF16, "wring") for _ in range(3)]
    kb.dma("sp", cols[:], cols_d, W=[cols.r])
    kb.dma("sp", ident[:], ident_d, W=[ident.r])
    kb.dma("sp", masks[:], masks_d, W=[masks.r])
    kb.dma("sp", tri[:], tri_d, W=[tri.r])
    kb.memset("pool", ones_bf[:], 1.0, W=[ones_bf.r])
    kb.memset("pool", ones64[:], 1.0, W=[ones64.r])
    kb.memset("pool", mean64[:], 1.0 / 64.0, W=[mean64.r])
    id64b_t = kb.alloc([64, 64], BF16, "id64b")
    kb.copy("act", id64b_t[:], ident[0:64, 0:64], R=[ident.r], W=[id64b_t.r])
    base_mark = kb.mark()

    def col(name, j=0, parts=128):
        i = colmap[name] + j
        return cols[0:parts, i:i + 1]

    def cast_weight(src, K, F, dst, dres, gain=None):
        S.barrier()
        m = kb.mark()
        wf = [kb.alloc([128, 2048], F32, "wf") for _ in range(3)]
        wbt = [kb.alloc([128, 2048], BF16, "wbt") for _ in range(3)]
        KCn = K // 128
        i = 0
        for c in range(KCn):
            for f0 in range(0, F, 2048):
                fw = min(2048, F - f0)
                a = wf[i % 3]
                b = wbt[i % 3]
                kb.dma("sp", a[:, 0:fw], src[c * 128:(c + 1) * 128, f0:f0 + fw], W=[a.r])
                eng = ("act", "pool", "dve")[i % 3]
                if gain is None:
                    kb.copy(eng, b[:, 0:fw], a[:, 0:fw], R=[a.r], W=[b.r])
                elif eng == "act":
                    kb.act(b[:, 0:fw], a[:, 0:fw], AF.Copy, R=[a.r, cols.r], W=[b.r], scale=col(gain, c))
                else:
                    kb.ts(eng, b[:, 0:fw], a[:, 0:fw], col(gain, c), None, ALU.mult, R=[a.r, cols.r], W=[b.r])
                dv = dst[f0 // 128:(f0 + fw) // 128, :, c * 128:(c + 1) * 128].rearrange("fc p f -> p fc f")
                kb.dma("pool", dv, b[:, 0:fw].rearrange("p (fc f) -> p fc f", f=128), R=[b.r], PW=[dres])
                i += 1
        kb.cur = m
        S.barrier()

    class Slot:
        pass

    def wring_slots():
        out = []
        for wt in wring:
            sl = Slot()
            sl.f32 = wt[:, 512:2560].bitcast(F32)
            sl.b16 = wt[:, 2560:3584]
            sl.rf = Res("slf")
            sl.rb = Res("slb")
            out.append(sl)
        return out

    def alloc_slots(n=3):
        out = []
        for _ in range(n):
            a = kb.alloc([128, 1024], F32, "slf")
            b = kb.alloc([128, 1024], BF16, "slb")
            sl = Slot()
            sl.f32, sl.b16, sl.rf, sl.rb = a[:], b[:], a.r, b.r
            out.append(sl)
        return out

    def cast_steps(jobs, slots):
        tiles = []
        for (src, K, F, dst, dres, gain) in jobs:
            for c in range(K // 128):
                for f0 in range(0, F, 1024):
                    tiles.append((src, dst, dres, gain, c, f0, min(1024, F - f0)))
        n = len(tiles)
        NS = len(slots)
        for k in range(n + 2):
            if k < n:
                (src, dst, dres, gain, c, f0, fw) = tiles[k]
                sl = slots[k % NS]
                kb.dma("sp", sl.f32[:, 0:fw], src[c * 128:(c + 1) * 128, f0:f0 + fw], W=[sl.rf])
            if 0 <= k - 1 < n:
                (src, dst, dres, gain, c, f0, fw) = tiles[k - 1]
                sl = slots[(k - 1) % NS]
                if gain is None:
                    kb.copy("act", sl.b16[:, 0:fw], sl.f32[:, 0:fw], R=[sl.rf], W=[sl.rb])
                else:
                    kb.ts("dve", sl.b16[:, 0:fw], sl.f32[:, 0:fw], col(gain, c), None, ALU.mult, R=[sl.rf, cols.r], W=[sl.rb])
            if 0 <= k - 2 < n:
                (src, dst, dres, gain, c, f0, fw) = tiles[k - 2]
                sl = slots[(k - 2) % NS]
                dv = dst[f0 // 128:(f0 + fw) // 128, :, c * 128:(c + 1) * 128].rearrange("fc p f -> p fc f")
                kb.dma("sp", dv, sl.b16[:, 0:fw].rearrange("p (fc f) -> p fc f", f=128), R=[sl.rb], PW=[dres])
            yield

    def bg_step(it):
        if it[0] is not None:
            try:
                next(it[0])
            except StopIteration:
                it[0] = None

    def bg_drain(it):
        while it[0] is not None:
            bg_step(it)

    def jobs_out_ffn(L):
        wo = wout_e[L // 2] if L % 2 == 0 else wout_o[L // 2]
        return [(wo, D, D, wb_out, wr["out"], None),
                (wg_d[L], D, DFF, wb_g, wr["g"], f"ffn_norm{L}"),
                (wu_d[L], D, DFF, wb_u, wr["u"], f"ffn_norm{L}"),
                (wd_d[L], DFF, D, wb_d, wr["d"], None)]

    def jobs_in(L):
        j_ = L // 2
        if L % 2 == 0:
            return [(win_e[j_], D, EV_FC * 128, wb_in, wr["in"], f"ev_norm{j_}")]
        return [(win_o[j_], D, OD_FC * 128, wb_in, wr["in"], f"od_norm{j_}"),
                (wuq_d[j_], 512, 24 * 128, wb_uq, wr["uq"], f"q_norm{j_}"),
                (wuk_d[j_], 512, 12 * 128, wb_uk, wr["uk"], f"kv_norm{j_}")]

    def linear(xT, KCn, wd, wres, flist, evac):
        for f in flist:
            wt = kb.ring("w", wring)
            kb.dma("sp", wt[:, 0:KCn * 128], wd[f], R=[wres], W=[wt.r])
            b = kb.bank()
            for c in range(KCn):
                kb.mm(ps[b][:, :], wt[:, c * 128:(c + 1) * 128], xT[:, c, :], start=(c == 0), stop=(c == KCn - 1),
                      R=[wt.r, xT.r], W=[psr[b]] if c == 0 else (), PW=() if c == 0 else [psr[b]])
            evac(f, b)

    def stats_rows(src_tiles, nchunks_total, rstd, inv_n, eps, sqring):
        b = kb.bank()
        k = 0
        for (tile_, n) in src_tiles:
            sq = kb.ring("sq", sqring)
            kb.tt("pool", sq[:, 0:n, :], tile_[:, 0:n, :], tile_[:, 0:n, :], ALU.mult, R=[tile_.r], W=[sq.r])
            for j in range(n):
                kb.mm(ps[b][:, :], ones_bf[:], sq[:, j, :], start=(k == 0), stop=(k == nchunks_total - 1),
                      R=[ones_bf.r, sq.r], W=[psr[b]] if k == 0 else (), PW=() if k == 0 else [psr[b]])
                k += 1
        kb.act(rstd[:], ps[b][:], AF.Sqrt, R=[psr[b], cols.r], W=[rstd.r], scale=inv_n, bias=col("eps6"))
        S.add("dve", lambda e: e.reciprocal(out=rstd[:], in_=rstd[:]), [rstd.r], [rstd.r])

    def norm_block(tb, xn, rstd, hring, sqring):
        tiles = []
        for q in range(4):
            ht = kb.ring("h", hring)
            kb.dma("sp", ht[:], hT_v[:, q * 4:(q + 1) * 4, tb * 512:(tb + 1) * 512], R=hT_r[tb][q * 4:(q + 1) * 4], W=[ht.r])
            kb.copy("act", xn[:, q * 4:(q + 1) * 4, :], ht[:], R=[ht.r], PW=[xn.r])
            tiles.append((ht, 4))
        stats_rows(tiles, 16, rstd, 1.0 / D, 1e-6, sqring)

    def resid_evac(tb, hcr, hnr):
        def ev(f, b):
            hc = kb.ring("hc", hcr)
            hn = kb.ring("hn", hnr)
            kb.dma("sp", hc[:], hT_d[f * 128:(f + 1) * 128, tb * 512:(tb + 1) * 512], R=[hT_r[tb][f]], W=[hc.r])
            kb.tt("dve", hn[:], ps[b][:], hc[:], ALU.add, R=[psr[b], hc.r], W=[hn.r])
            kb.dma("pool", hT_d[f * 128:(f + 1) * 128, tb * 512:(tb + 1) * 512], hn[:], R=[hn.r], W=[hT_r[tb][f]])
        return ev

    def outproj_phase(wsrc):
        kb.phase(base_mark)
        xm = [kb.alloc([128, 16, 512], BF16, "xm") for _ in range(2)]
        hcr = [kb.alloc([128, 512], F32, "hc") for _ in range(3)]
        hnr = [kb.alloc([128, 512], F32, "hn") for _ in range(3)]
        for tb in range(NTB):
            x_ = xm[tb % 2]
            kb.dma("sp", x_[:], mixT_v[:, :, tb * 512:(tb + 1) * 512], R=mix_r, W=[x_.r])
            linear(x_, 16, wb_out, wr["out"], range(16), resid_evac(tb, hcr, hnr))

    def ffn_phase(l, bg_jobs):
        kb.phase(base_mark)
        bgit = [cast_steps(bg_jobs, alloc_slots()) if bg_jobs else None]
        xn = kb.alloc([128, 16, 512], BF16, "xn")
        actT = kb.alloc([128, FFC, 512], BF16, "actT")
        rstd = kb.alloc([128, 512], F32, "rstd")
        hring = [kb.alloc([128, 4, 512], F32, "hr") for _ in range(4)]
        sqring = [kb.alloc([128, 4, 512], BF16, "sq") for _ in range(2)]
        t1r = [kb.alloc([128, 512], F32, "t1") for _ in range(2)]
        t2r = [kb.alloc([128, 512], F32, "t2") for _ in range(2)]
        t3r = [kb.alloc([128, 512], F32, "t3") for _ in range(2)]
        hcr = [kb.alloc([128, 512], F32, "hc") for _ in range(3)]
        hnr = [kb.alloc([128, 512], F32, "hn") for _ in range(3)]
        for tb in range(NTB):
            norm_block(tb, xn, rstd, hring, sqring)
            for f in range(FFC):
                bg = [None]
                linear(xn, 16, wb_g, wr["g"], [f], lambda f_, b_: bg.__setitem__(0, b_))
                bu = [None]
                linear(xn, 16, wb_u, wr["u"], [f], lambda f_, b_: bu.__setitem__(0, b_))
                t1 = kb.ring("t1", t1r)
                t2 = kb.ring("t2", t2r)
                t3 = kb.ring("t3", t3r)
                kb.tt("dve", t1[:], ps[bg[0]][:], rstd[:], ALU.mult, R=[psr[bg[0]], rstd.r], W=[t1.r])
                kb.act(t2[:], t1[:], AF.Silu, R=[t1.r], W=[t2.r])
                kb.tt("dve", t3[:], ps[bu[0]][:], rstd[:], ALU.mult, R=[psr[bu[0]], rstd.r], W=[t3.r])
                kb.tt("pool", actT[:, f, :], t2[:], t3[:], ALU.mult, R=[t2.r, t3.r], PW=[actT.r])
                bg_step(bgit)
            linear(actT, FFC, wb_d, wr["d"], range(16), resid_evac(tb, hcr, hnr))
        bg_drain(bgit)

    def inproj_phase(wsrc, FC, gain, precast, post_alloc=None):
        kb.phase(base_mark)
        if not precast:
            cast_weight(wsrc, D, FC * 128, wb_in, wr["in"], gain=gain)
            S.barrier()
        xn = kb.alloc([128, 16, 512], BF16, "xn")
        rstd = kb.alloc([128, 512], F32, "rstd")
        hring = [kb.alloc([128, 4, 512], F32, "hr") for _ in range(4)]
        sqring = [kb.alloc([128, 4, 512], BF16, "sq") for _ in range(2)]
        evr = [kb.alloc([128, 512], F32, "ev") for _ in range(3)]
        post_tb = post_alloc() if post_alloc is not None else None
        for tb in range(NTB):
            norm_block(tb, xn, rstd, hring, sqring)

            def ev(f, b, tb=tb):
                e_ = kb.ring("ev", evr)
                kb.tt("dve", e_[:], ps[b][:], rstd[:], ALU.mult, R=[psr[b], rstd.r], W=[e_.r])
                kb.dma("pool", pT_d[f * 128:(f + 1) * 128, tb * 512:(tb + 1) * 512], e_[:], R=[e_.r], PW=[pT_r[f]])
            linear(xn, 16, wb_in, wr["in"], range(FC), ev)
            if post_tb is not None:
                post_tb(tb)

    def load_phase():
        xt = [kb.alloc([128, D], F32, "xt") for _ in range(2)]
        hs = [kb.alloc([128, 16, 128], F32, "hs") for _ in range(2)]
        for t in range(NT // 128):
            a = xt[t % 2]
            h_ = hs[t % 2]
            kb.dma("sp", a[:], x_d[t * 128:(t + 1) * 128, :], W=[a.r])
            for g in range(4):
                b = kb.bank()
                for j in range(4):
                    c = g * 4 + j
                    kb.tr(ps[b][:, j * 128:(j + 1) * 128], a[:, c * 128:(c + 1) * 128], ident[:],
                          R=[a.r, ident.r], W=[psr[b]] if j == 0 else (), PW=() if j == 0 else [psr[b]])
                kb.copy("dve" if g % 2 == 0 else "act", h_[:, g * 4:(g + 1) * 4, :],
                        ps[b][:].rearrange("p (a b) -> p a b", a=4), R=[psr[b]], PW=[h_.r])
            kb.dma("pool", hT_v[:, :, t * 128:(t + 1) * 128], h_[:], R=[h_.r], PW=hT_r[t // 4])

    def rope_phase():
        kb.phase(base_mark)
        pi_ = kb.alloc([64, NT], I32, "posi")
        pf = kb.alloc([64, NT], F32, "posf")
        t1 = kb.alloc([64, NT], F32, "rt1")
        t2 = kb.alloc([64, NT], F32, "rt2")
        kb.dma("sp", pi_[:], pos_d.partition_broadcast(64), W=[pi_.r])
        kb.copy("dve", pf[:], pi_[:], R=[pi_.r], W=[pf.r])
        kb.ts("dve", pf[:], pf[:], col("invf", 0, 64), None, ALU.mult, R=[pf.r, cols.r], W=[pf.r])
        C1 = 6.28125
        C2 = 2.0 * math.pi - 6.28125
        ki = kb.alloc([64, NT], I32, "ki")
        for which, shift in ((0, math.pi / 2), (1, 0.0)):
            kb.ts("dve", t1[:], pf[:], shift, None, ALU.add, R=[pf.r], W=[t1.r])
            kb.ts("dve", t2[:], t1[:], 1.0 / (2.0 * math.pi), None, ALU.mult, R=[t1.r], W=[t2.r])
            kb.copy("dve", ki[:], t2[:], R=[t2.r], W=[ki.r])
            kb.copy("dve", t2[:], ki[:], R=[ki.r], W=[t2.r])
            kb.stt(t1[:], t2[:], -C1, t1[:], ALU.mult, ALU.add, R=[t1.r, t2.r], W=[t1.r])
            kb.stt(t1[:], t2[:], -C2, t1[:], ALU.mult, ALU.add, R=[t1.r, t2.r], W=[t1.r])
            kb.ts("dve", t1[:], t1[:], -3.141592, 3.141592, ALU.max, ALU.min, R=[t1.r], W=[t1.r])
            kb.act(t2[:], t1[:], AF.Sin, R=[t1.r], W=[t2.r])
            if which == 1:
                kb.ts("dve", t2[:], t2[:], col("sinsign", 0, 64), None, ALU.mult, R=[t2.r, cols.r], W=[t2.r])
            kb.dma("pool", cs_d[which], t2[:], R=[t2.r], PW=[cs_r])

    def conv_factory(j):
        NR = 2
        bt = [kb.alloc([128, 512], F32, "cb") for _ in range(NR)]
        ct = [kb.alloc([128, 514], F32, "cc") for _ in range(NR)]
        htl = [kb.alloc([128, 514], F32, "ch") for _ in range(NR)]
        ut = [kb.alloc([128, 514], F32, "cu") for _ in range(NR)]
        y1 = [kb.alloc([128, 512], F32, "cy") for _ in range(NR)]
        yo = [kb.alloc([128, 512], BF16, "co") for _ in range(NR)]
        cnt = [0]

        def conv_block(tb):
            for c in range(8):
                i = cnt[0]
                cnt[0] += 1
                b_, c_, h_, u_, y_, o_ = bt[i % NR], ct[i % NR], htl[i % NR], ut[i % NR], y1[i % NR], yo[i % NR]
                t0 = tb * 512
                kb.dma("sp", b_[:], pT_d[c * 128:(c + 1) * 128, t0:t0 + 512], R=[pT_r[c]], W=[b_.r])
                if tb == 0:
                    kb.memset("pool", c_[:, 0:2], 0.0, W=[c_.r])
                    kb.memset("pool", h_[:, 0:2], 0.0, W=[h_.r])
                    kb.dma("sp", c_[:, 2:514], pT_d[(8 + c) * 128:(9 + c) * 128, 0:512], R=[pT_r[8 + c]], PW=[c_.r])
                    kb.dma("sp", h_[:, 2:514], pT_d[(16 + c) * 128:(17 + c) * 128, 0:512], R=[pT_r[16 + c]], PW=[h_.r])
                else:
                    kb.dma("sp", c_[:], pT_d[(8 + c) * 128:(9 + c) * 128, t0 - 2:t0 + 512], R=[pT_r[8 + c]], W=[c_.r])
                    kb.dma("sp", h_[:], pT_d[(16 + c) * 128:(17 + c) * 128, t0 - 2:t0 + 512], R=[pT_r[16 + c]], W=[h_.r])
                kb.tt("pool", u_[:], c_[:], h_[:], ALU.mult, R=[c_.r, h_.r], W=[u_.r])
                kb.ts("dve", y_[:], u_[:, 2:514], col(f"conv{j}", c * 3 + 2), None, ALU.mult, R=[u_.r, cols.r], W=[y_.r])
                kb.stt(y_[:], u_[:, 1:513], col(f"conv{j}", c * 3 + 1), y_[:], ALU.mult, ALU.add, R=[u_.r, y_.r, cols.r], W=[y_.r])
                kb.stt(y_[:], u_[:, 0:512], col(f"conv{j}", c * 3 + 0), y_[:], ALU.mult, ALU.add, R=[u_.r, y_.r, cols.r], W=[y_.r])
                kb.tt("pool", o_[:], b_[:], y_[:], ALU.mult, R=[b_.r, y_.r], W=[o_.r])
                kb.dma("pool", mixT_d[c * 128:(c + 1) * 128, t0:t0 + 512], o_[:], R=[o_.r], PW=[mix_r[c]])
        return conv_block

    def rwkv_phase(j, bg_jobs):
        kb.phase(base_mark)
        bg = [cast_steps(bg_jobs, wring_slots())]
        A = lambda shape, dt, n: kb.alloc(shape, dt, n)
        wtmp = A([128, 1024], F32, "wtmp")
        w2b = A([64, 1024], BF16, "w2b")
        a2b = A([64, 1024], BF16, "a2b")
        g2b0 = A([128, 1024], BF16, "g2b0")
        g2b1 = A([32, 1024], BF16, "g2b1")
        for (src, r0, n, dst) in ((w2_d[j], 0, 64, w2b), (a2_d[j], 0, 64, a2b), (g2_d[j], 0, 128, g2b0), (g2_d[j], 128, 32, g2b1)):
            kb.dma("sp", wtmp[0:n, :], src[r0:r0 + n, :], W=[wtmp.r])
            kb.copy("act", dst[0:n, :], wtmp[0:n, :], R=[wtmp.r], W=[dst.r])
        ST = [A([64, 64], F32, "ST") for _ in range(NH_R)]
        STb = [A([64, 64], BF16, "STb") for _ in range(NH_R)]
        for s_ in ST + STb:
            kb.memset("pool", s_[:], 0.0, W=[s_.r])
        id64b = id64b_t[:]
        xwr = A([64, 513], F32, "xwr"); xar = A([64, 513], F32, "xar")
        xg0r = A([128, 513], F32, "xg0r"); xg1r = A([32, 513], F32, "xg1r")
        sh_t = A([128, 512], F32, "sht")
        txw = A([64, 512], BF16, "txw"); xab = A([64, 512], BF16, "xab")
        sxg0 = A([128, 512], BF16, "sxg0"); sxg1 = A([32, 512], BF16, "sxg1")
        def make_set():
            raw_ = {n: A([64, 513], F32, n) for n in ["rr", "kr", "vr"]}
            f_ = {}
            for n in ["rm", "km", "vm", "d", "lw", "a", "g", "kk", "sq", "kappa", "kp", "b", "cum", "Ep", "Em", "Epv", "t"]:
                f_[n] = A([64, 512], F32, n)
            for n in ["rt", "at", "kh", "bh", "vmb", "Atok", "Vtok", "Khtok", "Bhtok", "P0", "PT0", "P1_", "PT1", "XT0", "XT1",
                      "AKT", "RKT", "RBT", "Pone", "U0", "ApT", "U"]:
                f_[n] = A([64, 512], BF16, n)
            for new, old in (("C0", "kk"), ("YT", "cum"), ("yc", "Em"), ("rk", "Epv"), ("rs", "d")):
                f_[new] = f_[old]
            ob_ = A([64, 512], BF16, "ob")
            return raw_, f_, ob_
        sets = [make_set(), make_set()]
        M_UP, M_LO, M_UPI, M_I8, M_RST = (masks[:, i, :] for i in range(5))
        id64 = ident[0:64, 0:64]
        NEG_E = -math.exp(-0.5)

        def shift_mix(dst, rawt, mucol, parts, d):
            kb.tt("pool", d[0:parts, :], rawt[0:parts, 0:512], rawt[0:parts, 1:513], ALU.subtract, R=[rawt.r], W=[d.r])
            kb.stt(dst[0:parts, :], d[0:parts, :], mucol, rawt[0:parts, 1:513], ALU.mult, ALU.add, R=[d.r, rawt.r, cols.r], W=[dst.r])

        def load_halo(t_, row0, parts, tg, res):
            t0 = tg * 512
            if tg == 0:
                kb.memset("pool", t_[0:parts, 0:1], 0.0, W=[t_.r])
                kb.dma("sp", t_[0:parts, 1:513], pT_d[row0:row0 + parts, 0:512], R=[res], PW=[t_.r])
            else:
                kb.dma("sp", t_[0:parts, :], pT_d[row0:row0 + parts, t0 - 1:t0 + 512], R=[res], W=[t_.r])

        def per_chunk_mm(outb, lhs, rhs, Rl):
            for c in range(8):
                cs = slice(c * 64, (c + 1) * 64)
                kb.mm(ps[outb][0:64, cs], lhs[:, cs], rhs[:, cs], R=Rl,
                      W=[psr[outb]] if c == 0 else (), PW=() if c == 0 else [psr[outb]])

        for tg in range(NTB):
            load_halo(xwr, 48 * 128, 64, tg, pT_r[48])
            load_halo(xar, 49 * 128, 64, tg, pT_r[49])
            load_halo(xg0r, 50 * 128, 128, tg, pT_r[50])
            load_halo(xg1r, 51 * 128, 32, tg, pT_r[51])
            f = sets[0][1]
            shift_mix(f["t"], xwr, col(f"mu_xw{j}", 0, 64), 64, f["d"])
            kb.act(txw[:], f["t"][:], AF.Tanh, R=[f["t"].r], W=[txw.r])
            shift_mix(f["t"], xar, col(f"mu_xa{j}", 0, 64), 64, f["d"])
            kb.copy("act", xab[:], f["t"][:], R=[f["t"].r], W=[xab.r])
            tmp128 = sh_t
            kb.tt("pool", wtmp[:, 0:512], xg0r[:, 0:512], xg0r[:, 1:513], ALU.subtract, R=[xg0r.r], W=[wtmp.r])
            kb.stt(tmp128[:, :], wtmp[:, 0:512], col(f"mu_xg0{j}"), xg0r[:, 1:513], ALU.mult, ALU.add, R=[wtmp.r, xg0r.r, cols.r], W=[tmp128.r])
            kb.act(sxg0[:], tmp128[:], AF.Sigmoid, R=[tmp128.r], W=[sxg0.r])
            kb.tt("pool", wtmp[0:32, 0:512], xg1r[0:32, 0:512], xg1r[0:32, 1:513], ALU.subtract, R=[xg1r.r], W=[wtmp.r])
            kb.stt(tmp128[0:32, :], wtmp[0:32, 0:512], col(f"mu_xg1{j}", 0, 32), xg1r[0:32, 1:513], ALU.mult, ALU.add, R=[wtmp.r, xg1r.r, cols.r], W=[tmp128.r])
            kb.act(sxg1[:], tmp128[0:32, :], AF.Sigmoid, R=[tmp128.r], W=[sxg1.r])

            def head_gen(hd, raw, f, ob):
                c0 = hd * 64
                hc = lambda nm: col(f"{nm}{j}", hd, 64)
                for (nm, base) in (("rr", 24 * 128), ("kr", 32 * 128), ("vr", 40 * 128)):
                    load_halo(raw[nm], base + c0, 64, tg, pT_r[(base + c0) // 128])
                shift_mix(f["rm"], raw["rr"], hc("mu_r"), 64, f["d"])
                shift_mix(f["km"], raw["kr"], hc("mu_k"), 64, f["d"])
                shift_mix(f["vm"], raw["vr"], hc("mu_v"), 64, f["d"])
                kb.copy("act", f["vmb"][:], f["vm"][:], R=[f["vm"].r], W=[f["vmb"].r])
                yield
                b = kb.bank()
                kb.mm(ps[b][0:64, :], w2b[:, c0:c0 + 64], txw[:], R=[w2b.r, txw.r], W=[psr[b]])
                kb.act(f["lw"][:], ps[b][0:64, :], AF.Sigmoid, R=[psr[b], cols.r], W=[f["lw"].r], bias=hc("w0"))
                kb.ts("pool", f["lw"][:], f["lw"][:], NEG_E, None, ALU.mult, R=[f["lw"].r], W=[f["lw"].r])
                b = kb.bank()
                kb.mm(ps[b][0:64, :], a2b[:, c0:c0 + 64], xab[:], R=[a2b.r, xab.r], W=[psr[b]])
                kb.act(f["a"][:], ps[b][0:64, :], AF.Sigmoid, R=[psr[b], cols.r], W=[f["a"].r], bias=hc("a0"))
                b = kb.bank()
                kb.mm(ps[b][0:64, :], g2b0[:, c0:c0 + 64], sxg0[:], start=True, stop=False, R=[g2b0.r, sxg0.r], W=[psr[b]])
                kb.mm(ps[b][0:64, :], g2b1[:, c0:c0 + 64], sxg1[:], start=False, stop=True, R=[g2b1.r, sxg1.r], PW=[psr[b]])
                kb.copy("act", f["g"][:], ps[b][0:64, :], R=[psr[b]], W=[f["g"].r])
                yield
                kb.ts("pool", f["kk"][:], f["km"][:], hc("k_k"), None, ALU.mult, R=[f["km"].r, cols.r], W=[f["kk"].r])
                kb.tt("pool", f["sq"][:], f["kk"][:], f["kk"][:], ALU.mult, R=[f["kk"].r], W=[f["sq"].r])
                b = kb.bank()
                kb.mm(ps[b][0:64, :], ones64[:], f["sq"][:], R=[ones64.r, f["sq"].r], W=[psr[b]])
                kb.act(f["rs"][:], ps[b][0:64, :], AF.Sqrt, R=[psr[b], cols.r], W=[f["rs"].r], bias=col("tiny", 0, 64))
                S.add("dve", lambda e: e.reciprocal(out=f["rs"][:], in_=f["rs"][:]), [f["rs"].r], [f["rs"].r])
                kb.tt("pool", f["kappa"][:], f["kk"][:], f["rs"][:], ALU.mult, R=[f["kk"].r, f["rs"].r], W=[f["kappa"].r])
                kb.ts("dve", f["t"][:], f["a"][:], -1.0, hc("k_a"), ALU.add, ALU.mult, R=[f["a"].r, cols.r], W=[f["t"].r])
                kb.stt(f["kp"][:], f["t"][:], 1.0, f["km"][:], ALU.add, ALU.mult, R=[f["t"].r, f["km"].r], W=[f["kp"].r])
                kb.tt("pool", f["b"][:], f["kappa"][:], f["a"][:], ALU.mult, R=[f["kappa"].r, f["a"].r], W=[f["b"].r])
                yield
                S.add("dve", lambda e: e.tensor_tensor_scan(out=f["cum"][:], data0=M_RST, data1=f["lw"][:], initial=0.0,
                                                            op0=ALU.mult, op1=ALU.add), [masks.r, f["lw"].r], [f["cum"].r])
                kb.act(f["Ep"][:], f["cum"][:], AF.Exp, R=[f["cum"].r], W=[f["Ep"].r])
                kb.act(f["Em"][:], f["cum"][:], AF.Exp, R=[f["cum"].r], W=[f["Em"].r], scale=-1.0)
                kb.tt("pool", f["t"][:], f["cum"][:], f["lw"][:], ALU.subtract, R=[f["cum"].r, f["lw"].r], W=[f["t"].r])
                kb.act(f["Epv"][:], f["t"][:], AF.Exp, R=[f["t"].r], W=[f["Epv"].r])
                kb.tt("pool", f["rt"][:], f["rm"][:], f["Ep"][:], ALU.mult, R=[f["rm"].r, f["Ep"].r], W=[f["rt"].r])
                kb.stt(f["at"][:], f["kappa"][:], -1.0, f["Epv"][:], ALU.mult, ALU.mult, R=[f["kappa"].r, f["Epv"].r], W=[f["at"].r])
                kb.tt("pool", f["kh"][:], f["kp"][:], f["Em"][:], ALU.mult, R=[f["kp"].r, f["Em"].r], W=[f["kh"].r])
                kb.tt("dve", f["bh"][:], f["b"][:], f["Em"][:], ALU.mult, R=[f["b"].r, f["Em"].r], W=[f["bh"].r])
                yield
                for (src, dst, eng) in (("at", "Atok", "dve"), ("vmb", "Vtok", "act"), ("kh", "Khtok", "dve"), ("bh", "Bhtok", "act")):
                    b = kb.bank()
                    psb = ps[b][0:64, :].bitcast(BF16)
                    for c in range(8):
                        cs = slice(c * 64, (c + 1) * 64)
                        kb.tr(psb[:, cs], f[src][:, cs], id64b, R=[f[src].r, id64b_t.r],
                              W=[psr[b]] if c == 0 else (), PW=() if c == 0 else [psr[b]])
                    kb.copy(eng, f[dst][:], psb[:, 0:512], R=[psr[b]], W=[f[dst].r])
                    yield
                for (lhs, rhs, dst, mk, eng) in (("bh", "at", "PT0", M_UP, "dve"), ("at", "bh", "P0", M_LO, "dve"),
                                                 ("kh", "at", "AKT", M_UP, "dve"), ("kh", "rt", "RKT", M_UPI, "dve"),
                                                 ("bh", "rt", "RBT", M_UPI, "dve")):
                    b = kb.bank()
                    per_chunk_mm(b, f[lhs], f[rhs], [f[lhs].r, f[rhs].r])
                    kb.tt(eng, f[dst][:], ps[b][0:64, :], mk, ALU.mult, R=[psr[b], masks.r], W=[f[dst].r])
                    yield
                b = kb.bank()
                per_chunk_mm(b, f["AKT"], f["Vtok"], [f["AKT"].r, f["Vtok"].r])
                kb.copy("act", f["Pone"][:], ps[b][0:64, :], R=[psr[b]], W=[f["Pone"].r])
                yield
                kb.tt("pool", f["XT0"][:], f["PT0"][:], M_I8, ALU.add, R=[f["PT0"].r, masks.r], W=[f["XT0"].r])
                Pc, PTc, Xc = "P0", "PT0", "XT0"
                for lvl in range(1, 6):
                    Pn = "P1_" if Pc == "P0" else "P0"
                    PTn = "PT1" if PTc == "PT0" else "PT0"
                    Xn = "XT1" if Xc == "XT0" else "XT0"
                    b = kb.bank()
                    per_chunk_mm(b, f[PTc], f[Pc], [f[PTc].r, f[Pc].r])
                    b2 = None
                    if lvl < 5:
                        b2 = kb.bank()
                        per_chunk_mm(b2, f[Pc], f[PTc], [f[PTc].r, f[Pc].r])
                    kb.copy("act", f[Pn][:], ps[b][0:64, :], R=[psr[b]], W=[f[Pn].r])
                    if b2 is not None:
                        kb.copy("dve", f[PTn][:], ps[b2][0:64, :], R=[psr[b2]], W=[f[PTn].r])
                    yield
                    b3 = kb.bank()
                    per_chunk_mm(b3, f[Pn], f[Xc], [f[Pn].r, f[Xc].r])
                    kb.tt("dve", f[Xn][:], ps[b3][0:64, :], f[Xc][:], ALU.add, R=[psr[b3], f[Xc].r], W=[f[Xn].r])
                    Pc, PTc, Xc = Pn, PTn, Xn
                    yield
                XT = f[Xc]
                b = kb.bank()
                per_chunk_mm(b, XT, f["Pone"], [XT.r, f["Pone"].r])
                kb.copy("act", f["U0"][:], ps[b][0:64, :], R=[psr[b]], W=[f["U0"].r])
                yield
                b = kb.bank()
                per_chunk_mm(b, f["Atok"], XT, [XT.r, f["Atok"].r])
                kb.copy("dve", f["ApT"][:], ps[b][0:64, :], R=[psr[b]], W=[f["ApT"].r])
                yield
                b = kb.bank()
                per_chunk_mm(b, f["Khtok"], f["Vtok"], [f["Khtok"].r, f["Vtok"].r])
                kb.copy("act", f["C0"][:], ps[b][0:64, :], R=[psr[b]], W=[f["C0"].r])
                yield
                bU = kb.bank(hold=True); bY = kb.bank(hold=True); bS = kb.bank(hold=True)
                st = ST[hd]
                stb = STb[hd]
                for c in range(8):
                    cs = slice(c * 64, (c + 1) * 64)
                    kb.mm(ps[bU][0:64, cs], f["ApT"][:, cs], stb[:], start=True, stop=False, R=[f["ApT"].r, stb.r],
                          W=[psr[bU]] if c == 0 else (), PW=() if c == 0 else [psr[bU]])
                    kb.mm(ps[bU][0:64, cs], id64b, f["U0"][:, cs], start=False, stop=True, R=[id64b_t.r, f["U0"].r], PW=[psr[bU]])
                    kb.copy("dve", f["U"][:, cs], ps[bU][0:64, cs], R=[psr[bU]], PW=[f["U"].r])
                    yield
                    kb.mm(ps[bY][0:64, cs], f["Vtok"][:, cs], f["RKT"][:, cs], start=True, stop=False, R=[f["Vtok"].r, f["RKT"].r],
                          W=[psr[bY]] if c == 0 else (), PW=() if c == 0 else [psr[bY]])
                    kb.mm(ps[bY][0:64, cs], f["U"][:, cs], f["RBT"][:, cs], start=False, stop=False, R=[f["U"].r, f["RBT"].r], PW=[psr[bY]])
                    kb.mm(ps[bY][0:64, cs], stb[:], f["rt"][:, cs], start=False, stop=True, R=[stb.r, f["rt"].r], PW=[psr[bY]])
                    kb.mm(ps[bS][0:64, 0:64] if False else ps[bS][0:64, cs], f["Bhtok"][:, cs], f["U"][:, cs], start=True, stop=False,
                          R=[f["Bhtok"].r, f["U"].r], W=[psr[bS]] if c == 0 else (), PW=() if c == 0 else [psr[bS]])
                    kb.mm(ps[bS][0:64, cs], id64, st[:], start=False, stop=False, R=[ident.r, st.r], PW=[psr[bS]])
                    kb.mm(ps[bS][0:64, cs], id64, f["C0"][:, cs], start=False, stop=True, R=[ident.r, f["C0"].r], PW=[psr[bS]])
                    kb.ts("dve", stb[:], ps[bS][0:64, cs], f["Ep"][:, c * 64 + 63:c * 64 + 64], None, ALU.mult,
                          R=[psr[bS], f["Ep"].r], W=[stb.r])
                    kb.ts("dve", st[:], ps[bS][0:64, cs], f["Ep"][:, c * 64 + 63:c * 64 + 64], None, ALU.mult,
                          R=[psr[bS], f["Ep"].r], W=[st.r])
                    yield
                kb.copy("act", f["YT"][:], ps[bY][0:64, :], R=[psr[bY]], W=[f["YT"].r])
                kb.release(bU, bY, bS)
                yield
                b = kb.bank()
                kb.mm(ps[b][0:64, :], mean64[:], f["YT"][:], R=[mean64.r, f["YT"].r], W=[psr[b]])
                kb.stt(f["yc"][:], ps[b][0:64, :], -1.0, f["YT"][:], ALU.mult, ALU.add, R=[psr[b], f["YT"].r], W=[f["yc"].r])
                yield
                kb.tt("pool", f["sq"][:], f["yc"][:], f["yc"][:], ALU.mult, R=[f["yc"].r], W=[f["sq"].r])
                b = kb.bank()
                kb.mm(ps[b][0:64, :], mean64[:], f["sq"][:], R=[mean64.r, f["sq"].r], W=[psr[b]])
                kb.act(f["rs"][:], ps[b][0:64, :], AF.Sqrt, R=[psr[b], cols.r], W=[f["rs"].r], bias=col("gneps", 0, 64))
                S.add("dve", lambda e: e.reciprocal(out=f["rs"][:], in_=f["rs"][:]), [f["rs"].r], [f["rs"].r])
                yield
                kb.tt("pool", f["yc"][:], f["yc"][:], f["rs"][:], ALU.mult, R=[f["yc"].r, f["rs"].r], W=[f["yc"].r])
                kb.act(f["t"][:], f["yc"][:], AF.Identity, R=[f["yc"].r, cols.r], W=[f["t"].r], scale=hc("ln_w"), bias=hc("ln_b"))
                kb.stt(f["rk"][:], f["rm"][:], hc("r_k"), f["kp"][:], ALU.mult, ALU.mult, R=[f["rm"].r, f["kp"].r, cols.r], W=[f["rk"].r])
                b = kb.bank()
                kb.mm(ps[b][0:64, :], ones64[:], f["rk"][:], R=[ones64.r, f["rk"].r], W=[psr[b]])
                kb.tt("dve", f["rk"][:], ps[b][0:64, :], f["vm"][:], ALU.mult, R=[psr[b], f["vm"].r], W=[f["rk"].r])
                kb.tt("pool", f["t"][:], f["t"][:], f["rk"][:], ALU.add, R=[f["t"].r, f["rk"].r], W=[f["t"].r])
                kb.tt("pool", ob[:], f["t"][:], f["g"][:], ALU.mult, R=[f["t"].r, f["g"].r], W=[ob.r])
                kb.dma("pool", mixT_d[1024 + c0:1024 + c0 + 64, tg * 512:(tg + 1) * 512], ob[:], R=[ob.r], PW=[mix_r[8 + hd // 2]])
                if False:
                    for i_, nm_ in enumerate(["lw", "a", "g", "kappa", "kp", "cum", "rt", "at", "kh", "bh", "Atok", "Vtok", "AKT", "RKT", "RBT",
                                              "Pone", Xc, "U0", "ApT", "C0", "U", "YT", "yc", "t", "rm", "vm"]):
                        dbg_outs.append(kb.dma("pool", dbgr_d[i_], f[nm_][:], R=[f[nm_].r]))

            for hp in range(0, NH_R, 2):
                gens = [head_gen(hp + k_, *sets[k_]) for k_ in range(2)]
                while gens:
                    for g_ in list(gens):
                        try:
                            next(g_)
                        except StopIteration:
                            gens.remove(g_)
                    bg_step(bg)
        bg_drain(bg)

    def pool_phase(j):
        kb.phase(base_mark)
        pw = kb.alloc([128, 4, 128], F32, "pw")
        pfix = kb.alloc([128, 4, 16], F32, "pfix")
        kb.dma("sp", pw[:], poolw_d[j].rearrange("g i o -> i g o"), W=[pw.r])
        kb.dma("sp", pfix[:], poolfix_d, W=[pfix.r])
        NR = 2
        ur = [kb.alloc([128, 527], F32, "pu") for _ in range(NR)]
        sa = [kb.alloc([128, 527], F32, "psa") for _ in range(NR)]
        sb_ = [kb.alloc([128, 527], F32, "psb") for _ in range(NR)]
        dr = [kb.alloc([128, 512], F32, "pd") for _ in range(NR)]
        orr = [kb.alloc([128, 512], BF16, "po") for _ in range(NR)]
        i = 0
        for g in range(4):
            win = 2 << g
            for tb in range(NTB):
                u_, a_, b_, d_, o_ = ur[i % NR], sa[i % NR], sb_[i % NR], dr[i % NR], orr[i % NR]
                i += 1
                t0 = tb * 512
                if tb == 0:
                    kb.memset("pool", u_[:, 0:15], 0.0, W=[u_.r])
                    kb.dma("sp", u_[:, 15:527], pT_d[g * 128:(g + 1) * 128, 0:512], R=[pT_r[g]], PW=[u_.r])
                else:
                    kb.dma("sp", u_[:], pT_d[g * 128:(g + 1) * 128, t0 - 15:t0 + 512], R=[pT_r[g]], W=[u_.r])
                src = u_
                w = 1
                dsts = [a_, b_]
                k = 0
                while w < win:
                    dst = dsts[k % 2]
                    k += 1
                    kb.tt("pool" if k % 2 else "dve", dst[:, 2 * w - 1:527], src[:, 2 * w - 1:527], src[:, w - 1:527 - w], ALU.add,
                          R=[src.r], W=[dst.r])
                    src = dst
                    w *= 2
                kb.stt(d_[:], src[:, 15:527], 1.0 / win, u_[:, 15:527], ALU.mult, ALU.subtract, R=[src.r, u_.r], W=[d_.r])
                if tb == 0:
                    kb.tt("dve", d_[:, 0:16], src[:, 15:31], pfix[:, g, :], ALU.mult, R=[src.r, pfix.r, d_.r], W=[d_.r])
                    kb.tt("dve", d_[:, 0:16], d_[:, 0:16], u_[:, 15:31], ALU.subtract, R=[d_.r, u_.r], W=[d_.r])
                b = kb.bank()
                kb.mm(ps[b][:, :], pw[:, g, :], d_[:], R=[pw.r, d_.r], W=[psr[b]])
                kb.act(o_[:], ps[b][:], AF.Copy, R=[psr[b], cols.r], W=[o_.r], scale=col(f"pool_scale{j}", g))
                kb.dma("pool", mixT_d[g * 128:(g + 1) * 128, t0:t0 + 512], o_[:], R=[o_.r], PW=[mix_r[g]])

    def mla_phase(j, bg_jobs):
        kb.phase(base_mark)
        bg = [cast_steps(bg_jobs, wring_slots())]
        A = kb.alloc
        wv = A([128, 4, 1536], BF16, "wv")
        wtmp = A([128, 512], F32, "wvt")
        for c in range(4):
            for q3 in range(3):
                kb.dma("sp", wtmp[:], wuv_d[j][c * 128:(c + 1) * 128, q3 * 512:(q3 + 1) * 512], W=[wtmp.r])
                kb.act(wv[:, c, q3 * 512:(q3 + 1) * 512], wtmp[:], AF.Copy, R=[wtmp.r, cols.r], PW=[wv.r], scale=col(f"kv_norm{j}", c))
        cosr = [A([64, 512], F32, "cos2") for _ in range(2)]
        sinr = [A([64, 512], F32, "sin2") for _ in range(2)]

        def load_cs(tb):
            c_ = kb.ring("cos", cosr)
            s_ = kb.ring("sin", sinr)
            kb.dma("sp", c_[:], cs_d[0][:, tb * 512:(tb + 1) * 512], R=[cs_r], W=[c_.r])
            kb.dma("sp", s_[:], cs_d[1][:, tb * 512:(tb + 1) * 512], R=[cs_r], W=[s_.r])
            return c_, s_
        qn = A([128, 4, NT], BF16, "qn")
        kvn = A([128, 4, NT], BF16, "kvn")
        kpeT = A([64, NT], BF16, "kpeT")
        lat = [A([128, 4, 512], F32, "lat") for _ in range(1)]
        sqring = [A([128, 4, 512], BF16, "sq") for _ in range(2)]
        rstd = A([128, 512], F32, "rstd")
        t64a = A([64, 512], F32, "t64a")
        t64b = A([64, 512], F32, "t64b")
        for (c0, dst) in ((4, qn), (8, kvn)):
            for tb in range(NTB):
                lt = kb.ring("lat", lat)
                kb.dma("sp", lt[:], pT_d[c0 * 128:(c0 + 4) * 128, tb * 512:(tb + 1) * 512].rearrange("(c p) t -> p c t", p=128),
                       R=pT_r[c0:c0 + 4], W=[lt.r])
                stats_rows([(lt, 4)], 4, rstd, 1.0 / 512, 1e-6, sqring)
                for c in range(4):
                    kb.tt("dve", dst[:, c, tb * 512:(tb + 1) * 512], lt[:, c, :], rstd[:], ALU.mult, R=[lt.r, rstd.r], PW=[dst.r])
        kp1 = A([64, 512], F32, "kp1")
        kp2 = A([64, 512], F32, "kp2")
        for tb in range(NTB):
            ts_ = slice(tb * 512, (tb + 1) * 512)
            kb.dma("sp", kp1[:], pT_d[12 * 128:12 * 128 + 64, ts_], R=[pT_r[12]], W=[kp1.r])
            kb.dma("sp", kp2[:], pT_d[13 * 128:13 * 128 + 64, ts_], R=[pT_r[13]], W=[kp2.r])
            c_, s_ = load_cs(tb)
            kb.tt("pool", t64a[:], kp1[:], c_[:], ALU.mult, R=[kp1.r, c_.r], W=[t64a.r])
            kb.tt("pool", t64b[:], kp2[:], s_[:], ALU.mult, R=[kp2.r, s_.r], W=[t64b.r])
            kb.tt("pool", kpeT[:, ts_], t64a[:], t64b[:], ALU.add, R=[t64a.r, t64b.r], PW=[kpeT.r])
        Qn = A([128, NT], BF16, "Qn")
        Qp = A([64, NT], BF16, "Qp")
        Kn = A([128, NT], BF16, "Kn")
        V = A([128, NT // 128, 128], BF16, "V")
        Pt = [A([128, 512], BF16, "Pt") for _ in range(4)]
        rl = A([128, 512], F32, "rl")
        ob = [A([128, 512], BF16, "aob") for _ in range(2)]
        for h in range(NH_A):
            wq1 = kb.ring("w", wring)
            kb.dma("sp", wq1[:, 0:512], wb_uq[2 * h], R=[wr["uq"]], W=[wq1.r])
            wq2 = kb.ring("w", wring)
            kb.dma("sp", wq2[:, 0:512], wb_uq[2 * h + 1], R=[wr["uq"]], W=[wq2.r])
            wk = kb.ring("w", wring)
            kb.dma("sp", wk[:, 0:512], wb_uk[h], R=[wr["uk"]], W=[wk.r])
            for tb in range(NTB):
                ts_ = slice(tb * 512, (tb + 1) * 512)
                b = kb.bank()
                for c in range(4):
                    kb.mm(ps[b][:, :], wq1[:, c * 128:(c + 1) * 128], qn[:, c, ts_], start=(c == 0), stop=(c == 3),
                          R=[wq1.r, qn.r], W=[psr[b]] if c == 0 else (), PW=() if c == 0 else [psr[b]])
                kb.act(Qn[:, ts_], ps[b][:], AF.Copy, R=[psr[b]], PW=[Qn.r], scale=SM_SCALE)
                b1 = kb.bank()
                b2 = kb.bank()
                for c in range(4):
                    kb.mm(ps[b1][0:64, :], wq2[:, c * 128:c * 128 + 64], qn[:, c, ts_], start=(c == 0), stop=(c == 3),
                          R=[wq2.r, qn.r], W=[psr[b1]] if c == 0 else (), PW=() if c == 0 else [psr[b1]])
                for c in range(4):
                    kb.mm(ps[b2][0:64, :], wq2[:, c * 128 + 64:c * 128 + 128], qn[:, c, ts_], start=(c == 0), stop=(c == 3),
                          R=[wq2.r, qn.r], W=[psr[b2]] if c == 0 else (), PW=() if c == 0 else [psr[b2]])
                c_, s_ = load_cs(tb)
                kb.tt("dve", t64a[:], ps[b1][0:64, :], c_[:], ALU.mult, R=[psr[b1], c_.r], W=[t64a.r])
                kb.tt("dve", t64b[:], ps[b2][0:64, :], s_[:], ALU.mult, R=[psr[b2], s_.r], W=[t64b.r])
                kb.tt("dve", t64a[:], t64a[:], t64b[:], ALU.add, R=[t64a.r, t64b.r], W=[t64a.r])
                kb.ts("pool", Qp[:, ts_], t64a[:], SM_SCALE, None, ALU.mult, R=[t64a.r], PW=[Qp.r])
                b = kb.bank()
                for c in range(4):
                    kb.mm(ps[b][:, :], wk[:, c * 128:(c + 1) * 128], kvn[:, c, ts_], start=(c == 0), stop=(c == 3),
                          R=[wk.r, kvn.r], W=[psr[b]] if c == 0 else (), PW=() if c == 0 else [psr[b]])
                kb.copy("act", Kn[:, ts_], ps[b][:], R=[psr[b]], PW=[Kn.r])
                b = kb.bank()
                for i4 in range(4):
                    it = tb * 4 + i4
                    for c in range(4):
                        kb.mm(ps[b][:, i4 * 128:(i4 + 1) * 128], kvn[:, c, it * 128:(it + 1) * 128], wv[:, c, h * 128:(h + 1) * 128],
                              start=(c == 0), stop=(c == 3), R=[kvn.r, wv.r],
                              W=[psr[b]] if (c == 0 and i4 == 0) else (), PW=() if (c == 0 and i4 == 0) else [psr[b]])
                kb.copy("dve", V[:, tb * 4:(tb + 1) * 4, :], ps[b][:].rearrange("p (a b) -> p a b", a=4), R=[psr[b]], PW=[V.r])
            for g in range(NTB):
                bO = kb.bank()
                bL = kb.bank()
                nj = 4 * g + 4
                pend_pv = [None]
                for jt in range(nj):
                    q0 = max(0, jt - 4 * g) * 128
                    bS = kb.bank(avoid=(bO, bL))
                    qs = slice(g * 512 + q0, (g + 1) * 512)
                    kb.mm(ps[bS][:, q0:512], Kn[:, jt * 128:(jt + 1) * 128], Qn[:, qs], start=True, stop=False,
                          R=[Kn.r, Qn.r], W=[psr[bS]])
                    kb.mm(ps[bS][:, q0:512], kpeT[:, jt * 128:(jt + 1) * 128], Qp[:, qs], start=False, stop=True,
                          R=[kpeT.r, Qp.r], PW=[psr[bS]])
                    p_ = kb.ring("Pt", Pt)
                    kb.act(p_[:, q0:512], ps[bS][:, q0:512], AF.Exp, R=[psr[bS]], W=[p_.r])
                    if jt >= 4 * g:
                        kb.tt("pool", p_[:, q0:q0 + 128], p_[:, q0:q0 + 128], tri[:], ALU.mult, R=[p_.r, tri.r], W=[p_.r])

                    def pv(jt=jt, q0=q0, p_=p_):
                        kb.mm(ps[bO][:, q0:512], V[:, jt, :], p_[:, q0:512], start=(jt == 0), stop=(jt == nj - 1),
                              R=[V.r, p_.r], W=[psr[bO]] if jt == 0 else (), PW=() if jt == 0 else [psr[bO]])
                        kb.mm(ps[bL][:, q0:512], ones_bf[:], p_[:, q0:512], start=(jt == 0), stop=(jt == nj - 1),
                              R=[ones_bf.r, p_.r], W=[psr[bL]] if jt == 0 else (), PW=() if jt == 0 else [psr[bL]])
                    if pend_pv[0] is not None:
                        pend_pv[0]()
                    pend_pv[0] = pv
                    bg_step(bg)
                pend_pv[0]()
                pend_pv[0] = None
                S.add("dve", lambda e, bL=bL: e.reciprocal(out=rl[:], in_=ps[bL][:]), [psr[bL]], [rl.r])
                o_ = kb.ring("aob", ob)
                kb.tt("dve", o_[:], ps[bO][:], rl[:], ALU.mult, R=[psr[bO], rl.r], W=[o_.r])
                kb.dma("pool", mixT_d[512 + h * 128:512 + (h + 1) * 128, g * 512:(g + 1) * 512], o_[:], R=[o_.r], PW=[mix_r[4 + h]])
        bg_drain(bg)

    def final_phase():
        kb.phase(base_mark)
        hring = [kb.alloc([128, 4, 512], F32, "hr") for _ in range(4)]
        sqring = [kb.alloc([128, 4, 512], BF16, "sq") for _ in range(2)]
        rstd = kb.alloc([128, 512], F32, "rstd")
        yn = [kb.alloc([128, 4, 512], F32, "yn") for _ in range(2)]
        ot = [kb.alloc([128, D], F32, "ot") for _ in range(4)]
        outs = []
        for tb in range(NTB):
            tiles = []
            for q in range(4):
                ht = kb.ring("h", hring)
                kb.dma("sp", ht[:], hT_v[:, q * 4:(q + 1) * 4, tb * 512:(tb + 1) * 512], R=hT_r[tb][q * 4:(q + 1) * 4], W=[ht.r])
                tiles.append((ht, 4))
            stats_rows(tiles, 16, rstd, 1.0 / D, 1e-6, sqring)
            for q in range(4):
                ht = tiles[q][0]
                y_ = kb.ring("yn", yn)
                for c4 in range(4):
                    kb.stt(y_[:, c4, :], ht[:, c4, :], col("final_norm", q * 4 + c4), rstd[:], ALU.mult, ALU.mult,
                           R=[ht.r, rstd.r, cols.r], PW=[y_.r])
                for i4 in range(4):
                    o_ = ot[i4]
                    b = kb.bank()
                    for c4 in range(4):
                        kb.tr(ps[b][:, c4 * 128:(c4 + 1) * 128], y_[:, c4, i4 * 128:(i4 + 1) * 128], ident[:], R=[y_.r, ident.r],
                              W=[psr[b]] if c4 == 0 else (), PW=() if c4 == 0 else [psr[b]])
                    kb.copy("act" if i4 % 2 else "dve", o_[:, q * 512:(q + 1) * 512], ps[b][:], R=[psr[b]],
                            W=[o_.r] if q == 0 else (), PW=() if q == 0 else [o_.r])
            for i4 in range(4):
                t0 = tb * 512 + i4 * 128
                outs.append(kb.dma("pool", out_d[t0:t0 + 128, :], ot[i4][:], R=[ot[i4].r]))
        return outs

    def debug_dump():
        kb.phase(base_mark)
        t_ = [kb.alloc([128, 512], F32, "dbg") for _ in range(2)]
        outs = []
        for tb in range(NTB):
            for c in range(16):
                a = kb.ring("dbg", t_)
                kb.dma("sp", a[:], hT_d[c * 128:(c + 1) * 128, tb * 512:(tb + 1) * 512], R=[hT_r[tb][c]], W=[a.r])
                outs.append(kb.dma("pool", dbg_d[c * 128:(c + 1) * 128, tb * 512:(tb + 1) * 512], a[:], R=[a.r]))
        for tb in range(NTB):
            for c in range(EV_FC):
                a = kb.ring("dbg", t_)
                kb.dma("sp", a[:], pT_d[c * 128:(c + 1) * 128, tb * 512:(tb + 1) * 512], R=[pT_r[c]], W=[a.r])
                outs.append(kb.dma("pool", dbgp_d[c * 128:(c + 1) * 128, tb * 512:(tb + 1) * 512], a[:], R=[a.r]))
        tb_ = [kb.alloc([128, 512], BF16, "dbgb") for _ in range(2)]
        for tb in range(NTB):
            for c in range(16):
                a = kb.ring("dbgb", tb_)
                kb.dma("sp", a[:], mixT_d[c * 128:(c + 1) * 128, tb * 512:(tb + 1) * 512], R=[mix_r[c]], W=[a.r])
                outs.append(kb.dma("pool", dbgm_d[c * 128:(c + 1) * 128, tb * 512:(tb + 1) * 512], a[:], R=[a.r]))
        return outs

    load_phase()
    rope_phase()
    done = False
    for L in range(depth):
        S.new_epoch()
        j = L // 2
        if L % 2 == 0:
            inproj_phase(win_e[j], EV_FC, f"ev_norm{j}", precast=(L > 0), post_alloc=lambda j=j: conv_factory(j))
            rwkv_phase(j, jobs_out_ffn(L))
            outproj_phase(wout_e[j])
        else:
            inproj_phase(win_o[j], OD_FC, f"od_norm{j}", precast=True)
            pool_phase(j)
            mla_phase(j, jobs_out_ffn(L))
            outproj_phase(wout_o[j])
        if stop_after == (L, "mix"):
            done = True
            break
        S.new_epoch()
        ffn_phase(L, jobs_in(L + 1) if L + 1 < depth else None)
        if stop_after == (L, "ffn"):
            done = True
            break
    S.new_epoch()
    outs = final_phase()
    if stop_after is not None:
        outs = outs + debug_dump() + dbg_outs
    with ExitStack() as st:
        stats = S.emit(st, final_waits=outs)
    return nc, stats


def _colpack(vec, parts=128):
    v = np.asarray(vec, np.float32).reshape(-1)
    n = (len(v) + parts - 1) // parts
    out = np.zeros((128, n), np.float32)
    vv = np.zeros(n * parts, np.float32)
    vv[:len(v)] = v
    out[:parts, :] = vv.reshape(n, parts).T
    return out


def prepare(inputs, NT):
    f32 = lambda a: np.ascontiguousarray(np.asarray(a, np.float32))
    colmap = {}
    colsl = []
    ncol = [0]

    def addc(name, arr):
        colmap[name] = ncol[0]
        colsl.append(arr)
        ncol[0] += arr.shape[1]

    shared = {}
    for j in range(2):
        addc(f"ev_norm{j}", _colpack(inputs["ev_norm"][j]))
        addc(f"od_norm{j}", _colpack(inputs["od_norm"][j]))
        cw = f32(inputs["ev_conv_w"][j])
        addc(f"conv{j}", cw.reshape(8, 128, 3).transpose(1, 0, 2).reshape(128, 24))
        mu = f32(inputs["ev_mu"][j])
        addc(f"mu_r{j}", _colpack(mu[0:1024], 64))
        addc(f"mu_k{j}", _colpack(mu[1024:2048], 64))
        addc(f"mu_v{j}", _colpack(mu[2048:3072], 64))
        addc(f"mu_xw{j}", _colpack(mu[3072:3136], 64))
        addc(f"mu_xa{j}", _colpack(mu[3136:3200], 64))
        addc(f"mu_xg0{j}", _colpack(mu[3200:3328], 128))
        addc(f"mu_xg1{j}", _colpack(mu[3328:3360], 32))
        for nm in ("w0", "a0", "k_k", "k_a", "ln_w", "ln_b"):
            addc(f"{nm}{j}", _colpack(inputs["ev_" + nm][j], 64))
        addc(f"r_k{j}", _colpack(f32(inputs["ev_r_k"][j]).reshape(-1), 64))
        addc(f"pool_scale{j}", _colpack(inputs["od_pool_scale"][j]))
        addc(f"q_norm{j}", _colpack(inputs["od_q_norm"][j]))
        addc(f"kv_norm{j}", _colpack(inputs["od_kv_norm"][j]))
    for l in range(4):
        addc(f"ffn_norm{l}", _colpack(inputs["ffn_norm"][l]))
    addc("final_norm", _colpack(inputs["final_norm"]))
    inv = (1.0 / (10000.0 ** (np.arange(0, 64, 2, dtype=np.float32) / np.float32(64)))).astype(np.float32)
    addc("invf", _colpack(np.concatenate([inv, inv]), 64))
    addc("negpi", _colpack(np.full(64, -math.pi, np.float32), 64))
    addc("negone", _colpack(np.full(64, -1.0, np.float32), 64))
    addc("sinsign", _colpack(np.concatenate([np.full(32, -1.0), np.full(32, 1.0)]).astype(np.float32), 64))
    addc("eps6", _colpack(np.full(128, 1e-6, np.float32)))
    addc("gneps", _colpack(np.full(128, 64e-5, np.float32)))
    addc("tiny", _colpack(np.full(128, 1e-24, np.float32)))
    cols = np.ascontiguousarray(np.concatenate(colsl, axis=1))
    shared["cols"] = cols
    shared["ident"] = np.eye(128, dtype=np.float32)
    s_ = np.arange(64)[:, None]
    t_ = np.arange(64)[None, :]
    m = np.zeros((64, 5, 8, 64), np.float32)
    m[:, 0] = (s_ < t_).astype(np.float32)[:, None, :]
    m[:, 1] = (s_ > t_).astype(np.float32)[:, None, :]
    m[:, 2] = (s_ <= t_).astype(np.float32)[:, None, :]
    m[:, 3] = np.eye(64, dtype=np.float32)[:, None, :]
    rst = np.ones((8, 64), np.float32)
    rst[:, 0] = 0.0
    m[:, 4] = rst[None]
    shared["masks"] = np.ascontiguousarray(m.reshape(64, 5, 512))
    shared["tri"] = (np.arange(128)[None, :] >= np.arange(128)[:, None]).astype(ml_dtypes.bfloat16)
    pf = np.zeros((128, 4, 16), np.float32)
    for g in range(4):
        win = 2 << g
        pf[:, g, :] = 1.0 / np.minimum(np.arange(16) + 1, win)
    shared["poolfix"] = pf
    for j in range(2):
        w = f32(inputs["ev_w_in"][j])
        wp = np.zeros((D, EV_FC * 128), np.float32)
        wp[:, 0:6144] = w[:, 0:6144]
        wp[:, 6144:6144 + 64] = w[:, 6144:6208]
        wp[:, 6272:6272 + 64] = w[:, 6208:6272]
        wp[:, 6400:6400 + 128] = w[:, 6272:6400]
        wp[:, 6528:6528 + 32] = w[:, 6400:6432]
        shared[f"ev_w_in{j}"] = wp
        shared[f"ev_w_out{j}"] = f32(inputs["ev_w_out"][j])
        shared[f"ev_w2{j}"] = f32(inputs["ev_w2"][j])
        shared[f"ev_a2{j}"] = f32(inputs["ev_a2"][j])
        shared[f"ev_g2{j}"] = f32(inputs["ev_g2"][j])
        w = f32(inputs["od_w_in"][j])
        wp = np.zeros((D, OD_FC * 128), np.float32)
        wp[:, 0:1536] = w[:, 0:1536]
        kpe = w[:, 1536:1600]
        wp[:, 1536:1600] = kpe
        wp[:, 1664:1696] = kpe[:, 32:64]
        wp[:, 1696:1728] = kpe[:, 0:32]
        shared[f"od_w_in{j}"] = wp
        shared[f"od_w_out{j}"] = f32(inputs["od_w_out"][j])
        shared[f"od_pool_w{j}"] = f32(inputs["od_pool_w"][j])
        wq = f32(inputs["od_w_uq"][j]).reshape(512, 12, 192)
        wqp = np.zeros((512, 12, 256), np.float32)
        wqp[:, :, 0:128] = wq[:, :, 0:128]
        wqp[:, :, 128:192] = wq[:, :, 128:192]
        wqp[:, :, 192:224] = wq[:, :, 160:192]
        wqp[:, :, 224:256] = wq[:, :, 128:160]
        shared[f"od_w_uq{j}"] = np.ascontiguousarray(wqp.reshape(512, 3072))
        wkv = f32(inputs["od_w_ukv"][j]).reshape(512, 12, 256)
        shared[f"od_w_uk{j}"] = np.ascontiguousarray(wkv[:, :, 0:128].reshape(512, 1536))
        shared[f"od_w_uv{j}"] = np.ascontiguousarray(wkv[:, :, 128:256].reshape(512, 1536))
    for l in range(4):
        shared[f"ffn_wg{l}"] = f32(inputs["ffn_w_gate"][l])
        shared[f"ffn_wu{l}"] = f32(inputs["ffn_w_up"][l])
        shared[f"ffn_wd{l}"] = f32(inputs["ffn_w_down"][l])
    return shared, colmap, cols.shape[1]


_CACHE = {}


def kernel(**inputs):
    x = np.asarray(inputs["x"], np.float32)
    pos = np.asarray(inputs["positions"], np.int32)
    B, NT, _ = x.shape
    shared, colmap, ncol = prepare(inputs, NT)
    key = (NT, ncol)
    if key not in _CACHE:
        _CACHE[key] = build_program(NT, colmap, ncol)
    nc, stats = _CACHE[key]
    n_cores = 8
    in_maps = []
    for c in range(n_cores):
        b = c % B
        m = dict(shared)
        m["x"] = np.ascontiguousarray(x[b])
        m["pos"] = np.ascontiguousarray(pos[b].reshape(1, NT))
        in_maps.append(m)
    res = run_bass_kernel_spmd(nc, in_maps, core_ids=list(range(n_cores)))
    out = np.stack([np.asarray(res.results[b]["out"], np.float32) for b in range(B)], axis=0)
    return out
```

```python
import math
import numpy as np
import ml_dtypes
from contextlib import ExitStack
import concourse.bass as bass
import concourse.mybir as mybir
from concourse.bass_utils import run_bass_kernel_spmd

F32 = mybir.dt.float32
BF16 = mybir.dt.bfloat16
I32 = mybir.dt.int32
ALU = mybir.AluOpType
AF = mybir.ActivationFunctionType

ENGS = ("pe", "act", "dve", "pool", "sp")
DMA_RING = 8

D = 2048
DFF = 5632
FFC = 44
NH_R = 16
NH_A = 12
EV_FC = 52
OD_FC = 14
SM_SCALE = 192.0 ** -0.5


class Res:
    __slots__ = ("name", "writers", "readers", "prev_readers")

    def __init__(self, name=""):
        self.name = name
        self.writers = []
        self.readers = []
        self.prev_readers = []


class Op:
    __slots__ = ("eng", "fn", "deps", "sig", "sig_no", "dma", "dma_idx", "epoch", "idx")


class Sched:
    def __init__(self, nc):
        self.nc = nc
        self.ops = {e: [] for e in ENGS}
        self.epoch = 0
        self.dma_count = {e: 0 for e in ENGS}
        self.pending = {e: None for e in ENGS}

    def new_epoch(self):
        self.epoch += 1

    def barrier(self):
        last = []
        for e in ENGS:
            ops = self.ops[e]
            if not ops:
                continue
            if e == "sp" or any(o.dma for o in ops[-DMA_RING * 4:]):
                dm = [o for o in ops if o.dma][-DMA_RING:]
                last.extend(dm)
            nd = [o for o in ops if not o.dma]
            if nd:
                last.append(nd[-1])
        for e in ENGS:
            self.pending[e] = list(last)

    def add(self, eng, fn, reads=(), writes=(), dma=False, pwrites=()):
        op = Op()
        op.eng = eng
        op.fn = fn
        op.dma = dma
        op.sig = False
        op.sig_no = None
        op.epoch = self.epoch
        op.idx = len(self.ops[eng])
        deps = {}

        def add_dep(p):
            if p.dma:
                deps[id(p)] = p
                return
            if p.eng == eng and not dma and eng == "pe":
                return
            key = ("c", p.eng)
            q = deps.get(key)
            if q is None or q.idx < p.idx:
                deps[key] = p

        if self.pending[eng] is not None:
            for p in self.pending[eng]:
                add_dep(p)
            self.pending[eng] = None
        for r in reads:
            for p in r.writers:
                add_dep(p)
        for w in writes:
            for p in w.writers:
                add_dep(p)
            for rd in w.readers:
                add_dep(rd)
        for w in pwrites:
            for rd in w.readers:
                add_dep(rd)
            for rd in w.prev_readers:
                add_dep(rd)
        for r in reads:
            if not dma:
                r.readers = [x for x in r.readers if x.dma or x.eng != eng]
            r.readers.append(op)
        for w in writes:
            w.prev_readers = w.readers
            w.writers = [op]
            w.readers = []
        for w in pwrites:
            if w.readers:
                w.prev_readers = w.readers
                w.writers = [op]
                w.readers = []
            else:
                if not dma:
                    w.writers = [x for x in w.writers if x.dma or x.eng != eng]
                w.writers.append(op)
        op.deps = list(deps.values())
        for p in op.deps:
            p.sig = True
        if dma:
            op.dma_idx = self.dma_count[eng]
            self.dma_count[eng] += 1
            op.sig = True
        self.ops[eng].append(op)
        return op

    def emit(self, stack, final_waits=()):
        nc = self.nc
        n_epochs = self.epoch + 1
        csem = {}
        for e in ("pe", "act", "dve", "pool"):
            for ep in range(n_epochs):
                if any(o.sig and not o.dma and o.epoch == ep for o in self.ops[e]):
                    csem[(e, ep)] = stack.enter_context(nc.semaphore(f"c_{e}_{ep}"))
        dsem = {}
        for e in ENGS:
            if self.dma_count[e]:
                dsem[e] = [stack.enter_context(nc.semaphore(f"d_{e}_{i}")) for i in range(DMA_RING)]
        for e in ENGS:
            cnt = {}
            for o in self.ops[e]:
                if o.dma:
                    continue
                if o.sig:
                    cnt[o.epoch] = cnt.get(o.epoch, 0) + 1
                    o.sig_no = cnt[o.epoch]

        def waits_for(o):
            ws = []
            for p in o.deps:
                if p.dma:
                    ws.append((dsem[p.eng][p.dma_idx % DMA_RING], 16 * (p.dma_idx // DMA_RING + 1)))
                else:
                    ws.append((csem[(p.eng, p.epoch)], p.sig_no))
            if o.dma and o.dma_idx >= DMA_RING:
                ws.append((dsem[o.eng][o.dma_idx % DMA_RING], 16 * (o.dma_idx // DMA_RING)))
            return ws

        block = stack.enter_context(nc.Block())
        engmap = {"pe": block.tensor, "act": block.scalar, "dve": block.vector,
                  "pool": block.gpsimd, "sp": block.sync}
        stats = {}
        for e in ENGS:
            ops = self.ops[e]
            if not ops:
                continue
            stats[e] = len(ops)

            def body(eng, ops=ops, e=e):
                have = {}
                for o in ops:
                    for (s, v) in waits_for(o):
                        k = id(s)
                        if have.get(k, 0) >= v:
                            continue
                        have[k] = v
                        eng.wait_ge(s, v)
                    ins = o.fn(eng)
                    if o.sig:
                        if o.dma:
                            ins.then_inc(dsem[o.eng][o.dma_idx % DMA_RING], 16)
                        else:
                            ins.then_inc(csem[(o.eng, o.epoch)], 1)
                if e == "sp":
                    for o in final_waits:
                        eng.wait_ge(dsem[o.eng][o.dma_idx % DMA_RING], 16 * (o.dma_idx // DMA_RING + 1))
            engmap[e](body)
        return stats


class T:
    def __init__(self, h, name=""):
        self.h = h
        self.r = Res(name)

    def __getitem__(self, k):
        return self.h[k]


DTSZ = {F32: 4, BF16: 2, I32: 4}


class KB:
    def __init__(self, nc, NT):
        self.nc = nc
        self.NT = NT
        self.NTB = NT // 512
        self.S = Sched(nc)
        self.cur = 16512
        self.end = 229376
        self.uid = 0
        self.bank_i = 0
        self.wi = 0
        self.ri = {}
        self.held = set()

    def alloc(self, shape, dt, name="t"):
        n = 1
        for s in shape[1:]:
            n *= s
        nb = (n * DTSZ[dt] + 63) // 64 * 64
        off = self.cur
        self.cur += nb
        assert self.cur <= self.end, f"SBUF overflow {self.cur} allocating {name}"
        self.uid += 1
        h = self.nc.alloc_sbuf_tensor_at(f"{name}{self.uid}", list(shape), dt, offset=off)
        return T(h, name)

    def mark(self):
        return self.cur

    def phase(self, mark):
        self.cur = mark
        self.S.barrier()

    def bank(self, avoid=(), hold=False):
        while True:
            b = self.bank_i % 8
            self.bank_i += 1
            if b not in avoid and b not in self.held:
                if hold:
                    self.held.add(b)
                return b

    def release(self, *bs):
        for b in bs:
            self.held.discard(b)

    def ring(self, key, tiles):
        i = self.ri.get(key, 0)
        self.ri[key] = i + 1
        return tiles[i % len(tiles)]

    def mm(self, out, lhsT, rhs, start=True, stop=True, R=(), W=(), PW=()):
        return self.S.add("pe", lambda e: e.matmul(out, lhsT=lhsT, rhs=rhs, start=start, stop=stop), R, W, pwrites=PW)

    def tr(self, out, in_, ident, R=(), W=(), PW=()):
        return self.S.add("pe", lambda e: e.transpose(out=out, in_=in_, identity=ident), R, W, pwrites=PW)

    def act(self, out, in_, func, R=(), W=(), PW=(), **kw):
        return self.S.add("act", lambda e: e.activation(out=out, in_=in_, func=func, **kw), R, W, pwrites=PW)

    def tt(self, eng, out, in0, in1, op, R=(), W=(), PW=()):
        return self.S.add(eng, lambda e: e.tensor_tensor(out=out, in0=in0, in1=in1, op=op), R, W, pwrites=PW)

    def ts(self, eng, out, in0, s1, s2, op0, op1=None, R=(), W=(), PW=()):
        if op1 is None:
            return self.S.add(eng, lambda e: e.tensor_scalar(out=out, in0=in0, scalar1=s1, scalar2=None, op0=op0), R, W, pwrites=PW)
        return self.S.add(eng, lambda e: e.tensor_scalar(out=out, in0=in0, scalar1=s1, scalar2=s2, op0=op0, op1=op1), R, W, pwrites=PW)

    def stt(self, out, in0, scalar, in1, op0, op1, R=(), W=(), PW=()):
        return self.S.add("dve", lambda e: e.scalar_tensor_tensor(out=out, in0=in0, scalar=scalar, in1=in1, op0=op0, op1=op1), R, W, pwrites=PW)

    def copy(self, eng, out, in_, R=(), W=(), PW=()):
        if eng == "act":
            return self.S.add("act", lambda e: e.copy(out=out, in_=in_), R, W, pwrites=PW)
        return self.S.add(eng, lambda e: e.tensor_copy(out=out, in_=in_), R, W, pwrites=PW)

    def memset(self, eng, ap, val, W=(), PW=()):
        return self.S.add(eng, lambda e: e.memset(ap, val), (), W, pwrites=PW)

    def dma(self, q, out, in_, R=(), W=(), PW=()):
        return self.S.add(q, lambda e: e.dma_start(out=out, in_=in_), R, W, dma=True, pwrites=PW)


def build_program(NT, colmap, NCOL, stop_after=None, depth=4):
    nc = bass.Bass("TRN2", target_bir_lowering=False)
    kb = KB(nc, NT)
    S = kb.S
    NTB = NT // 512
    dram_in = lambda n, shp, dt: nc.dram_tensor(n, list(shp), dt, kind="ExternalInput").ap()
    dram_tmp = lambda n, shp, dt: nc.dram_tensor(n, list(shp), dt, kind="Internal").ap()

    x_d = dram_in("x", [NT, D], F32)
    pos_d = dram_in("pos", [1, NT], I32)
    cols_d = dram_in("cols", [128, NCOL], F32)
    ident_d = dram_in("ident", [128, 128], F32)
    masks_d = dram_in("masks", [64, 5, 512], F32)
    tri_d = dram_in("tri", [128, 128], BF16)
    poolfix_d = dram_in("poolfix", [128, 4, 16], F32)
    win_e = [dram_in(f"ev_w_in{j}", [D, EV_FC * 128], F32) for j in range(2)]
    wout_e = [dram_in(f"ev_w_out{j}", [D, D], F32) for j in range(2)]
    w2_d = [dram_in(f"ev_w2{j}", [64, 1024], F32) for j in range(2)]
    a2_d = [dram_in(f"ev_a2{j}", [64, 1024], F32) for j in range(2)]
    g2_d = [dram_in(f"ev_g2{j}", [160, 1024], F32) for j in range(2)]
    win_o = [dram_in(f"od_w_in{j}", [D, OD_FC * 128], F32) for j in range(2)]
    wout_o = [dram_in(f"od_w_out{j}", [D, D], F32) for j in range(2)]
    poolw_d = [dram_in(f"od_pool_w{j}", [4, 128, 128], F32) for j in range(2)]
    wuq_d = [dram_in(f"od_w_uq{j}", [512, 24 * 128], F32) for j in range(2)]
    wuk_d = [dram_in(f"od_w_uk{j}", [512, 12 * 128], F32) for j in range(2)]
    wuv_d = [dram_in(f"od_w_uv{j}", [512, 12 * 128], F32) for j in range(2)]
    wg_d = [dram_in(f"ffn_wg{l}", [D, DFF], F32) for l in range(4)]
    wu_d = [dram_in(f"ffn_wu{l}", [D, DFF], F32) for l in range(4)]
    wd_d = [dram_in(f"ffn_wd{l}", [DFF, D], F32) for l in range(4)]
    out_d = nc.dram_tensor("out", [NT, D], F32, kind="ExternalOutput").ap()
    dbg_d = nc.dram_tensor("dbg", [D, NT], F32, kind="ExternalOutput").ap() if stop_after is not None else None
    dbgp_d = nc.dram_tensor("dbgp", [EV_FC * 128, NT], F32, kind="ExternalOutput").ap() if stop_after is not None else None
    dbgm_d = nc.dram_tensor("dbgm", [D, NT], BF16, kind="ExternalOutput").ap() if stop_after is not None else None
    dbgr_d = nc.dram_tensor("dbgr", [26, 64, 512], F32, kind="ExternalOutput").ap() if stop_after is not None else None
    dbg_outs = []

    hT_d = dram_tmp("hT", [D, NT], F32)
    pT_d = dram_tmp("pT", [EV_FC * 128, NT], F32)
    mixT_d = dram_tmp("mixT", [D, NT], BF16)
    cs_d = dram_tmp("cs", [2, 64, NT], F32)
    wb_in = dram_tmp("wb_in", [EV_FC, 128, 16 * 128], BF16)
    wb_out = dram_tmp("wb_out", [16, 128, 16 * 128], BF16)
    wb_g = dram_tmp("wb_g", [FFC, 128, 16 * 128], BF16)
    wb_u = dram_tmp("wb_u", [FFC, 128, 16 * 128], BF16)
    wb_d = dram_tmp("wb_d", [16, 128, FFC * 128], BF16)
    wb_uq = dram_tmp("wb_uq", [24, 128, 4 * 128], BF16)
    wb_uk = dram_tmp("wb_uk", [12, 128, 4 * 128], BF16)
    hT_r = [[Res(f"hT{i}_{c}") for c in range(16)] for i in range(NTB)]
    pT_r = [Res(f"pT{i}") for i in range(EV_FC)]
    mix_r = [Res(f"mix{i}") for i in range(16)]
    cs_r = Res("cs")
    wr = {k: Res(k) for k in ("in", "out", "g", "u", "d", "uq", "uk")}
    hT_v = hT_d.rearrange("(c p) t -> p c t", p=128)
    mixT_v = mixT_d.rearrange("(c p) t -> p c t", p=128)

    ps = [nc.alloc_psum_tensor(f"ps{i}", [128, 512], F32) for i in range(8)]
    psr = [Res(f"ps{i}") for i in range(8)]
    cols = kb.alloc([128, NCOL], F32, "cols")
    ident = kb.alloc([128, 128], F32, "ident")
    ones_bf = kb.alloc([128, 128], BF16, "ones")
    ones64 = kb.alloc([64, 64], F32, "ones64")
    mean64 = kb.alloc([64, 64], F32, "mean64")
    masks = kb.alloc([64, 5, 512], F32, "masks")
    tri = kb.alloc([128, 128], BF16, "tri")
    wring = [kb.alloc([128, FFC * 128], BF16, "wring") for _ in range(3)]
    kb.dma("sp", cols[:], cols_d, W=[cols.r])
    kb.dma("sp", ident[:], ident_d, W=[ident.r])
    kb.dma("sp", masks[:], masks_d, W=[masks.r])
    kb.dma("sp", tri[:], tri_d, W=[tri.r])
    kb.memset("pool", ones_bf[:], 1.0, W=[ones_bf.r])
    kb.memset("pool", ones64[:], 1.0, W=[ones64.r])
    kb.memset("pool", mean64[:], 1.0 / 64.0, W=[mean64.r])
    id64b_t = kb.alloc([64, 64], BF16, "id64b")
    kb.copy("act", id64b_t[:], ident[0:64, 0:64], R=[ident.r], W=[id64b_t.r])
    base_mark = kb.mark()

    def col(name, j=0, parts=128):
        i = colmap[name] + j
        return cols[0:parts, i:i + 1]

    def cast_weight(src, K, F, dst, dres, gain=None):
        S.barrier()
        m = kb.mark()
        wf = [kb.alloc([128, 2048], F32, "wf") for _ in range(3)]
        wbt = [kb.alloc([128, 2048], BF16, "wbt") for _ in range(3)]
        KCn = K // 128
        i = 0
        for c in range(KCn):
            for f0 in range(0, F, 2048):
                fw = min(2048, F - f0)
                a = wf[i % 3]
                b = wbt[i % 3]
                kb.dma("sp", a[:, 0:fw], src[c * 128:(c + 1) * 128, f0:f0 + fw], W=[a.r])
                eng = ("act", "pool", "dve")[i % 3]
                if gain is None:
                    kb.copy(eng, b[:, 0:fw], a[:, 0:fw], R=[a.r], W=[b.r])
                elif eng == "act":
                    kb.act(b[:, 0:fw], a[:, 0:fw], AF.Copy, R=[a.r, cols.r], W=[b.r], scale=col(gain, c))
                else:
                    kb.ts(eng, b[:, 0:fw], a[:, 0:fw], col(gain, c), None, ALU.mult, R=[a.r, cols.r], W=[b.r])
                dv = dst[f0 // 128:(f0 + fw) // 128, :, c * 128:(c + 1) * 128].rearrange("fc p f -> p fc f")
                kb.dma("pool", dv, b[:, 0:fw].rearrange("p (fc f) -> p fc f", f=128), R=[b.r], PW=[dres])
                i += 1
        kb.cur = m
        S.barrier()

    class Slot:
        pass

    def wring_slots():
        out = []
        for wt in wring:
            sl = Slot()
            sl.f32 = wt[:, 512:2560].bitcast(F32)
            sl.b16 = wt[:, 2560:3584]
            sl.rf = Res("slf")
            sl.rb = Res("slb")
            out.append(sl)
        return out

    def alloc_slots(n=3):
        out = []
        for _ in range(n):
            a = kb.alloc([128, 1024], F32, "slf")
            b = kb.alloc([128, 1024], BF16, "slb")
            sl = Slot()
            sl.f32, sl.b16, sl.rf, sl.rb = a[:], b[:], a.r, b.r
            out.append(sl)
        return out

    def cast_steps(jobs, slots):
        tiles = []
        for (src, K, F, dst, dres, gain) in jobs:
            for c in range(K // 128):
                for f0 in range(0, F, 1024):
                    tiles.append((src, dst, dres, gain, c, f0, min(1024, F - f0)))
        n = len(tiles)
        NS = len(slots)
        for k in range(n + 2):
            if k < n:
                (src, dst, dres, gain, c, f0, fw) = tiles[k]
                sl = slots[k % NS]
                kb.dma("sp", sl.f32[:, 0:fw], src[c * 128:(c + 1) * 128, f0:f0 + fw], W=[sl.rf])
            if 0 <= k - 1 < n:
                (src, dst, dres, gain, c, f0, fw) = tiles[k - 1]
                sl = slots[(k - 1) % NS]
                if gain is None:
                    kb.copy("act", sl.b16[:, 0:fw], sl.f32[:, 0:fw], R=[sl.rf], W=[sl.rb])
                else:
                    kb.ts("dve", sl.b16[:, 0:fw], sl.f32[:, 0:fw], col(gain, c), None, ALU.mult, R=[sl.rf, cols.r], W=[sl.rb])
            if 0 <= k - 2 < n:
                (src, dst, dres, gain, c, f0, fw) = tiles[k - 2]
                sl = slots[(k - 2) % NS]
                dv = dst[f0 // 128:(f0 + fw) // 128, :, c * 128:(c + 1) * 128].rearrange("fc p f -> p fc f")
                kb.dma("sp", dv, sl.b16[:, 0:fw].rearrange("p (fc f) -> p fc f", f=128), R=[sl.rb], PW=[dres])
            yield

    def bg_step(it):
        if it[0] is not None:
            try:
                next(it[0])
            except StopIteration:
                it[0] = None

    def bg_drain(it):
        while it[0] is not None:
            bg_step(it)

    def jobs_out_ffn(L):
        wo = wout_e[L // 2] if L % 2 == 0 else wout_o[L // 2]
        return [(wo, D, D, wb_out, wr["out"], None),
                (wg_d[L], D, DFF, wb_g, wr["g"], f"ffn_norm{L}"),
                (wu_d[L], D, DFF, wb_u, wr["u"], f"ffn_norm{L}"),
                (wd_d[L], DFF, D, wb_d, wr["d"], None)]

    def jobs_in(L):
        j_ = L // 2
        if L % 2 == 0:
            return [(win_e[j_], D, EV_FC * 128, wb_in, wr["in"], f"ev_norm{j_}")]
        return [(win_o[j_], D, OD_FC * 128, wb_in, wr["in"], f"od_norm{j_}"),
                (wuq_d[j_], 512, 24 * 128, wb_uq, wr["uq"], f"q_norm{j_}"),
                (wuk_d[j_], 512, 12 * 128, wb_uk, wr["uk"], f"kv_norm{j_}")]

    def linear(xT, KCn, wd, wres, flist, evac):
        for f in flist:
            wt = kb.ring("w", wring)
            kb.dma("sp", wt[:, 0:KCn * 128], wd[f], R=[wres], W=[wt.r])
            b = kb.bank()
            for c in range(KCn):
                kb.mm(ps[b][:, :], wt[:, c * 128:(c + 1) * 128], xT[:, c, :], start=(c == 0), stop=(c == KCn - 1),
                      R=[wt.r, xT.r], W=[psr[b]] if c == 0 else (), PW=() if c == 0 else [psr[b]])
            evac(f, b)

    def stats_rows(src_tiles, nchunks_total, rstd, inv_n, eps, sqring):
        b = kb.bank()
        k = 0
        for (tile_, n) in src_tiles:
            sq = kb.ring("sq", sqring)
            kb.tt("pool", sq[:, 0:n, :], tile_[:, 0:n, :], tile_[:, 0:n, :], ALU.mult, R=[tile_.r], W=[sq.r])
            for j in range(n):
                kb.mm(ps[b][:, :], ones_bf[:], sq[:, j, :], start=(k == 0), stop=(k == nchunks_total - 1),
                      R=[ones_bf.r, sq.r], W=[psr[b]] if k == 0 else (), PW=() if k == 0 else [psr[b]])
                k += 1
        kb.act(rstd[:], ps[b][:], AF.Sqrt, R=[psr[b], cols.r], W=[rstd.r], scale=inv_n, bias=col("eps6"))
        S.add("dve", lambda e: e.reciprocal(out=rstd[:], in_=rstd[:]), [rstd.r], [rstd.r])

    def norm_block(tb, xn, rstd, hring, sqring):
        tiles = []
        for q in range(4):
            ht = kb.ring("h", hring)
            kb.dma("sp", ht[:], hT_v[:, q * 4:(q + 1) * 4, tb * 512:(tb + 1) * 512], R=hT_r[tb][q * 4:(q + 1) * 4], W=[ht.r])
            kb.copy("act", xn[:, q * 4:(q + 1) * 4, :], ht[:], R=[ht.r], PW=[xn.r])
            tiles.append((ht, 4))
        stats_rows(tiles, 16, rstd, 1.0 / D, 1e-6, sqring)

    def resid_evac(tb, hcr, hnr):
        def ev(f, b):
            hc = kb.ring("hc", hcr)
            hn = kb.ring("hn", hnr)
            kb.dma("sp", hc[:], hT_d[f * 128:(f + 1) * 128, tb * 512:(tb + 1) * 512], R=[hT_r[tb][f]], W=[hc.r])
            kb.tt("dve", hn[:], ps[b][:], hc[:], ALU.add, R=[psr[b], hc.r], W=[hn.r])
            kb.dma("pool", hT_d[f * 128:(f + 1) * 128, tb * 512:(tb + 1) * 512], hn[:], R=[hn.r], W=[hT_r[tb][f]])
        return ev

    def outproj_phase(wsrc):
        kb.phase(base_mark)
        xm = [kb.alloc([128, 16, 512], BF16, "xm") for _ in range(2)]
        hcr = [kb.alloc([128, 512], F32, "hc") for _ in range(3)]
        hnr = [kb.alloc([128, 512], F32, "hn") for _ in range(3)]
        for tb in range(NTB):
            x_ = xm[tb % 2]
            kb.dma("sp", x_[:], mixT_v[:, :, tb * 512:(tb + 1) * 512], R=mix_r, W=[x_.r])
            linear(x_, 16, wb_out, wr["out"], range(16), resid_evac(tb, hcr, hnr))

    def ffn_phase(l, bg_jobs):
        kb.phase(base_mark)
        bgit = [cast_steps(bg_jobs, alloc_slots()) if bg_jobs else None]
        xn = kb.alloc([128, 16, 512], BF16, "xn")
        actT = kb.alloc([128, FFC, 512], BF16, "actT")
        rstd = kb.alloc([128, 512], F32, "rstd")
        hring = [kb.alloc([128, 4, 512], F32, "hr") for _ in range(4)]
        sqring = [kb.alloc([128, 4, 512], BF16, "sq") for _ in range(2)]
        t1r = [kb.alloc([128, 512], F32, "t1") for _ in range(2)]
        t2r = [kb.alloc([128, 512], F32, "t2") for _ in range(2)]
        t3r = [kb.alloc([128, 512], F32, "t3") for _ in range(2)]
        hcr = [kb.alloc([128, 512], F32, "hc") for _ in range(3)]
        hnr = [kb.alloc([128, 512], F32, "hn") for _ in range(3)]
        for tb in range(NTB):
            norm_block(tb, xn, rstd, hring, sqring)
            for f in range(FFC):
                bg = [None]
                linear(xn, 16, wb_g, wr["g"], [f], lambda f_, b_: bg.__setitem__(0, b_))
                bu = [None]
                linear(xn, 16, wb_u, wr["u"], [f], lambda f_, b_: bu.__setitem__(0, b_))
                t1 = kb.ring("t1", t1r)
                t2 = kb.ring("t2", t2r)
                t3 = kb.ring("t3", t3r)
                kb.tt("dve", t1[:], ps[bg[0]][:], rstd[:], ALU.mult, R=[psr[bg[0]], rstd.r], W=[t1.r])
                kb.act(t2[:], t1[:], AF.Silu, R=[t1.r], W=[t2.r])
                kb.tt("dve", t3[:], ps[bu[0]][:], rstd[:], ALU.mult, R=[psr[bu[0]], rstd.r], W=[t3.r])
                kb.tt("pool", actT[:, f, :], t2[:], t3[:], ALU.mult, R=[t2.r, t3.r], PW=[actT.r])
                bg_step(bgit)
            linear(actT, FFC, wb_d, wr["d"], range(16), resid_evac(tb, hcr, hnr))
        bg_drain(bgit)

    def inproj_phase(wsrc, FC, gain, precast, post_alloc=None):
        kb.phase(base_mark)
        if not precast:
            cast_weight(wsrc, D, FC * 128, wb_in, wr["in"], gain=gain)
            S.barrier()
        xn = kb.alloc([128, 16, 512], BF16, "xn")
        rstd = kb.alloc([128, 512], F32, "rstd")
        hring = [kb.alloc([128, 4, 512], F32, "hr") for _ in range(4)]
        sqring = [kb.alloc([128, 4, 512], BF16, "sq") for _ in range(2)]
        evr = [kb.alloc([128, 512], F32, "ev") for _ in range(3)]
        post_tb = post_alloc() if post_alloc is not None else None
        for tb in range(NTB):
            norm_block(tb, xn, rstd, hring, sqring)

            def ev(f, b, tb=tb):
                e_ = kb.ring("ev", evr)
                kb.tt("dve", e_[:], ps[b][:], rstd[:], ALU.mult, R=[psr[b], rstd.r], W=[e_.r])
                kb.dma("pool", pT_d[f * 128:(f + 1) * 128, tb * 512:(tb + 1) * 512], e_[:], R=[e_.r], PW=[pT_r[f]])
            linear(xn, 16, wb_in, wr["in"], range(FC), ev)
            if post_tb is not None:
                post_tb(tb)

    def load_phase():
        xt = [kb.alloc([128, D], F32, "xt") for _ in range(2)]
        hs = [kb.alloc([128, 16, 128], F32, "hs") for _ in range(2)]
        for t in range(NT // 128):
            a = xt[t % 2]
            h_ = hs[t % 2]
            kb.dma("sp", a[:], x_d[t * 128:(t + 1) * 128, :], W=[a.r])
            for g in range(4):
                b = kb.bank()
                for j in range(4):
                    c = g * 4 + j
                    kb.tr(ps[b][:, j * 128:(j + 1) * 128], a[:, c * 128:(c + 1) * 128], ident[:],
                          R=[a.r, ident.r], W=[psr[b]] if j == 0 else (), PW=() if j == 0 else [psr[b]])
                kb.copy("dve" if g % 2 == 0 else "act", h_[:, g * 4:(g + 1) * 4, :],
                        ps[b][:].rearrange("p (a b) -> p a b", a=4), R=[psr[b]], PW=[h_.r])
            kb.dma("pool", hT_v[:, :, t * 128:(t + 1) * 128], h_[:], R=[h_.r], PW=hT_r[t // 4])

    def rope_phase():
        kb.phase(base_mark)
        pi_ = kb.alloc([64, NT], I32, "posi")
        pf = kb.alloc([64, NT], F32, "posf")
        t1 = kb.alloc([64, NT], F32, "rt1")
        t2 = kb.alloc([64, NT], F32, "rt2")
        kb.dma("sp", pi_[:], pos_d.partition_broadcast(64), W=[pi_.r])
        kb.copy("dve", pf[:], pi_[:], R=[pi_.r], W=[pf.r])
        kb.ts("dve", pf[:], pf[:], col("invf", 0, 64), None, ALU.mult, R=[pf.r, cols.r], W=[pf.r])
        C1 = 6.28125
        C2 = 2.0 * math.pi - 6.28125
        ki = kb.alloc([64, NT], I32, "ki")
        for which, shift in ((0, math.pi / 2), (1, 0.0)):
            kb.ts("dve", t1[:], pf[:], shift, None, ALU.add, R=[pf.r], W=[t1.r])
            kb.ts("dve", t2[:], t1[:], 1.0 / (2.0 * math.pi), None, ALU.mult, R=[t1.r], W=[t2.r])
            kb.copy("dve", ki[:], t2[:], R=[t2.r], W=[ki.r])
            kb.copy("dve", t2[:], ki[:], R=[ki.r], W=[t2.r])
            kb.stt(t1[:], t2[:], -C1, t1[:], ALU.mult, ALU.add, R=[t1.r, t2.r], W=[t1.r])
            kb.stt(t1[:], t2[:], -C2, t1[:], ALU.mult, ALU.add, R=[t1.r, t2.r], W=[t1.r])
            kb.ts("dve", t1[:], t1[:], -3.141592, 3.141592, ALU.max, ALU.min, R=[t1.r], W=[t1.r])
            kb.act(t2[:], t1[:], AF.Sin, R=[t1.r], W=[t2.r])
            if which == 1:
                kb.ts("dve", t2[:], t2[:], col("sinsign", 0, 64), None, ALU.mult, R=[t2.r, cols.r], W=[t2.r])
            kb.dma("pool", cs_d[which], t2[:], R=[t2.r], PW=[cs_r])

    def conv_factory(j):
        NR = 2
        bt = [kb.alloc([128, 512], F32, "cb") for _ in range(NR)]
        ct = [kb.alloc([128, 514], F32, "cc") for _ in range(NR)]
        htl = [kb.alloc([128, 514], F32, "ch") for _ in range(NR)]
        ut = [kb.alloc([128, 514], F32, "cu") for _ in range(NR)]
        y1 = [kb.alloc([128, 512], F32, "cy") for _ in range(NR)]
        yo = [kb.alloc([128, 512], BF16, "co") for _ in range(NR)]
        cnt = [0]

        def conv_block(tb):
            for c in range(8):
                i = cnt[0]
                cnt[0] += 1
                b_, c_, h_, u_, y_, o_ = bt[i % NR], ct[i % NR], htl[i % NR], ut[i % NR], y1[i % NR], yo[i % NR]
                t0 = tb * 512
                kb.dma("sp", b_[:], pT_d[c * 128:(c + 1) * 128, t0:t0 + 512], R=[pT_r[c]], W=[b_.r])
                if tb == 0:
                    kb.memset("pool", c_[:, 0:2], 0.0, W=[c_.r])
                    kb.memset("pool", h_[:, 0:2], 0.0, W=[h_.r])
                    kb.dma("sp", c_[:, 2:514], pT_d[(8 + c) * 128:(9 + c) * 128, 0:512], R=[pT_r[8 + c]], PW=[c_.r])
                    kb.dma("sp", h_[:, 2:514], pT_d[(16 + c) * 128:(17 + c) * 128, 0:512], R=[pT_r[16 + c]], PW=[h_.r])
                else:
                    kb.dma("sp", c_[:], pT_d[(8 + c) * 128:(9 + c) * 128, t0 - 2:t0 + 512], R=[pT_r[8 + c]], W=[c_.r])
                    kb.dma("sp", h_[:], pT_d[(16 + c) * 128:(17 + c) * 128, t0 - 2:t0 + 512], R=[pT_r[16 + c]], W=[h_.r])
                kb.tt("pool", u_[:], c_[:], h_[:], ALU.mult, R=[c_.r, h_.r], W=[u_.r])
                kb.ts("dve", y_[:], u_[:, 2:514], col(f"conv{j}", c * 3 + 2), None, ALU.mult, R=[u_.r, cols.r], W=[y_.r])
                kb.stt(y_[:], u_[:, 1:513], col(f"conv{j}", c * 3 + 1), y_[:], ALU.mult, ALU.add, R=[u_.r, y_.r, cols.r], W=[y_.r])
                kb.stt(y_[:], u_[:, 0:512], col(f"conv{j}", c * 3 + 0), y_[:], ALU.mult, ALU.add, R=[u_.r, y_.r, cols.r], W=[y_.r])
                kb.tt("pool", o_[:], b_[:], y_[:], ALU.mult, R=[b_.r, y_.r], W=[o_.r])
                kb.dma("pool", mixT_d[c * 128:(c + 1) * 128, t0:t0 + 512], o_[:], R=[o_.r], PW=[mix_r[c]])
        return conv_block

    def rwkv_phase(j, bg_jobs):
        kb.phase(base_mark)
        bg = [cast_steps(bg_jobs, wring_slots())]
        A = lambda shape, dt, n: kb.alloc(shape, dt, n)
        wtmp = A([128, 1024], F32, "wtmp")
        w2b = A([64, 1024], BF16, "w2b")
        a2b = A([64, 1024], BF16, "a2b")
        g2b0 = A([128, 1024], BF16, "g2b0")
        g2b1 = A([32, 1024], BF16, "g2b1")
        for (src, r0, n, dst) in ((w2_d[j], 0, 64, w2b), (a2_d[j], 0, 64, a2b), (g2_d[j], 0, 128, g2b0), (g2_d[j], 128, 32, g2b1)):
            kb.dma("sp", wtmp[0:n, :], src[r0:r0 + n, :], W=[wtmp.r])
            kb.copy("act", dst[0:n, :], wtmp[0:n, :], R=[wtmp.r], W=[dst.r])
        ST = [A([64, 64], F32, "ST") for _ in range(NH_R)]
        STb = [A([64, 64], BF16, "STb") for _ in range(NH_R)]
        for s_ in ST + STb:
            kb.memset("pool", s_[:], 0.0, W=[s_.r])
        id64b = id64b_t[:]
        xwr = A([64, 513], F32, "xwr"); xar = A([64, 513], F32, "xar")
        xg0r = A([128, 513], F32, "xg0r"); xg1r = A([32, 513], F32, "xg1r")
        sh_t = A([128, 512], F32, "sht")
        txw = A([64, 512], BF16, "txw"); xab = A([64, 512], BF16, "xab")
        sxg0 = A([128, 512], BF16, "sxg0"); sxg1 = A([32, 512], BF16, "sxg1")
        def make_set():
            raw_ = {n: A([64, 513], F32, n) for n in ["rr", "kr", "vr"]}
            f_ = {}
            for n in ["rm", "km", "vm", "d", "lw", "a", "g", "kk", "sq", "kappa", "kp", "b", "cum", "Ep", "Em", "Epv", "t"]:
                f_[n] = A([64, 512], F32, n)
            for n in ["rt", "at", "kh", "bh", "vmb", "Atok", "Vtok", "Khtok", "Bhtok", "P0", "PT0", "P1_", "PT1", "XT0", "XT1",
                      "AKT", "RKT", "RBT", "Pone", "U0", "ApT", "U"]:
                f_[n] = A([64, 512], BF16, n)
            for new, old in (("C0", "kk"), ("YT", "cum"), ("yc", "Em"), ("rk", "Epv"), ("rs", "d")):
                f_[new] = f_[old]
            ob_ = A([64, 512], BF16, "ob")
            return raw_, f_, ob_
        sets = [make_set(), make_set()]
        M_UP, M_LO, M_UPI, M_I8, M_RST = (masks[:, i, :] for i in range(5))
        id64 = ident[0:64, 0:64]
        NEG_E = -math.exp(-0.5)

        def shift_mix(dst, rawt, mucol, parts, d):
            kb.tt("pool", d[0:parts, :], rawt[0:parts, 0:512], rawt[0:parts, 1:513], ALU.subtract, R=[rawt.r], W=[d.r])
            kb.stt(dst[0:parts, :], d[0:parts, :], mucol, rawt[0:parts, 1:513], ALU.mult, ALU.add, R=[d.r, rawt.r, cols.r], W=[dst.r])

        def load_halo(t_, row0, parts, tg, res):
            t0 = tg * 512
            if tg == 0:
                kb.memset("pool", t_[0:parts, 0:1], 0.0, W=[t_.r])
                kb.dma("sp", t_[0:parts, 1:513], pT_d[row0:row0 + parts, 0:512], R=[res], PW=[t_.r])
            else:
                kb.dma("sp", t_[0:parts, :], pT_d[row0:row0 + parts, t0 - 1:t0 + 512], R=[res], W=[t_.r])

        def per_chunk_mm(outb, lhs, rhs, Rl):
            for c in range(8):
                cs = slice(c * 64, (c + 1) * 64)
                kb.mm(ps[outb][0:64, cs], lhs[:, cs], rhs[:, cs], R=Rl,
                      W=[psr[outb]] if c == 0 else (), PW=() if c == 0 else [psr[outb]])

        for tg in range(NTB):
            load_halo(xwr, 48 * 128, 64, tg, pT_r[48])
            load_halo(xar, 49 * 128, 64, tg, pT_r[49])
            load_halo(xg0r, 50 * 128, 128, tg, pT_r[50])
            load_halo(xg1r, 51 * 128, 32, tg, pT_r[51])
            f = sets[0][1]
            shift_mix(f["t"], xwr, col(f"mu_xw{j}", 0, 64), 64, f["d"])
            kb.act(txw[:], f["t"][:], AF.Tanh, R=[f["t"].r], W=[txw.r])
            shift_mix(f["t"], xar, col(f"mu_xa{j}", 0, 64), 64, f["d"])
            kb.copy("act", xab[:], f["t"][:], R=[f["t"].r], W=[xab.r])
            tmp128 = sh_t
            kb.tt("pool", wtmp[:, 0:512], xg0r[:, 0:512], xg0r[:, 1:513], ALU.subtract, R=[xg0r.r], W=[wtmp.r])
            kb.stt(tmp128[:, :], wtmp[:, 0:512], col(f"mu_xg0{j}"), xg0r[:, 1:513], ALU.mult, ALU.add, R=[wtmp.r, xg0r.r, cols.r], W=[tmp128.r])
            kb.act(sxg0[:], tmp128[:], AF.Sigmoid, R=[tmp128.r], W=[sxg0.r])
            kb.tt("pool", wtmp[0:32, 0:512], xg1r[0:32, 0:512], xg1r[0:32, 1:513], ALU.subtract, R=[xg1r.r], W=[wtmp.r])
            kb.stt(tmp128[0:32, :], wtmp[0:32, 0:512], col(f"mu_xg1{j}", 0, 32), xg1r[0:32, 1:513], ALU.mult, ALU.add, R=[wtmp.r, xg1r.r, cols.r], W=[tmp128.r])
            kb.act(sxg1[:], tmp128[0:32, :], AF.Sigmoid, R=[tmp128.r], W=[sxg1.r])

            def head_gen(hd, raw, f, ob):
                c0 = hd * 64
                hc = lambda nm: col(f"{nm}{j}", hd, 64)
                for (nm, base) in (("rr", 24 * 128), ("kr", 32 * 128), ("vr", 40 * 128)):
                    load_halo(raw[nm], base + c0, 64, tg, pT_r[(base + c0) // 128])
                shift_mix(f["rm"], raw["rr"], hc("mu_r"), 64, f["d"])
                shift_mix(f["km"], raw["kr"], hc("mu_k"), 64, f["d"])
                shift_mix(f["vm"], raw["vr"], hc("mu_v"), 64, f["d"])
                kb.copy("act", f["vmb"][:], f["vm"][:], R=[f["vm"].r], W=[f["vmb"].r])
                yield
                b = kb.bank()
                kb.mm(ps[b][0:64, :], w2b[:, c0:c0 + 64], txw[:], R=[w2b.r, txw.r], W=[psr[b]])
                kb.act(f["lw"][:], ps[b][0:64, :], AF.Sigmoid, R=[psr[b], cols.r], W=[f["lw"].r], bias=hc("w0"))
                kb.ts("pool", f["lw"][:], f["lw"][:], NEG_E, None, ALU.mult, R=[f["lw"].r], W=[f["lw"].r])
                b = kb.bank()
                kb.mm(ps[b][0:64, :], a2b[:, c0:c0 + 64], xab[:], R=[a2b.r, xab.r], W=[psr[b]])
                kb.act(f["a"][:], ps[b][0:64, :], AF.Sigmoid, R=[psr[b], cols.r], W=[f["a"].r], bias=hc("a0"))
                b = kb.bank()
                kb.mm(ps[b][0:64, :], g2b0[:, c0:c0 + 64], sxg0[:], start=True, stop=False, R=[g2b0.r, sxg0.r], W=[psr[b]])
                kb.mm(ps[b][0:64, :], g2b1[:, c0:c0 + 64], sxg1[:], start=False, stop=True, R=[g2b1.r, sxg1.r], PW=[psr[b]])
                kb.copy("act", f["g"][:], ps[b][0:64, :], R=[psr[b]], W=[f["g"].r])
                yield
                kb.ts("pool", f["kk"][:], f["km"][:], hc("k_k"), None, ALU.mult, R=[f["km"].r, cols.r], W=[f["kk"].r])
                kb.tt("pool", f["sq"][:], f["kk"][:], f["kk"][:], ALU.mult, R=[f["kk"].r], W=[f["sq"].r])
                b = kb.bank()
                kb.mm(ps[b][0:64, :], ones64[:], f["sq"][:], R=[ones64.r, f["sq"].r], W=[psr[b]])
                kb.act(f["rs"][:], ps[b][0:64, :], AF.Sqrt, R=[psr[b], cols.r], W=[f["rs"].r], bias=col("tiny", 0, 64))
                S.add("dve", lambda e: e.reciprocal(out=f["rs"][:], in_=f["rs"][:]), [f["rs"].r], [f["rs"].r])
                kb.tt("pool", f["kappa"][:], f["kk"][:], f["rs"][:], ALU.mult, R=[f["kk"].r, f["rs"].r], W=[f["kappa"].r])
                kb.ts("dve", f["t"][:], f["a"][:], -1.0, hc("k_a"), ALU.add, ALU.mult, R=[f["a"].r, cols.r], W=[f["t"].r])
                kb.stt(f["kp"][:], f["t"][:], 1.0, f["km"][:], ALU.add, ALU.mult, R=[f["t"].r, f["km"].r], W=[f["kp"].r])
                kb.tt("pool", f["b"][:], f["kappa"][:], f["a"][:], ALU.mult, R=[f["kappa"].r, f["a"].r], W=[f["b"].r])
                yield
                S.add("dve", lambda e: e.tensor_tensor_scan(out=f["cum"][:], data0=M_RST, data1=f["lw"][:], initial=0.0,
                                                            op0=ALU.mult, op1=ALU.add), [masks.r, f["lw"].r], [f["cum"].r])
                kb.act(f["Ep"][:], f["cum"][:], AF.Exp, R=[f["cum"].r], W=[f["Ep"].r])
                kb.act(f["Em"][:], f["cum"][:], AF.Exp, R=[f["cum"].r], W=[f["Em"].r], scale=-1.0)
                kb.tt("pool", f["t"][:], f["cum"][:], f["lw"][:], ALU.subtract, R=[f["cum"].r, f["lw"].r], W=[f["t"].r])
                kb.act(f["Epv"][:], f["t"][:], AF.Exp, R=[f["t"].r], W=[f["Epv"].r])
                kb.tt("pool", f["rt"][:], f["rm"][:], f["Ep"][:], ALU.mult, R=[f["rm"].r, f["Ep"].r], W=[f["rt"].r])
                kb.stt(f["at"][:], f["kappa"][:], -1.0, f["Epv"][:], ALU.mult, ALU.mult, R=[f["kappa"].r, f["Epv"].r], W=[f["at"].r])
                kb.tt("pool", f["kh"][:], f["kp"][:], f["Em"][:], ALU.mult, R=[f["kp"].r, f["Em"].r], W=[f["kh"].r])
                kb.tt("dve", f["bh"][:], f["b"][:], f["Em"][:], ALU.mult, R=[f["b"].r, f["Em"].r], W=[f["bh"].r])
                yield
                for (src, dst, eng) in (("at", "Atok", "dve"), ("vmb", "Vtok", "act"), ("kh", "Khtok", "dve"), ("bh", "Bhtok", "act")):
                    b = kb.bank()
                    psb = ps[b][0:64, :].bitcast(BF16)
                    for c in range(8):
                        cs = slice(c * 64, (c + 1) * 64)
                        kb.tr(psb[:, cs], f[src][:, cs], id64b, R=[f[src].r, id64b_t.r],
                              W=[psr[b]] if c == 0 else (), PW=() if c == 0 else [psr[b]])
                    kb.copy(eng, f[dst][:], psb[:, 0:512], R=[psr[b]], W=[f[dst].r])
                    yield
                for (lhs, rhs, dst, mk, eng) in (("bh", "at", "PT0", M_UP, "dve"), ("at", "bh", "P0", M_LO, "dve"),
                                                 ("kh", "at", "AKT", M_UP, "dve"), ("kh", "rt", "RKT", M_UPI, "dve"),
                                                 ("bh", "rt", "RBT", M_UPI, "dve")):
                    b = kb.bank()
                    per_chunk_mm(b, f[lhs], f[rhs], [f[lhs].r, f[rhs].r])
                    kb.tt(eng, f[dst][:], ps[b][0:64, :], mk, ALU.mult, R=[psr[b], masks.r], W=[f[dst].r])
                    yield
                b = kb.bank()
                per_chunk_mm(b, f["AKT"], f["Vtok"], [f["AKT"].r, f["Vtok"].r])
                kb.copy("act", f["Pone"][:], ps[b][0:64, :], R=[psr[b]], W=[f["Pone"].r])
                yield
                kb.tt("pool", f["XT0"][:], f["PT0"][:], M_I8, ALU.add, R=[f["PT0"].r, masks.r], W=[f["XT0"].r])
                Pc, PTc, Xc = "P0", "PT0", "XT0"
                for lvl in range(1, 6):
                    Pn = "P1_" if Pc == "P0" else "P0"
                    PTn = "PT1" if PTc == "PT0" else "PT0"
                    Xn = "XT1" if Xc == "XT0" else "XT0"
                    b = kb.bank()
                    per_chunk_mm(b, f[PTc], f[Pc], [f[PTc].r, f[Pc].r])
                    b2 = None
                    if lvl < 5:
                        b2 = kb.bank()
                        per_chunk_mm(b2, f[Pc], f[PTc], [f[PTc].r, f[Pc].r])
                    kb.copy("act", f[Pn][:], ps[b][0:64, :], R=[psr[b]], W=[f[Pn].r])
                    if b2 is not None:
                        kb.copy("dve", f[PTn][:], ps[b2][0:64, :], R=[psr[b2]], W=[f[PTn].r])
                    yield
                    b3 = kb.bank()
                    per_chunk_mm(b3, f[Pn], f[Xc], [f[Pn].r, f[Xc].r])
                    kb.tt("dve", f[Xn][:], ps[b3][0:64, :], f[Xc][:], ALU.add, R=[psr[b3], f[Xc].r], W=[f[Xn].r])
                    Pc, PTc, Xc = Pn, PTn, Xn
                    yield
                XT = f[Xc]
                b = kb.bank()
                per_chunk_mm(b, XT, f["Pone"], [XT.r, f["Pone"].r])
                kb.copy("act", f["U0"][:], ps[b][0:64, :], R=[psr[b]], W=[f["U0"].r])
                yield
                b = kb.bank()
                per_chunk_mm(b, f["Atok"], XT, [XT.r, f["Atok"].r])
                kb.copy("dve", f["ApT"][:], ps[b][0:64, :], R=[psr[b]], W=[f["ApT"].r])
                yield
                b = kb.bank()
                per_chunk_mm(b, f["Khtok"], f["Vtok"], [f["Khtok"].r, f["Vtok"].r])
                kb.copy("act", f["C0"][:], ps[b][0:64, :], R=[psr[b]], W=[f["C0"].r])
                yield
                bU = kb.bank(hold=True); bY = kb.bank(hold=True); bS = kb.bank(hold=True)
                st = ST[hd]
                stb = STb[hd]
                for c in range(8):
                    cs = slice(c * 64, (c + 1) * 64)
                    kb.mm(ps[bU][0:64, cs], f["ApT"][:, cs], stb[:], start=True, stop=False, R=[f["ApT"].r, stb.r],
                          W=[psr[bU]] if c == 0 else (), PW=() if c == 0 else [psr[bU]])
                    kb.mm(ps[bU][0:64, cs], id64b, f["U0"][:, cs], start=False, stop=True, R=[id64b_t.r, f["U0"].r], PW=[psr[bU]])
                    kb.copy("dve", f["U"][:, cs], ps[bU][0:64, cs], R=[psr[bU]], PW=[f["U"].r])
                    yield
                    kb.mm(ps[bY][0:64, cs], f["Vtok"][:, cs], f["RKT"][:, cs], start=True, stop=False, R=[f["Vtok"].r, f["RKT"].r],
                          W=[psr[bY]] if c == 0 else (), PW=() if c == 0 else [psr[bY]])
                    kb.mm(ps[bY][0:64, cs], f["U"][:, cs], f["RBT"][:, cs], start=False, stop=False, R=[f["U"].r, f["RBT"].r], PW=[psr[bY]])
                    kb.mm(ps[bY][0:64, cs], stb[:], f["rt"][:, cs], start=False, stop=True, R=[stb.r, f["rt"].r], PW=[psr[bY]])
                    kb.mm(ps[bS][0:64, 0:64] if False else ps[bS][0:64, cs], f["Bhtok"][:, cs], f["U"][:, cs], start=True, stop=False,
                          R=[f["Bhtok"].r, f["U"].r], W=[psr[bS]] if c == 0 else (), PW=() if c == 0 else [psr[bS]])
                    kb.mm(ps[bS][0:64, cs], id64, st[:], start=False, stop=False, R=[ident.r, st.r], PW=[psr[bS]])
                    kb.mm(ps[bS][0:64, cs], id64, f["C0"][:, cs], start=False, stop=True, R=[ident.r, f["C0"].r], PW=[psr[bS]])
                    kb.ts("dve", stb[:], ps[bS][0:64, cs], f["Ep"][:, c * 64 + 63:c * 64 + 64], None, ALU.mult,
                          R=[psr[bS], f["Ep"].r], W=[stb.r])
                    kb.ts("dve", st[:], ps[bS][0:64, cs], f["Ep"][:, c * 64 + 63:c * 64 + 64], None, ALU.mult,
                          R=[psr[bS], f["Ep"].r], W=[st.r])
                    yield
                kb.copy("act", f["YT"][:], ps[bY][0:64, :], R=[psr[bY]], W=[f["YT"].r])
                kb.release(bU, bY, bS)
                yield
                b = kb.bank()
                kb.mm(ps[b][0:64, :], mean64[:], f["YT"][:], R=[mean64.r, f["YT"].r], W=[psr[b]])
                kb.stt(f["yc"][:], ps[b][0:64, :], -1.0, f["YT"][:], ALU.mult, ALU.add, R=[psr[b], f["YT"].r], W=[f["yc"].r])
                yield
                kb.tt("pool", f["sq"][:], f["yc"][:], f["yc"][:], ALU.mult, R=[f["yc"].r], W=[f["sq"].r])
                b = kb.bank()
                kb.mm(ps[b][0:64, :], mean64[:], f["sq"][:], R=[mean64.r, f["sq"].r], W=[psr[b]])
                kb.act(f["rs"][:], ps[b][0:64, :], AF.Sqrt, R=[psr[b], cols.r], W=[f["rs"].r], bias=col("gneps", 0, 64))
                S.add("dve", lambda e: e.reciprocal(out=f["rs"][:], in_=f["rs"][:]), [f["rs"].r], [f["rs"].r])
                yield
                kb.tt("pool", f["yc"][:], f["yc"][:], f["rs"][:], ALU.mult, R=[f["yc"].r, f["rs"].r], W=[f["yc"].r])
                kb.act(f["t"][:], f["yc"][:], AF.Identity, R=[f["yc"].r, cols.r], W=[f["t"].r], scale=hc("ln_w"), bias=hc("ln_b"))
                kb.stt(f["rk"][:], f["rm"][:], hc("r_k"), f["kp"][:], ALU.mult, ALU.mult, R=[f["rm"].r, f["kp"].r, cols.r], W=[f["rk"].r])
                b = kb.bank()
                kb.mm(ps[b][0:64, :], ones64[:], f["rk"][:], R=[ones64.r, f["rk"].r], W=[psr[b]])
                kb.tt("dve", f["rk"][:], ps[b][0:64, :], f["vm"][:], ALU.mult, R=[psr[b], f["vm"].r], W=[f["rk"].r])
                kb.tt("pool", f["t"][:], f["t"][:], f["rk"][:], ALU.add, R=[f["t"].r, f["rk"].r], W=[f["t"].r])
                kb.tt("pool", ob[:], f["t"][:], f["g"][:], ALU.mult, R=[f["t"].r, f["g"].r], W=[ob.r])
                kb.dma("pool", mixT_d[1024 + c0:1024 + c0 + 64, tg * 512:(tg + 1) * 512], ob[:], R=[ob.r], PW=[mix_r[8 + hd // 2]])
                if False:
                    for i_, nm_ in enumerate(["lw", "a", "g", "kappa", "kp", "cum", "rt", "at", "kh", "bh", "Atok", "Vtok", "AKT", "RKT", "RBT",
                                              "Pone", Xc, "U0", "ApT", "C0", "U", "YT", "yc", "t", "rm", "vm"]):
                        dbg_outs.append(kb.dma("pool", dbgr_d[i_], f[nm_][:], R=[f[nm_].r]))

            for hp in range(0, NH_R, 2):
                gens = [head_gen(hp + k_, *sets[k_]) for k_ in range(2)]
                while gens:
                    for g_ in list(gens):
                        try:
                            next(g_)
                        except StopIteration:
                            gens.remove(g_)
                    bg_step(bg)
        bg_drain(bg)

    def pool_factory(j):
        pw = kb.alloc([128, 4, 128], F32, "pw")
        pfix = kb.alloc([128, 4, 16], F32, "pfix")
        kb.dma("sp", pw[:], poolw_d[j].rearrange("g i o -> i g o"), W=[pw.r])
        kb.dma("sp", pfix[:], poolfix_d, W=[pfix.r])
        NR = 2
        ur = [kb.alloc([128, 527], F32, "pu") for _ in range(NR)]
        sa = [kb.alloc([128, 527], F32, "psa") for _ in range(NR)]
        sb_ = [kb.alloc([128, 527], F32, "psb") for _ in range(NR)]
        dr = [kb.alloc([128, 512], F32, "pd") for _ in range(NR)]
        orr = [kb.alloc([128, 512], BF16, "po") for _ in range(NR)]
        cnt = [0]

        def pool_block(tb):
            for g in range(4):
                win = 2 << g
                i = cnt[0]
                cnt[0] += 1
                u_, a_, b_, d_, o_ = ur[i % NR], sa[i % NR], sb_[i % NR], dr[i % NR], orr[i % NR]
                t0 = tb * 512
                if tb == 0:
                    kb.memset("pool", u_[:, 0:15], 0.0, W=[u_.r])
                    kb.dma("sp", u_[:, 15:527], pT_d[g * 128:(g + 1) * 128, 0:512], R=[pT_r[g]], PW=[u_.r])
                else:
                    kb.dma("sp", u_[:], pT_d[g * 128:(g + 1) * 128, t0 - 15:t0 + 512], R=[pT_r[g]], W=[u_.r])
                src = u_
                w = 1
                dsts = [a_, b_]
                k = 0
                while w < win:
                    dst = dsts[k % 2]
                    k += 1
                    kb.tt("pool" if k % 2 else "dve", dst[:, 2 * w - 1:527], src[:, 2 * w - 1:527], src[:, w - 1:527 - w], ALU.add,
                          R=[src.r], W=[dst.r])
                    src = dst
                    w *= 2
                kb.stt(d_[:], src[:, 15:527], 1.0 / win, u_[:, 15:527], ALU.mult, ALU.subtract, R=[src.r, u_.r], W=[d_.r])
                if tb == 0:
                    kb.tt("dve", d_[:, 0:16], src[:, 15:31], pfix[:, g, :], ALU.mult, R=[src.r, pfix.r, d_.r], W=[d_.r])
                    kb.tt("dve", d_[:, 0:16], d_[:, 0:16], u_[:, 15:31], ALU.subtract, R=[d_.r, u_.r], W=[d_.r])
                b = kb.bank()
                kb.mm(ps[b][:, :], pw[:, g, :], d_[:], R=[pw.r, d_.r], W=[psr[b]])
                kb.act(o_[:], ps[b][:], AF.Copy, R=[psr[b], cols.r], W=[o_.r], scale=col(f"pool_scale{j}", g))
                kb.dma("pool", mixT_d[g * 128:(g + 1) * 128, t0:t0 + 512], o_[:], R=[o_.r], PW=[mix_r[g]])
        return pool_block

    def mla_phase(j, bg_jobs):
        kb.phase(base_mark)
        bg = [cast_steps(bg_jobs, wring_slots())]
        A = kb.alloc
        wv = A([128, 4, 1536], BF16, "wv")
        wtmp = A([128, 512], F32, "wvt")
        for c in range(4):
            for q3 in range(3):
                kb.dma("sp", wtmp[:], wuv_d[j][c * 128:(c + 1) * 128, q3 * 512:(q3 + 1) * 512], W=[wtmp.r])
                kb.act(wv[:, c, q3 * 512:(q3 + 1) * 512], wtmp[:], AF.Copy, R=[wtmp.r, cols.r], PW=[wv.r], scale=col(f"kv_norm{j}", c))
        cosr = [A([64, 512], F32, "cos2") for _ in range(2)]
        sinr = [A([64, 512], F32, "sin2") for _ in range(2)]

        def load_cs(tb):
            c_ = kb.ring("cos", cosr)
            s_ = kb.ring("sin", sinr)
            kb.dma("sp", c_[:], cs_d[0][:, tb * 512:(tb + 1) * 512], R=[cs_r], W=[c_.r])
            kb.dma("sp", s_[:], cs_d[1][:, tb * 512:(tb + 1) * 512], R=[cs_r], W=[s_.r])
            return c_, s_
        qn = A([128, 4, NT], BF16, "qn")
        kvn = A([128, 4, NT], BF16, "kvn")
        kpeT = A([64, NT], BF16, "kpeT")
        lat = [A([128, 4, 512], F32, "lat") for _ in range(1)]
        sqring = [A([128, 4, 512], BF16, "sq") for _ in range(2)]
        rstd = A([128, 512], F32, "rstd")
        t64a = A([64, 512], F32, "t64a")
        t64b = A([64, 512], F32, "t64b")
        for (c0, dst) in ((4, qn), (8, kvn)):
            for tb in range(NTB):
                lt = kb.ring("lat", lat)
                kb.dma("sp", lt[:], pT_d[c0 * 128:(c0 + 4) * 128, tb * 512:(tb + 1) * 512].rearrange("(c p) t -> p c t", p=128),
                       R=pT_r[c0:c0 + 4], W=[lt.r])
                stats_rows([(lt, 4)], 4, rstd, 1.0 / 512, 1e-6, sqring)
                for c in range(4):
                    kb.tt("dve", dst[:, c, tb * 512:(tb + 1) * 512], lt[:, c, :], rstd[:], ALU.mult, R=[lt.r, rstd.r], PW=[dst.r])
        kp1 = A([64, 512], F32, "kp1")
        kp2 = A([64, 512], F32, "kp2")
        for tb in range(NTB):
            ts_ = slice(tb * 512, (tb + 1) * 512)
            kb.dma("sp", kp1[:], pT_d[12 * 128:12 * 128 + 64, ts_], R=[pT_r[12]], W=[kp1.r])
            kb.dma("sp", kp2[:], pT_d[13 * 128:13 * 128 + 64, ts_], R=[pT_r[13]], W=[kp2.r])
            c_, s_ = load_cs(tb)
            kb.tt("pool", t64a[:], kp1[:], c_[:], ALU.mult, R=[kp1.r, c_.r], W=[t64a.r])
            kb.tt("pool", t64b[:], kp2[:], s_[:], ALU.mult, R=[kp2.r, s_.r], W=[t64b.r])
            kb.tt("pool", kpeT[:, ts_], t64a[:], t64b[:], ALU.add, R=[t64a.r, t64b.r], PW=[kpeT.r])
        Qn = A([128, NT], BF16, "Qn")
        Qp = A([64, NT], BF16, "Qp")
        Kn = A([128, NT], BF16, "Kn")
        V = A([128, NT // 128, 128], BF16, "V")
        Pt = [A([128, 512], BF16, "Pt") for _ in range(4)]
        rl = A([128, 512], F32, "rl")
        ob = [A([128, 512], BF16, "aob") for _ in range(2)]
        for h in range(NH_A):
            wq1 = kb.ring("w", wring)
            kb.dma("sp", wq1[:, 0:512], wb_uq[2 * h], R=[wr["uq"]], W=[wq1.r])
            wq2 = kb.ring("w", wring)
            kb.dma("sp", wq2[:, 0:512], wb_uq[2 * h + 1], R=[wr["uq"]], W=[wq2.r])
            wk = kb.ring("w", wring)
            kb.dma("sp", wk[:, 0:512], wb_uk[h], R=[wr["uk"]], W=[wk.r])
            for tb in range(NTB):
                ts_ = slice(tb * 512, (tb + 1) * 512)
                b = kb.bank()
                for c in range(4):
                    kb.mm(ps[b][:, :], wq1[:, c * 128:(c + 1) * 128], qn[:, c, ts_], start=(c == 0), stop=(c == 3),
                          R=[wq1.r, qn.r], W=[psr[b]] if c == 0 else (), PW=() if c == 0 else [psr[b]])
                kb.act(Qn[:, ts_], ps[b][:], AF.Copy, R=[psr[b]], PW=[Qn.r], scale=SM_SCALE)
                b1 = kb.bank()
                b2 = kb.bank()
                for c in range(4):
                    kb.mm(ps[b1][0:64, :], wq2[:, c * 128:c * 128 + 64], qn[:, c, ts_], start=(c == 0), stop=(c == 3),
                          R=[wq2.r, qn.r], W=[psr[b1]] if c == 0 else (), PW=() if c == 0 else [psr[b1]])
                for c in range(4):
                    kb.mm(ps[b2][0:64, :], wq2[:, c * 128 + 64:c * 128 + 128], qn[:, c, ts_], start=(c == 0), stop=(c == 3),
                          R=[wq2.r, qn.r], W=[psr[b2]] if c == 0 else (), PW=() if c == 0 else [psr[b2]])
                c_, s_ = load_cs(tb)
                kb.tt("dve", t64a[:], ps[b1][0:64, :], c_[:], ALU.mult, R=[psr[b1], c_.r], W=[t64a.r])
                kb.tt("dve", t64b[:], ps[b2][0:64, :], s_[:], ALU.mult, R=[psr[b2], s_.r], W=[t64b.r])
                kb.tt("dve", t64a[:], t64a[:], t64b[:], ALU.add, R=[t64a.r, t64b.r], W=[t64a.r])
                kb.ts("pool", Qp[:, ts_], t64a[:], SM_SCALE, None, ALU.mult, R=[t64a.r], PW=[Qp.r])
                b = kb.bank()
                for c in range(4):
                    kb.mm(ps[b][:, :], wk[:, c * 128:(c + 1) * 128], kvn[:, c, ts_], start=(c == 0), stop=(c == 3),
                          R=[wk.r, kvn.r], W=[psr[b]] if c == 0 else (), PW=() if c == 0 else [psr[b]])
                kb.copy("act", Kn[:, ts_], ps[b][:], R=[psr[b]], PW=[Kn.r])
                b = kb.bank()
                for i4 in range(4):
                    it = tb * 4 + i4
                    for c in range(4):
                        kb.mm(ps[b][:, i4 * 128:(i4 + 1) * 128], kvn[:, c, it * 128:(it + 1) * 128], wv[:, c, h * 128:(h + 1) * 128],
                              start=(c == 0), stop=(c == 3), R=[kvn.r, wv.r],
                              W=[psr[b]] if (c == 0 and i4 == 0) else (), PW=() if (c == 0 and i4 == 0) else [psr[b]])
                kb.copy("dve", V[:, tb * 4:(tb + 1) * 4, :], ps[b][:].rearrange("p (a b) -> p a b", a=4), R=[psr[b]], PW=[V.r])
            for g in range(NTB):
                bO = kb.bank()
                bL = kb.bank()
                nj = 4 * g + 4
                pend_pv = [None]
                for jt in range(nj):
                    q0 = max(0, jt - 4 * g) * 128
                    bS = kb.bank(avoid=(bO, bL))
                    qs = slice(g * 512 + q0, (g + 1) * 512)
                    kb.mm(ps[bS][:, q0:512], Kn[:, jt * 128:(jt + 1) * 128], Qn[:, qs], start=True, stop=False,
                          R=[Kn.r, Qn.r], W=[psr[bS]])
                    kb.mm(ps[bS][:, q0:512], kpeT[:, jt * 128:(jt + 1) * 128], Qp[:, qs], start=False, stop=True,
                          R=[kpeT.r, Qp.r], PW=[psr[bS]])
                    p_ = kb.ring("Pt", Pt)
                    kb.act(p_[:, q0:512], ps[bS][:, q0:512], AF.Exp, R=[psr[bS]], W=[p_.r])
                    if jt >= 4 * g:
                        kb.tt("pool", p_[:, q0:q0 + 128], p_[:, q0:q0 + 128], tri[:], ALU.mult, R=[p_.r, tri.r], W=[p_.r])

                    def pv(jt=jt, q0=q0, p_=p_):
                        kb.mm(ps[bO][:, q0:512], V[:, jt, :], p_[:, q0:512], start=(jt == 0), stop=(jt == nj - 1),
                              R=[V.r, p_.r], W=[psr[bO]] if jt == 0 else (), PW=() if jt == 0 else [psr[bO]])
                        kb.mm(ps[bL][:, q0:512], ones_bf[:], p_[:, q0:512], start=(jt == 0), stop=(jt == nj - 1),
                              R=[ones_bf.r, p_.r], W=[psr[bL]] if jt == 0 else (), PW=() if jt == 0 else [psr[bL]])
                    if pend_pv[0] is not None:
                        pend_pv[0]()
                    pend_pv[0] = pv
                    bg_step(bg)
                pend_pv[0]()
                pend_pv[0] = None
                S.add("dve", lambda e, bL=bL: e.reciprocal(out=rl[:], in_=ps[bL][:]), [psr[bL]], [rl.r])
                o_ = kb.ring("aob", ob)
                kb.tt("dve", o_[:], ps[bO][:], rl[:], ALU.mult, R=[psr[bO], rl.r], W=[o_.r])
                kb.dma("pool", mixT_d[512 + h * 128:512 + (h + 1) * 128, g * 512:(g + 1) * 512], o_[:], R=[o_.r], PW=[mix_r[4 + h]])
        bg_drain(bg)

    def final_phase():
        kb.phase(base_mark)
        hring = [kb.alloc([128, 4, 512], F32, "hr") for _ in range(4)]
        sqring = [kb.alloc([128, 4, 512], BF16, "sq") for _ in range(2)]
        rstd = kb.alloc([128, 512], F32, "rstd")
        yn = [kb.alloc([128, 4, 512], F32, "yn") for _ in range(2)]
        ot = [kb.alloc([128, D], F32, "ot") for _ in range(4)]
        outs = []
        for tb in range(NTB):
            tiles = []
            for q in range(4):
                ht = kb.ring("h", hring)
                kb.dma("sp", ht[:], hT_v[:, q * 4:(q + 1) * 4, tb * 512:(tb + 1) * 512], R=hT_r[tb][q * 4:(q + 1) * 4], W=[ht.r])
                tiles.append((ht, 4))
            stats_rows(tiles, 16, rstd, 1.0 / D, 1e-6, sqring)
            for q in range(4):
                ht = tiles[q][0]
                y_ = kb.ring("yn", yn)
                for c4 in range(4):
                    kb.stt(y_[:, c4, :], ht[:, c4, :], col("final_norm", q * 4 + c4), rstd[:], ALU.mult, ALU.mult,
                           R=[ht.r, rstd.r, cols.r], PW=[y_.r])
                for i4 in range(4):
                    o_ = ot[i4]
                    b = kb.bank()
                    for c4 in range(4):
                        kb.tr(ps[b][:, c4 * 128:(c4 + 1) * 128], y_[:, c4, i4 * 128:(i4 + 1) * 128], ident[:], R=[y_.r, ident.r],
                              W=[psr[b]] if c4 == 0 else (), PW=() if c4 == 0 else [psr[b]])
                    kb.copy("act" if i4 % 2 else "dve", o_[:, q * 512:(q + 1) * 512], ps[b][:], R=[psr[b]],
                            W=[o_.r] if q == 0 else (), PW=() if q == 0 else [o_.r])
            for i4 in range(4):
                t0 = tb * 512 + i4 * 128
                outs.append(kb.dma("pool", out_d[t0:t0 + 128, :], ot[i4][:], R=[ot[i4].r]))
        return outs

    def debug_dump():
        kb.phase(base_mark)
        t_ = [kb.alloc([128, 512], F32, "dbg") for _ in range(2)]
        outs = []
        for tb in range(NTB):
            for c in range(16):
                a = kb.ring("dbg", t_)
                kb.dma("sp", a[:], hT_d[c * 128:(c + 1) * 128, tb * 512:(tb + 1) * 512], R=[hT_r[tb][c]], W=[a.r])
                outs.append(kb.dma("pool", dbg_d[c * 128:(c + 1) * 128, tb * 512:(tb + 1) * 512], a[:], R=[a.r]))
        for tb in range(NTB):
            for c in range(EV_FC):
                a = kb.ring("dbg", t_)
                kb.dma("sp", a[:], pT_d[c * 128:(c + 1) * 128, tb * 512:(tb + 1) * 512], R=[pT_r[c]], W=[a.r])
                outs.append(kb.dma("pool", dbgp_d[c * 128:(c + 1) * 128, tb * 512:(tb + 1) * 512], a[:], R=[a.r]))
        tb_ = [kb.alloc([128, 512], BF16, "dbgb") for _ in range(2)]
        for tb in range(NTB):
            for c in range(16):
                a = kb.ring("dbgb", tb_)
                kb.dma("sp", a[:], mixT_d[c * 128:(c + 1) * 128, tb * 512:(tb + 1) * 512], R=[mix_r[c]], W=[a.r])
                outs.append(kb.dma("pool", dbgm_d[c * 128:(c + 1) * 128, tb * 512:(tb + 1) * 512], a[:], R=[a.r]))
        return outs

    load_phase()
    rope_phase()
    done = False
    for L in range(depth):
        S.new_epoch()
        j = L // 2
        if L % 2 == 0:
            inproj_phase(win_e[j], EV_FC, f"ev_norm{j}", precast=(L > 0), post_alloc=lambda j=j: conv_factory(j))
            rwkv_phase(j, jobs_out_ffn(L))
            outproj_phase(wout_e[j])
        else:
            inproj_phase(win_o[j], OD_FC, f"od_norm{j}", precast=True, post_alloc=lambda j=j: pool_factory(j))
            mla_phase(j, jobs_out_ffn(L))
            outproj_phase(wout_o[j])
        if stop_after == (L, "mix"):
            done = True
            break
        S.new_epoch()
        ffn_phase(L, jobs_in(L + 1) if L + 1 < depth else None)
        if stop_after == (L, "ffn"):
            done = True
            break
    S.new_epoch()
    outs = final_phase()
    if stop_after is not None:
        outs = outs + debug_dump() + dbg_outs
    with ExitStack() as st:
        stats = S.emit(st, final_waits=outs)
    return nc, stats


def _colpack(vec, parts=128):
    v = np.asarray(vec, np.float32).reshape(-1)
    n = (len(v) + parts - 1) // parts
    out = np.zeros((128, n), np.float32)
    vv = np.zeros(n * parts, np.float32)
    vv[:len(v)] = v
    out[:parts, :] = vv.reshape(n, parts).T
    return out


def prepare(inputs, NT):
    f32 = lambda a: np.ascontiguousarray(np.asarray(a, np.float32))
    colmap = {}
    colsl = []
    ncol = [0]

    def addc(name, arr):
        colmap[name] = ncol[0]
        colsl.append(arr)
        ncol[0] += arr.shape[1]

    shared = {}
    for j in range(2):
        addc(f"ev_norm{j}", _colpack(inputs["ev_norm"][j]))
        addc(f"od_norm{j}", _colpack(inputs["od_norm"][j]))
        cw = f32(inputs["ev_conv_w"][j])
        addc(f"conv{j}", cw.reshape(8, 128, 3).transpose(1, 0, 2).reshape(128, 24))
        mu = f32(inputs["ev_mu"][j])
        addc(f"mu_r{j}", _colpack(mu[0:1024], 64))
        addc(f"mu_k{j}", _colpack(mu[1024:2048], 64))
        addc(f"mu_v{j}", _colpack(mu[2048:3072], 64))
        addc(f"mu_xw{j}", _colpack(mu[3072:3136], 64))
        addc(f"mu_xa{j}", _colpack(mu[3136:3200], 64))
        addc(f"mu_xg0{j}", _colpack(mu[3200:3328], 128))
        addc(f"mu_xg1{j}", _colpack(mu[3328:3360], 32))
        for nm in ("w0", "a0", "k_k", "k_a", "ln_w", "ln_b"):
            addc(f"{nm}{j}", _colpack(inputs["ev_" + nm][j], 64))
        addc(f"r_k{j}", _colpack(f32(inputs["ev_r_k"][j]).reshape(-1), 64))
        addc(f"pool_scale{j}", _colpack(inputs["od_pool_scale"][j]))
        addc(f"q_norm{j}", _colpack(inputs["od_q_norm"][j]))
        addc(f"kv_norm{j}", _colpack(inputs["od_kv_norm"][j]))
    for l in range(4):
        addc(f"ffn_norm{l}", _colpack(inputs["ffn_norm"][l]))
    addc("final_norm", _colpack(inputs["final_norm"]))
    inv = (1.0 / (10000.0 ** (np.arange(0, 64, 2, dtype=np.float32) / np.float32(64)))).astype(np.float32)
    addc("invf", _colpack(np.concatenate([inv, inv]), 64))
    addc("negpi", _colpack(np.full(64, -math.pi, np.float32), 64))
    addc("negone", _colpack(np.full(64, -1.0, np.float32), 64))
    addc("sinsign", _colpack(np.concatenate([np.full(32, -1.0), np.full(32, 1.0)]).astype(np.float32), 64))
    addc("eps6", _colpack(np.full(128, 1e-6, np.float32)))
    addc("gneps", _colpack(np.full(128, 64e-5, np.float32)))
    addc("tiny", _colpack(np.full(128, 1e-24, np.float32)))
    cols = np.ascontiguousarray(np.concatenate(colsl, axis=1))
    shared["cols"] = cols
    shared["ident"] = np.eye(128, dtype=np.float32)
    s_ = np.arange(64)[:, None]
    t_ = np.arange(64)[None, :]
    m = np.zeros((64, 5, 8, 64), np.float32)
    m[:, 0] = (s_ < t_).astype(np.float32)[:, None, :]
    m[:, 1] = (s_ > t_).astype(np.float32)[:, None, :]
    m[:, 2] = (s_ <= t_).astype(np.float32)[:, None, :]
    m[:, 3] = np.eye(64, dtype=np.float32)[:, None, :]
    rst = np.ones((8, 64), np.float32)
    rst[:, 0] = 0.0
    m[:, 4] = rst[None]
    shared["masks"] = np.ascontiguousarray(m.reshape(64, 5, 512))
    shared["tri"] = (np.arange(128)[None, :] >= np.arange(128)[:, None]).astype(ml_dtypes.bfloat16)
    pf = np.zeros((128, 4, 16), np.float32)
    for g in range(4):
        win = 2 << g
        pf[:, g, :] = 1.0 / np.minimum(np.arange(16) + 1, win)
    shared["poolfix"] = pf
    for j in range(2):
        w = f32(inputs["ev_w_in"][j])
        wp = np.zeros((D, EV_FC * 128), np.float32)
        wp[:, 0:6144] = w[:, 0:6144]
        wp[:, 6144:6144 + 64] = w[:, 6144:6208]
        wp[:, 6272:6272 + 64] = w[:, 6208:6272]
        wp[:, 6400:6400 + 128] = w[:, 6272:6400]
        wp[:, 6528:6528 + 32] = w[:, 6400:6432]
        shared[f"ev_w_in{j}"] = wp
        shared[f"ev_w_out{j}"] = f32(inputs["ev_w_out"][j])
        shared[f"ev_w2{j}"] = f32(inputs["ev_w2"][j])
        shared[f"ev_a2{j}"] = f32(inputs["ev_a2"][j])
        shared[f"ev_g2{j}"] = f32(inputs["ev_g2"][j])
        w = f32(inputs["od_w_in"][j])
        wp = np.zeros((D, OD_FC * 128), np.float32)
        wp[:, 0:1536] = w[:, 0:1536]
        kpe = w[:, 1536:1600]
        wp[:, 1536:1600] = kpe
        wp[:, 1664:1696] = kpe[:, 32:64]
        wp[:, 1696:1728] = kpe[:, 0:32]
        shared[f"od_w_in{j}"] = wp
        shared[f"od_w_out{j}"] = f32(inputs["od_w_out"][j])
        shared[f"od_pool_w{j}"] = f32(inputs["od_pool_w"][j])
        wq = f32(inputs["od_w_uq"][j]).reshape(512, 12, 192)
        wqp = np.zeros((512, 12, 256), np.float32)
        wqp[:, :, 0:128] = wq[:, :, 0:128]
        wqp[:, :, 128:192] = wq[:, :, 128:192]
        wqp[:, :, 192:224] = wq[:, :, 160:192]
        wqp[:, :, 224:256] = wq[:, :, 128:160]
        shared[f"od_w_uq{j}"] = np.ascontiguousarray(wqp.reshape(512, 3072))
        wkv = f32(inputs["od_w_ukv"][j]).reshape(512, 12, 256)
        shared[f"od_w_uk{j}"] = np.ascontiguousarray(wkv[:, :, 0:128].reshape(512, 1536))
        shared[f"od_w_uv{j}"] = np.ascontiguousarray(wkv[:, :, 128:256].reshape(512, 1536))
    for l in range(4):
        shared[f"ffn_wg{l}"] = f32(inputs["ffn_w_gate"][l])
        shared[f"ffn_wu{l}"] = f32(inputs["ffn_w_up"][l])
        shared[f"ffn_wd{l}"] = f32(inputs["ffn_w_down"][l])
    return shared, colmap, cols.shape[1]


_CACHE = {}


def kernel(**inputs):
    x = np.asarray(inputs["x"], np.float32)
    pos = np.asarray(inputs["positions"], np.int32)
    B, NT, _ = x.shape
    shared, colmap, ncol = prepare(inputs, NT)
    key = (NT, ncol)
    if key not in _CACHE:
        _CACHE[key] = build_program(NT, colmap, ncol)
    nc, stats = _CACHE[key]
    n_cores = 8
    in_maps = []
    for c in range(n_cores):
        b = c % B
        m = dict(shared)
        m["x"] = np.ascontiguousarray(x[b])
        m["pos"] = np.ascontiguousarray(pos[b].reshape(1, NT))
        in_maps.append(m)
    res = run_bass_kernel_spmd(nc, in_maps, core_ids=list(range(n_cores)))
    out = np.stack([np.asarray(res.results[b]["out"], np.float32) for b in range(B)], axis=0)
    return out
```
